# Optimizing a Trainium2 kernel written in Bass

```python
import math
import jax
import jax.numpy as jnp
from jax import lax
import numpy as np

D_MODEL = 1024
BATCH = 8
SEQ = 2048
DEPTH = 4
DEC_BATCH = 8
DEC_SEQ = 16
PAST_LEN = 2048

CHUNK = 64
Q_BLOCK = 128
A_HEADS = 8
NOPE_DIM = 64
ROPE_DIM = 32
V_DIM = 64
Q_LORA = 384
KV_LORA = 256
ROPE_THETA = 10000.0
W_A = A_HEADS * V_DIM
SM_SCALE = (NOPE_DIM + ROPE_DIM) ** -0.5
C_B = 256
CONV_B_WIDTH = 31
C_C = 256
C_GROUPS = 4
GMLP_CHUNK = 128
C_D = 256
CONV_D_WIDTH = 3
N_BRANCH = 4
D_FF = 2816
EPS = 1e-6
O_Q = Q_LORA
O_KV = O_Q + KV_LORA
O_KR = O_KV + ROPE_DIM
O_B = O_KR + 2 * C_B
O_C = O_B + 2 * C_C
O_D = O_C + 3 * C_D
D_IN = O_D + N_BRANCH * D_MODEL

kernel_name = 'hybrid_mla_conformer_gmlp_shortconv_stream_step'


def rmsnorm(x, g):
    xf = x.astype(jnp.float32)
    xf = xf * lax.rsqrt(jnp.mean(jnp.square(xf), axis=-1, keepdims=True) + EPS)
    return (xf * g.astype(jnp.float32)).astype(x.dtype)


def layernorm(x, g, b):
    xf = x.astype(jnp.float32)
    mu = jnp.mean(xf, axis=-1, keepdims=True)
    var = jnp.mean(jnp.square(xf - mu), axis=-1, keepdims=True)
    y = (xf - mu) * lax.rsqrt(var + EPS)
    return (y * g.astype(jnp.float32) + b.astype(jnp.float32)).astype(x.dtype)


def rope_tables(pos):
    half = ROPE_DIM // 2
    inv = jnp.exp(-math.log(ROPE_THETA) * jnp.arange(half, dtype=jnp.float32) / half)
    ang = pos.astype(jnp.float32)[:, None] * inv[None, :]
    return jnp.cos(ang), jnp.sin(ang)


def apply_rope(x, cos, sin):
    x1, x2 = jnp.split(x, 2, axis=-1)
    cos = cos.astype(x.dtype)
    sin = sin.astype(x.dtype)
    return jnp.concatenate([x1 * cos - x2 * sin, x2 * cos + x1 * sin], axis=-1)


def causal_dwconv(xpad, w):
    return lax.conv_general_dilated(
        xpad, w[:, None, :].astype(xpad.dtype), window_strides=(1,), padding='VALID',
        dimension_numbers=('NWC', 'WIO', 'NWC'), feature_group_count=xpad.shape[-1])


def swiglu_ffn(x, g_pre, g_post, w_gu, w_down):
    gate, up = jnp.split(rmsnorm(x, g_pre) @ w_gu, 2, axis=-1)
    return rmsnorm((jax.nn.silu(gate) * up) @ w_down, g_post)


def mla_expand(latent, w_ukv):
    b, l = latent.shape[:2]
    kv = (latent @ w_ukv).reshape(b, l, A_HEADS, NOPE_DIM + V_DIM)
    return kv[..., :NOPE_DIM], kv[..., NOPE_DIM:]


def mla_scores(q_nope, q_rope, k_nope, k_rope):
    s = (jnp.einsum('bqhd,bkhd->bhqk', q_nope, k_nope)
         + jnp.einsum('bqhr,bkr->bhqk', q_rope, k_rope))
    return s.astype(jnp.float32) * SM_SCALE


def mla_prompt(q_nope, q_rope, latent, k_rope, w_ukv):
    b, s = q_nope.shape[:2]
    nb = s // Q_BLOCK
    k_nope, v = mla_expand(latent, w_ukv)
    key_chunk = jnp.arange(s) // CHUNK
    qn = q_nope.reshape(b, nb, Q_BLOCK, A_HEADS, NOPE_DIM).swapaxes(0, 1)
    qr = q_rope.reshape(b, nb, Q_BLOCK, A_HEADS, ROPE_DIM).swapaxes(0, 1)

    def block(args):
        qn_b, qr_b, i = args
        q_chunk = (i * Q_BLOCK + jnp.arange(Q_BLOCK)) // CHUNK
        sc = mla_scores(qn_b, qr_b, k_nope, k_rope)
        sc = jnp.where(key_chunk[None, :] <= q_chunk[:, None], sc, -jnp.inf)
        prob = jax.nn.softmax(sc, axis=-1).astype(v.dtype)
        return jnp.einsum('bhqk,bkhd->bqhd', prob, v)

    o = lax.map(block, (qn, qr, jnp.arange(nb)))
    return o.swapaxes(0, 1).reshape(b, s, W_A)


def mla_sample(q_nope, q_rope, lat_all, kr_all, w_ukv):
    b, t = q_nope.shape[:2]
    k_nope, v = mla_expand(lat_all, w_ukv)
    prob = jax.nn.softmax(mla_scores(q_nope, q_rope, k_nope, kr_all), axis=-1).astype(v.dtype)
    return jnp.einsum('bhqk,bkhd->bqhd', prob, v).reshape(b, t, W_A)


def gmlp_mask():
    ar = jnp.arange(GMLP_CHUNK) // CHUNK
    return ar[None, :] <= ar[:, None]


def spatial_mix_prompt(v, w_s, b_s):
    b, s, _ = v.shape
    w = jnp.where(gmlp_mask()[None], w_s, 0.0).astype(v.dtype)
    vr = v.reshape(b, s // GMLP_CHUNK, GMLP_CHUNK, C_GROUPS, C_C // C_GROUPS)
    out = jnp.einsum('gij,bnjgc->bnigc', w, vr) + b_s.T[:, :, None]
    return out.reshape(b, s, C_C)


def spatial_mix_sample(v, w_s, b_s):
    b, t, _ = v.shape
    w = jnp.where(gmlp_mask()[None], w_s, 0.0).astype(v.dtype)[:, :t, :t]
    vr = v.reshape(b, t, C_GROUPS, C_C // C_GROUPS)
    out = jnp.einsum('gij,bjgc->bigc', w, vr) + b_s[:, :t].T[:, :, None]
    return out.reshape(b, t, C_C)


def token_mixer(n, cos, sin, p, cache_lat, cache_kr, buf_b, buf_d):
    b, t, _ = n.shape
    h = n @ p['w_in']
    cq, ckv, kr, glu_in, uv, dproj, g = jnp.split(h, [O_Q, O_KV, O_KR, O_B, O_C, O_D], axis=-1)

    q = (rmsnorm(cq, p['q_norm']) @ p['w_uq']).reshape(b, t, A_HEADS, NOPE_DIM + ROPE_DIM)
    q_nope = q[..., :NOPE_DIM]
    q_rope = apply_rope(q[..., NOPE_DIM:], cos[:, None], sin[:, None])
    latent = rmsnorm(ckv, p['kv_norm'])
    k_rope = apply_rope(kr, cos, sin)
    if cache_lat is None:
        a = mla_prompt(q_nope, q_rope, latent, k_rope, p['w_ukv'])
    else:
        a = mla_sample(q_nope, q_rope,
                       jnp.concatenate([cache_lat.astype(latent.dtype), latent], axis=1),
                       jnp.concatenate([cache_kr.astype(k_rope.dtype), k_rope], axis=1),
                       p['w_ukv'])

    ga, gg = jnp.split(glu_in, 2, axis=-1)
    xb = ga * jax.nn.sigmoid(gg)
    bpad = jnp.concatenate([buf_b.astype(xb.dtype), xb], axis=1)
    yb = causal_dwconv(bpad, p['conv_b_w']) + p['conv_b_bias']
    yb = jax.nn.silu(layernorm(yb, p['conv_b_ln_g'], p['conv_b_ln_b']))
    new_b = bpad[:, -(CONV_B_WIDTH - 1):]

    u, v = jnp.split(uv, 2, axis=-1)
    v = layernorm(v, p['gmlp_vn_g'], p['gmlp_vn_b'])
    if cache_lat is None:
        yc = u * spatial_mix_prompt(v, p['gmlp_w_s'], p['gmlp_b_s'])
    else:
        yc = u * spatial_mix_sample(v, p['gmlp_w_s'], p['gmlp_b_s'])

    bg, cg, hd = jnp.split(dproj, 3, axis=-1)
    xd = cg * hd
    dpad = jnp.concatenate([buf_d.astype(xd.dtype), xd], axis=1)
    yd = bg * causal_dwconv(dpad, p['conv_d_w'])
    new_d = dpad[:, -(CONV_D_WIDTH - 1):]

    gates = jax.nn.sigmoid(g).reshape(b, t, N_BRANCH, D_MODEL)
    merged = (gates[:, :, 0] * (a @ p['w_br_a']) + gates[:, :, 1] * (yb @ p['w_br_b'])
              + gates[:, :, 2] * (yc @ p['w_br_c']) + gates[:, :, 3] * (yd @ p['w_br_d']))
    return merged @ p['w_o'], (latent, k_rope, new_b, v, new_d)


def setup_inputs(seed: int = 0) -> dict:
    key = jax.random.key(seed)
    ks = iter(jax.random.split(key, 64))

    def nrm(shape, scale):
        return scale * jax.random.normal(next(ks), shape, jnp.float32)

    def gain(shape):
        return 1.0 + nrm(shape, 0.05)

    L = DEPTH
    return {
        'x_prompt': nrm((BATCH, SEQ, D_MODEL), 1.0),
        'x_sample': nrm((DEC_BATCH, DEC_SEQ, D_MODEL), 1.0),
        'cache_kv_latent': nrm((L, DEC_BATCH, PAST_LEN, KV_LORA), 1.0),
        'cache_k_rope': nrm((L, DEC_BATCH, PAST_LEN, ROPE_DIM), 1.0),
        'state_conv_b': nrm((L, DEC_BATCH, CONV_B_WIDTH - 1, C_B), 0.5),
        'state_conv_d': nrm((L, DEC_BATCH, CONV_D_WIDTH - 1, C_D), 0.5),
        'ffn1_norm_pre': gain((L, D_MODEL)),
        'ffn1_norm_post': gain((L, D_MODEL)),
        'ffn1_w_gu': nrm((L, D_MODEL, 2 * D_FF), D_MODEL ** -0.5),
        'ffn1_w_down': nrm((L, D_FF, D_MODEL), D_FF ** -0.5),
        'mix_norm_pre': gain((L, D_MODEL)),
        'mix_norm_post': gain((L, D_MODEL)),
        'w_in': nrm((L, D_MODEL, D_IN), D_MODEL ** -0.5),
        'q_norm': gain((L, Q_LORA)),
        'w_uq': nrm((L, Q_LORA, A_HEADS * (NOPE_DIM + ROPE_DIM)), Q_LORA ** -0.5),
        'kv_norm': gain((L, KV_LORA)),
        'w_ukv': nrm((L, KV_LORA, A_HEADS * (NOPE_DIM + V_DIM)), KV_LORA ** -0.5),
        'conv_b_w': nrm((L, CONV_B_WIDTH, C_B), CONV_B_WIDTH ** -0.5),
        'conv_b_bias': nrm((L, C_B), 0.02),
        'conv_b_ln_g': gain((L, C_B)),
        'conv_b_ln_b': nrm((L, C_B), 0.02),
        'gmlp_vn_g': gain((L, C_C)),
        'gmlp_vn_b': nrm((L, C_C), 0.02),
        'gmlp_w_s': nrm((L, C_GROUPS, GMLP_CHUNK, GMLP_CHUNK), GMLP_CHUNK ** -0.5),
        'gmlp_b_s': 1.0 + nrm((L, C_GROUPS, GMLP_CHUNK), 0.1),
        'conv_d_w': nrm((L, CONV_D_WIDTH, C_D), CONV_D_WIDTH ** -0.5),
        'w_br_a': nrm((L, W_A, D_MODEL), W_A ** -0.5),
        'w_br_b': nrm((L, C_B, D_MODEL), C_B ** -0.5),
        'w_br_c': nrm((L, C_C, D_MODEL), C_C ** -0.5),
        'w_br_d': nrm((L, C_D, D_MODEL), C_D ** -0.5),
        'w_o': nrm((L, D_MODEL, D_MODEL), D_MODEL ** -0.5),
        'ffn2_norm_pre': gain((L, D_MODEL)),
        'ffn2_norm_post': gain((L, D_MODEL)),
        'ffn2_w_gu': nrm((L, D_MODEL, 2 * D_FF), D_MODEL ** -0.5),
        'ffn2_w_down': nrm((L, D_FF, D_MODEL), D_FF ** -0.5),
    }


def reference(x_prompt, x_sample, cache_kv_latent, cache_k_rope, state_conv_b, state_conv_d,
              ffn1_norm_pre, ffn1_norm_post, ffn1_w_gu, ffn1_w_down,
              mix_norm_pre, mix_norm_post, w_in, q_norm, w_uq, kv_norm, w_ukv,
              conv_b_w, conv_b_bias, conv_b_ln_g, conv_b_ln_b,
              gmlp_vn_g, gmlp_vn_b, gmlp_w_s, gmlp_b_s, conv_d_w,
              w_br_a, w_br_b, w_br_c, w_br_d, w_o,
              ffn2_norm_pre, ffn2_norm_post, ffn2_w_gu, ffn2_w_down):
    s = x_prompt.shape[1]
    t = x_sample.shape[1]
    past = cache_kv_latent.shape[2]
    cos_p, sin_p = rope_tables(jnp.arange(s))
    cos_s, sin_s = rope_tables(past + jnp.arange(t))
    xp, xs = x_prompt, x_sample
    zb = jnp.zeros((xp.shape[0], CONV_B_WIDTH - 1, C_B), xp.dtype)
    zd = jnp.zeros((xp.shape[0], CONV_D_WIDTH - 1, C_D), xp.dtype)
    lat_p, kr_p, cb_p, cd_p = [], [], [], []
    lat_s, kr_s, cb_s, vc_s, cd_s = [], [], [], [], []
    for l in range(DEPTH):
        p = {'w_in': w_in[l], 'q_norm': q_norm[l], 'w_uq': w_uq[l], 'kv_norm': kv_norm[l],
             'w_ukv': w_ukv[l], 'conv_b_w': conv_b_w[l], 'conv_b_bias': conv_b_bias[l],
             'conv_b_ln_g': conv_b_ln_g[l], 'conv_b_ln_b': conv_b_ln_b[l],
             'gmlp_vn_g': gmlp_vn_g[l], 'gmlp_vn_b': gmlp_vn_b[l], 'gmlp_w_s': gmlp_w_s[l],
             'gmlp_b_s': gmlp_b_s[l], 'conv_d_w': conv_d_w[l], 'w_br_a': w_br_a[l],
             'w_br_b': w_br_b[l], 'w_br_c': w_br_c[l], 'w_br_d': w_br_d[l], 'w_o': w_o[l]}
        xp = xp + 0.5 * swiglu_ffn(xp, ffn1_norm_pre[l], ffn1_norm_post[l], ffn1_w_gu[l], ffn1_w_down[l])
        xs = xs + 0.5 * swiglu_ffn(xs, ffn1_norm_pre[l], ffn1_norm_post[l], ffn1_w_gu[l], ffn1_w_down[l])
        hp, (a1, a2, a3, _, a5) = token_mixer(rmsnorm(xp, mix_norm_pre[l]), cos_p, sin_p, p,
                                              None, None, zb, zd)
        xp = xp + rmsnorm(hp, mix_norm_post[l])
        hs, (b1, b2, b3, b4, b5) = token_mixer(rmsnorm(xs, mix_norm_pre[l]), cos_s, sin_s, p,
                                               cache_kv_latent[l], cache_k_rope[l],
                                               state_conv_b[l], state_conv_d[l])
        xs = xs + rmsnorm(hs, mix_norm_post[l])
        xp = xp + 0.5 * swiglu_ffn(xp, ffn2_norm_pre[l], ffn2_norm_post[l], ffn2_w_gu[l], ffn2_w_down[l])
        xs = xs + 0.5 * swiglu_ffn(xs, ffn2_norm_pre[l], ffn2_norm_post[l], ffn2_w_gu[l], ffn2_w_down[l])
        lat_p.append(a1); kr_p.append(a2); cb_p.append(a3); cd_p.append(a5)
        lat_s.append(b1); kr_s.append(b2); cb_s.append(b3); vc_s.append(b4); cd_s.append(b5)
    return (xp, xs,
            jnp.stack(lat_p), jnp.stack(kr_p), jnp.stack(cb_p), jnp.stack(cd_p),
            jnp.stack(lat_s), jnp.stack(kr_s), jnp.stack(cb_s), jnp.stack(vc_s), jnp.stack(cd_s))
```

```python
import contextlib
import math
import numpy as np
import concourse.bass as bass
import concourse.mybir as mybir
from concourse.bass_utils import run_bass_kernel_spmd

F32 = mybir.dt.float32
BF16 = mybir.dt.bfloat16
AF = mybir.ActivationFunctionType
ALU = mybir.AluOpType

ATOM = 8


class Ref:
    __slots__ = ("ap", "atoms")

    def __init__(self, ap, atoms):
        self.ap = ap
        self.atoms = atoms


class Buf:
    def __init__(self, prog, space, base32, ncols, dt, name=""):
        self.prog = prog
        self.space = space
        self.base32 = base32
        self.ncols = ncols
        self.dt = dt
        self.name = name
        self.esz = 2 if dt == BF16 else 4
        n32 = (ncols * self.esz + 3) // 4
        self.n32 = n32
        t = prog.tensors[space]
        ap = t[:, base32:base32 + n32]
        if dt != F32:
            ap = ap.bitcast(dt)
        self.full = ap

    def s(self, c0, c1, p0=0, p1=128, step=None):
        assert 0 <= c0 < c1 <= self.ncols, (self.name, c0, c1, self.ncols)
        if step is None:
            ap = self.full[p0:p1, c0:c1]
        else:
            ap = self.full[p0:p1, c0:c1:step]
        asz = (512 if self.space == "PS" else ATOM) * 4
        a0 = (self.base32 * 4 + c0 * self.esz) // asz
        a1 = (self.base32 * 4 + c1 * self.esz - 1) // asz
        return Ref(ap, [(self.space, a) for a in range(a0, a1 + 1)])

    def all(self):
        return self.s(0, self.ncols)


class Op:
    __slots__ = ("eng", "fn", "waits", "need_sig", "sigval", "dma_sem", "dma_val", "eidx")

    def __init__(self, eng, fn):
        self.eng = eng
        self.fn = fn
        self.waits = []
        self.need_sig = False
        self.sigval = None
        self.dma_sem = None
        self.dma_val = None
        self.eidx = None


ENGINES = ("pe", "act", "dve", "pool", "sp")


class Prog:
    def __init__(self, nc):
        self.nc = nc
        self.tensors = {}
        self.ops = {e: [] for e in ENGINES}
        self.last_w = {}
        self.readers = {}
        self.seen = {e: {} for e in ENGINES}
        self.dma_sems = {}
        self.n_dma_sems = 0
        self.rr = 0
        self.n_mm = 0
        self.marks = []

    def _add_wait(self, op, dep, raw):
        if dep is op:
            return
        e = op.eng
        if dep.dma_sem is not None:
            key = ("dma", dep.dma_sem)
            if self.seen[e].get(key, 0) >= dep.dma_val:
                return
            self.seen[e][key] = dep.dma_val
            op.waits.append(("dma", dep.dma_sem, dep.dma_val))
            return
        if dep.eng == e and op.dma_sem is None:
            if e == "pe":
                return
        key = ("eng", dep.eng)
        if self.seen[e].get(key, -1) >= dep.eidx:
            return
        self.seen[e][key] = dep.eidx
        dep.need_sig = True
        op.waits.append(("op", dep))

    def record(self, eng, fn, reads=(), writes=(), dma_key=None):
        op = Op(eng, fn)
        op.eidx = len(self.ops[eng])
        if dma_key is not None:
            v = self.dma_sems.get(dma_key, 0) + 16
            self.dma_sems[dma_key] = v
            op.dma_sem = dma_key
            op.dma_val = v
        for r in reads:
            for a in r.atoms:
                w = self.last_w.get(a)
                if w is not None:
                    self._add_wait(op, w, True)
        for r in writes:
            for a in r.atoms:
                w = self.last_w.get(a)
                if w is not None:
                    self._add_wait(op, w, False)
                for rd in self.readers.get(a, ()):
                    self._add_wait(op, rd, False)
        for r in reads:
            for a in r.atoms:
                lst = self.readers.setdefault(a, [])
                if not lst or lst[-1] is not op:
                    lst.append(op)
        for r in writes:
            for a in r.atoms:
                self.last_w[a] = op
                self.readers[a] = []
        self.ops[eng].append(op)
        return op

    def mm(self, out, pairs, start=True, stop=True):
        reads = []
        for l, r in pairs:
            reads.append(l)
            reads.append(r)
        n = len(pairs)
        self.n_mm += n

        def fn(eng, out=out, pairs=pairs):
            ins = None
            for i, (l, r) in enumerate(pairs):
                ins = eng.matmul(out.ap, l.ap, r.ap, start=(start and i == 0),
                                 stop=(stop and i == n - 1))
            return ins
        return self.record("pe", fn, reads, [out])

    def act(self, out, in_, func, scale=1.0, bias=0.0, accum=None, eng="act"):
        reads = [in_]
        sc = scale.ap if isinstance(scale, Ref) else scale
        bi = bias.ap if isinstance(bias, Ref) else bias
        if isinstance(scale, Ref):
            reads.append(scale)
        if isinstance(bias, Ref):
            reads.append(bias)
        writes = [out]
        if accum is not None:
            writes.append(accum)

        def fn(eng, out=out, in_=in_):
            kw = {}
            if accum is not None:
                kw["accum_out"] = accum.ap
            return eng.activation(out.ap, in_.ap, func, bias=bi, scale=sc, **kw)
        return self.record("act", fn, reads, writes)

    def tt(self, out, a, b, op, eng="dve"):
        def fn(e, out=out, a=a, b=b):
            return e.tensor_tensor(out.ap, a.ap, b.ap, op)
        return self.record(eng, fn, [a, b], [out])

    def stt(self, out, in0, scalar, in1, op0, op1, eng="dve"):
        reads = [in0, in1]
        sc = scalar.ap if isinstance(scalar, Ref) else scalar
        if isinstance(scalar, Ref):
            reads.append(scalar)

        def fn(e, out=out, in0=in0, in1=in1):
            return e.scalar_tensor_tensor(out.ap, in0.ap, sc, in1.ap, op0, op1)
        return self.record(eng, fn, reads, [out])

    def ts(self, out, in0, s1, op0, s2=None, op1=None, eng="dve"):
        reads = [in0]
        a1 = s1.ap if isinstance(s1, Ref) else s1
        a2 = s2.ap if isinstance(s2, Ref) else s2
        if isinstance(s1, Ref):
            reads.append(s1)
        if isinstance(s2, Ref):
            reads.append(s2)

        def fn(e, out=out, in0=in0):
            if op1 is None:
                return e.tensor_scalar(out.ap, in0.ap, a1, None, op0)
            return e.tensor_scalar(out.ap, in0.ap, a1, a2, op0, op1)
        return self.record(eng, fn, reads, [out])

    def copy(self, out, in_, eng="dve"):
        def fn(e, out=out, in_=in_):
            return e.tensor_copy(out.ap, in_.ap)
        return self.record(eng, fn, [in_], [out])

    def memset(self, out, val, eng="pool"):
        def fn(e, out=out):
            return e.memset(out.ap, val)
        return self.record(eng, fn, [], [out])

    def dma(self, queue, out, in_, key=None):
        reads = [in_] if isinstance(in_, Ref) else []
        writes = [out] if isinstance(out, Ref) else []
        oap = out.ap if isinstance(out, Ref) else out
        iap = in_.ap if isinstance(in_, Ref) else in_
        if key is None:
            key = ("auto", self.rr % 24)
            self.rr += 1

        def fn(e):
            return e.dma_start(out=oap, in_=iap)
        prev = self.dma_sems.get(key, 0)
        op = self.record(queue, fn, reads, writes, dma_key=key)
        if prev > 0 and self.seen[queue].get(("dma", key), 0) < prev:
            self.seen[queue][("dma", key)] = prev
            op.waits.append(("dma", key, prev))
        return op

    def mark(self, label):
        self.marks.append((label, self.n_mm))

    def barrier_on(self, eng, refs):
        return self.record(eng, None, refs, refs, dma_key=None)

    def emit(self):
        nc = self.nc
        with contextlib.ExitStack() as st:
            esem = {e: st.enter_context(nc.semaphore("sem_" + e)) for e in ENGINES}
            dsem = {}
            for k in self.dma_sems:
                dsem[k] = st.enter_context(nc.semaphore("dsem%d" % len(dsem)))
            for e in ENGINES:
                c = 0
                for op in self.ops[e]:
                    if op.need_sig:
                        c += 1
                        op.sigval = c
            block = st.enter_context(nc.Block())
            bmap = {"pe": block.tensor, "act": block.scalar, "dve": block.vector,
                    "pool": block.gpsimd, "sp": block.sync}

            def make(ename):
                def body(eng):
                    for op in self.ops[ename]:
                        for w in op.waits:
                            if w[0] == "dma":
                                eng.wait_ge(dsem[w[1]], w[2])
                            else:
                                d = w[1]
                                eng.wait_ge(esem[d.eng], d.sigval)
                        if op.fn is None:
                            continue
                        ins = op.fn(eng)
                        if op.dma_sem is not None:
                            ins.then_inc(dsem[op.dma_sem], 16)
                        elif op.need_sig:
                            ins.then_inc(esem[ename], 1)
                return body
            for e in ENGINES:
                if self.ops[e]:
                    bmap[e](make(e))


D = 1024
KC = 8
SEQ = 2048
DSEQ = 16
NT = SEQ + DSEQ
L = 4
DFF = 2816
FC = DFF // 128
QL, KVL, RD = 384, 256, 32
NH = 8
EPS = 1e-6
SM_SCALE = (64 + 32) ** -0.5
CBW, CDW = 31, 3
PADB, PADD = CBW - 1, CDW - 1

VOFF = {}
_o = 0
for _n, _w in (("f1pre", 8), ("f1post", 8), ("mpre", 8), ("mpost", 8), ("f2pre", 8), ("f2post", 8),
               ("qn", 3), ("kvn", 2), ("cbb", 2), ("cblg", 2), ("cblb", 2), ("cbw", CBW * 2), ("cdw", CDW * 2),
               ("vng", 2), ("vnb", 2)):
    VOFF[_n] = _o
    _o += _w
NV = _o

PASSES = (
    (0, 1024, ((0, 512), (512, 512))),
    (1024, 1040, ((1024, 512), (1536, 512), (2048, 16))),
)


class Arena:
    def __init__(self, prog, space, size):
        self.prog, self.space, self.size = prog, space, size
        self.top = 0
        self.peak = 0

    def alloc(self, ncols, dt, name=""):
        esz = 2 if dt == BF16 else 4
        n32 = (ncols * esz + 3) // 4
        n32 = (n32 + 127) // 128 * 128
        assert self.top + n32 <= self.size, ("arena overflow", name, self.top, n32, self.size)
        b = Buf(self.prog, self.space, self.top, ncols, dt, name)
        self.top += n32
        self.peak = max(self.peak, self.top)
        return b

    def at(self, base32, ncols, dt, name=""):
        return Buf(self.prog, self.space, base32, ncols, dt, name)


import os
CCUT = int(os.environ.get('CCUT', '99'))


def build_program(n_layers=L, do_mixer=True, do_ffn=True, stop=99):
    nc = bass.Bass("TRN2", target_bir_lowering=False)

    def din(name, shape):
        return nc.dram_tensor(name, list(shape), F32, kind="ExternalInput").ap()

    def dout(name, shape):
        return nc.dram_tensor(name, list(shape), F32, kind="ExternalOutput").ap()

    d_x = din("xT", (128, KC * NT))
    d_vecs = din("vecs", (128, L * NV))
    d_ident = din("ident", (128, 128))
    d_cos = din("cosT", (32, NT))
    d_sin = din("sinS", (32, NT))
    d_wgu = din("wgu", (L, 2, FC, 128, KC * 256))
    d_wdn = din("wdn", (L, 2, 8, 128, FC * 128))
    d_wcq = din("wcq", (L, 128, KC * QL))
    d_wckv = din("wckv", (L, 128, KC * KVL))
    d_wkr = din("wkr", (L, 2, 128, KC * 96))
    d_wglu = din("wglu", (L, 128, KC * 512))
    d_wuv = din("wuv", (L, 128, KC * 512))
    d_wdp = din("wdp", (L, 128, KC * 768))
    d_wgm = din("wgm", (L, 8, 128, KC * 512 + 10 * 128))
    d_wo = din("wo", (L, 8, 128, KC * 128))
    d_wuq = din("wuq", (L, 2, 128, 3 * 768))
    d_wuk = din("wuk", (L, 128, 2 * NH * 64))
    d_wuv2 = din("wuv2", (L, 128, 2 * 512))
    d_wukT = din("wukT", (L, 64, NH * 256))
    d_wsT = din("wsT", (L, 128, 512))
    d_bs = din("bsrow", (L, 1, 512))
    d_gvb = din("gvb", (L, 128, 512))
    d_clatT = din("clatT", (L, 128, 2 * SEQ))
    d_clat = din("clat", (L, 128, 16 * 256))
    d_ckr = din("ckrT", (L, 32, SEQ))
    d_scb = din("scbT", (L, 128, 2 * PADB))
    d_scd = din("scdT", (L, 128, 2 * PADD))

    o_y = dout("yT", (128, KC * NT))
    o_lat = dout("latT", (L, 128, 2 * NT))
    o_kr = dout("krT", (L, 32, NT))
    o_cb = dout("cbT", (L, 128, 2 * 2 * PADB))
    o_cd = dout("cdT", (L, 128, 2 * 2 * PADD))
    o_gv = dout("gvT", (L, 128, 2 * DSEQ))

    with contextlib.ExitStack() as st:
        XCOLS = KC * NT
        CCOLS = L * NV + 128 + 64 + 64 + 2 * ATOM + 8
        CCOLS = (CCOLS + 127) // 128 * 128
        ASIZE = 35456
        t_x = st.enter_context(nc.sbuf_tensor("xres", [128, XCOLS], F32))
        t_c = st.enter_context(nc.sbuf_tensor("consts", [128, CCOLS], F32))
        t_a = st.enter_context(nc.sbuf_tensor("arena", [128, ASIZE], F32))
        t_p = st.enter_context(nc.psum_tensor("psum", [128, 8 * 512], F32))
        P = Prog(nc)
        P.tensors.update({"X": t_x, "C": t_c, "A": t_a, "PS": t_p})
        xT = Buf(P, "X", 0, XCOLS, F32, "xT")
        vecs = Buf(P, "C", 0, L * NV, F32, "vecs")
        identf = Buf(P, "C", L * NV, 128, F32, "identf")
        onesb = Buf(P, "C", L * NV + 128, 128, BF16, "ones")
        identb = Buf(P, "C", L * NV + 192, 128, BF16, "identb")
        banks = [Buf(P, "PS", i * 512, 512, F32, "bank%d" % i) for i in range(8)]
        AR = Arena(P, "A", ASIZE)
        state = {"bank": 0}

        def bank(lo=0, hi=8):
            b = banks[lo + state["bank"] % (hi - lo)]
            state["bank"] += 1
            return b

        def vcol(l, name, i=0):
            c = l * NV + VOFF[name] + i
            return vecs.s(c, c + 1)

        def xs(k, t0, n):
            return xT.s(k * NT + t0, k * NT + t0 + n)

        for k in range(KC):
            P.dma("sp", xT.s(k * NT, (k + 1) * NT), d_x[:, k * NT:(k + 1) * NT])
        P.dma("sp", vecs.all(), d_vecs)
        P.dma("sp", identf.all(), d_ident)
        epsb = {1.0: Buf(P, "C", L * NV + 256, 1, F32, "eps1"), 4.0: Buf(P, "C", L * NV + 256 + ATOM, 1, F32, "eps4")}
        for f2_, b_ in epsb.items():
            P.memset(b_.all(), EPS * f2_, eng="dve")
        epsb = {k_: v_.all() for k_, v_ in epsb.items()}
        P.memset(onesb.all(), 1.0, eng="dve")
        P.copy(identb.all(), identf.all(), eng="dve")

        wq = {"i": 0}

        def wload(dst, src):
            P.dma("pool", dst, src, key=("w", wq["i"] % 6))
            wq["i"] += 1

        def recip(out, in_):
            P.record("dve", (lambda e, o=out, i=in_: e.reciprocal(o.ap, i.ap)), [in_], [out])

        def rstd_from(srcs, n, Dn, out, sq, post=1.0):
            sqs = []
            for i, r in enumerate(srcs):
                q = sq.s(i * 512, i * 512 + n)
                P.act(q, r, AF.Square)
                sqs.append(q)
            ps = bank().s(0, n)
            P.mm(ps, [(onesb.all(), q) for q in sqs])
            f2 = 1.0 / (post * post)
            P.act(out, ps, AF.Ln, scale=f2 / Dn, bias=epsb[f2])
            P.act(out, out, AF.Exp, scale=-0.5)

        def prenorm(l, gname, tiles, T0, TP, nT, sq, rstd):
            for (t0, n) in tiles:
                tl = t0 - T0
                r = rstd.s(0, n)
                rstd_from([xs(k, t0, n) for k in range(KC)], n, D, r, sq)
                for k in range(KC):
                    P.stt(nT.s(k * TP + tl, k * TP + tl + n), xs(k, t0, n), vcol(l, gname, k), r,
                          ALU.mult, ALU.mult)

        def postnorm_residual(l, gname, tiles, T0, TP, ytmp, sq, rstd, half):
            for (t0, n) in tiles:
                tl = t0 - T0
                r = rstd.s(0, n)
                ys = [ytmp.s(k * TP + tl, k * TP + tl + n) for k in range(KC)]
                rstd_from(ys, n, D, r, sq, post=0.5 if half else 1.0)
                for k in range(KC):
                    P.stt(ys[k], ys[k], vcol(l, gname, k), r, ALU.mult, ALU.mult)
                    P.tt(xs(k, t0, n), xs(k, t0, n), ys[k], ALU.add)

        def ffn_pair(l, f, next_pre=None, skip_preA=False):
            pre, post = ("f1pre", "f1post") if f == 0 else ("f2pre", "f2post")
            (TA0, TPA, tilesA), (TB0, TPB, tilesB) = PASSES
            AR.top = 0
            ytmp = AR.alloc(KC * TPB, F32, "ytmp")
            nTA = AR.at(ytmp.base32, KC * TPA, BF16, "nTA")
            wgu = [AR.alloc(KC * 256, BF16, "wgu%d" % i) for i in range(3)]
            assert AR.top <= 12288
            nTB = AR.alloc(KC * TPB, BF16, "nTB")
            hT = AR.alloc(FC * TPB, BF16, "hT")
            wdn = [AR.alloc(FC * 128, BF16, "wdn%d" % i) for i in range(2)]
            sq = AR.alloc(KC * 512, BF16, "sq")
            rstd = AR.alloc(512, F32, "rstd")
            sg = [AR.alloc(512, F32, "sg%d" % i) for i in range(2)]
            cnt = {"s": 0}

            def up(T0, TP, tiles, nT, hook=None):
                for c in range(FC):
                    w = wgu[c % 3]
                    wload(w.all(), d_wgu[l, f, c])
                    for (t0, n) in tiles:
                        tl = t0 - T0
                        pg = bank().s(0, n)
                        pu = bank().s(0, n)
                        P.mm(pg, [(w.s(k * 256, k * 256 + 128), nT.s(k * TP + tl, k * TP + tl + n)) for k in range(KC)])
                        P.mm(pu, [(w.s(k * 256 + 128, k * 256 + 256), nT.s(k * TP + tl, k * TP + tl + n)) for k in range(KC)])
                        s_ = sg[cnt["s"] % 2].s(0, n)
                        cnt["s"] += 1
                        P.act(s_, pg, AF.Silu)
                        P.tt(hT.s(c * TP + tl, c * TP + tl + n), pu, s_, ALU.mult)
                    if hook is not None:
                        hook(c)

            def down(T0, TP, tiles):
                for m in range(8):
                    w = wdn[m % 2]
                    wload(w.all(), d_wdn[l, f, m])
                    for (t0, n) in tiles:
                        tl = t0 - T0
                        py = bank().s(0, n)
                        P.mm(py, [(w.s(k * 128, k * 128 + 128), hT.s(k * TP + tl, k * TP + tl + n)) for k in range(FC)])
                        P.act(ytmp.s(m * TP + tl, m * TP + tl + n), py, AF.Copy)

            def hookA(c):
                if c % 2 == 1 and c // 2 < len(tilesA):
                    postnorm_residual(l, post, [tilesA[c // 2]], TA0, TPA, ytmp, sq, rstd, True)

            P.mark("L%d ffn%d upA" % (l, f))
            if not skip_preA:
                prenorm(l, pre, tilesA, TA0, TPA, nTA, sq, rstd)
            up(TA0, TPA, tilesA, nTA)
            P.mark("L%d ffn%d downA" % (l, f))
            prenorm(l, pre, tilesB, TB0, TPB, nTB, sq, rstd)
            down(TA0, TPA, tilesA)
            P.mark("L%d ffn%d upB" % (l, f))
            up(TB0, TPB, tilesB, nTB, hook=hookA)
            P.mark("L%d ffn%d downB" % (l, f))
            down(TB0, TPB, tilesB)
            P.mark("L%d ffn%d end" % (l, f))
            if next_pre is not None:
                next_pre()
            postnorm_residual(l, post, tilesB, TB0, TPB, ytmp, sq, rstd, True)

        def mixer_layer(l, early_f2=None):
            AR.top = 0
            kT = AR.alloc(NH * SEQ, BF16, "kT")
            Vt = AR.alloc(16 * 512, BF16, "V")
            carryb = AR.alloc(2 * PADB, BF16, "carryb")
            carryd = AR.alloc(2 * PADD, BF16, "carryd")
            base_top = AR.top
            for pi, (T0, TP, tiles) in enumerate(PASSES):
                AR.top = base_top
                for _ in mixer_pass(l, pi, T0, TP, tiles, kT, Vt, carryb, carryd, early_f2):
                    if pi == 0:
                        yield

        def mixer_pass(l, pi, T0, TP, tiles, kT, Vt, carryb, carryd, early_f2=None):
            isB = pi == 1
            ptiles = [t for t in tiles if t[0] < SEQ]
            stile = (SEQ, DSEQ) if isB else None
            PT0 = T0
            PTN = sum(n for (_, n) in ptiles)
            ybase = AR.top
            nT = AR.alloc(KC * TP, BF16, "nT")
            aT = AR.alloc(4 * TP, BF16, "aT")
            ybT = AR.alloc(2 * TP, BF16, "ybT")
            ycT = AR.alloc(2 * TP, BF16, "ycT")
            ydT = AR.alloc(2 * TP, BF16, "ydT")
            need = (KC * TP * 4 - (AR.top - ybase) * 4)
            if need > 0:
                AR.alloc(need // 4, F32, "ypad")
            ytmp = AR.at(ybase, KC * TP, F32, "ytmp")
            sq = AR.alloc(KC * 512, BF16, "sq")
            rstd = AR.alloc(512, F32, "rstd")
            tf = [AR.alloc(512, F32, "tf%d" % i) for i in range(4)]
            tabc = AR.alloc(512, F32, "tabc")
            tabs = AR.alloc(512, F32, "tabs")
            stage_top = AR.top

            def ntk(k, tl, n):
                return nT.s(k * TP + tl, k * TP + tl + n)

            prenorm(l, "mpre", tiles, T0, TP, nT, sq, rstd)
            yield
            P.mark("L%d mix%d st2" % (l, pi))

            cqn = AR.alloc(3 * TP, BF16, "cqn")
            att_top = AR.top
            latb = AR.alloc(2 * TP, BF16, "latb")
            krs = AR.alloc(DSEQ, BF16, "krs")
            wsl = AR.alloc(2 * KC * 96, BF16, "wslot")
            wsl2 = AR.alloc(KC * KVL, BF16, "wslot2")
            latf = AR.alloc(2 * 512, F32, "latf")
            krf = AR.alloc(512, F32, "krf")
            cqf = AR.at(tf[1].base32, 3 * 512, F32, "cqf")
            r = None
            wUK = AR.at(wsl2.base32, 2 * NH * 64, BF16, "wUK")
            wUV = AR.at(wsl2.base32 + 512, 2 * 512, BF16, "wUV")

            def load_cd():
                wload(wsl.s(0, KC * 96), d_wkr[l, 0])
                wload(wsl.s(KC * 96, 2 * KC * 96), d_wkr[l, 1])
                wload(wUK.all(), d_wuk[l])
                wload(wUV.all(), d_wuv2[l])
            assert wsl2.base32 == wsl.base32 + 768
            if pi == 0:
                load_cd()
            if pi == 0:
                wslA = AR.at(Vt.base32 + 128, KC * QL, BF16, "wslA")
                wslB = AR.at(Vt.base32 + 128 + 1536, KC * KVL, BF16, "wslB")
            else:
                wslA = AR.at(wsl.base32, KC * QL, BF16, "wslA")
                wslB = AR.at(wsl.base32, KC * KVL, BF16, "wslB")
            wload(wslA.all(), d_wcq[l])
            if pi == 0:
                wload(wslB.all(), d_wckv[l])
            for (t0, n) in tiles:
                tl = t0 - T0
                for mc in range(3):
                    ps = bank().s(0, n)
                    P.mm(ps, [(wslA.s(k * QL + mc * 128, k * QL + mc * 128 + 128), ntk(k, tl, n)) for k in range(KC)])
                    P.act(cqf.s(mc * 512, mc * 512 + n), ps, AF.Copy)
                r = rstd.s(0, n)
                rstd_from([cqf.s(mc * 512, mc * 512 + n) for mc in range(3)], n, QL, r, sq)
                for mc in range(3):
                    P.stt(cqn.s(mc * TP + tl, mc * TP + tl + n), cqf.s(mc * 512, mc * 512 + n),
                          vcol(l, "qn", mc), r, ALU.mult, ALU.mult)
            if pi == 1:
                wload(wslB.all(), d_wckv[l])
            for (t0, n) in tiles:
                tl = t0 - T0
                r = rstd.s(0, n)
                for mc in range(2):
                    ps = bank().s(0, n)
                    P.mm(ps, [(wslB.s(k * KVL + mc * 128, k * KVL + mc * 128 + 128), ntk(k, tl, n)) for k in range(KC)])
                    P.act(latf.s(mc * 512, mc * 512 + n), ps, AF.Copy)
                rstd_from([latf.s(mc * 512, mc * 512 + n) for mc in range(2)], n, KVL, r, sq)
                for mc in range(2):
                    P.stt(latf.s(mc * 512, mc * 512 + n), latf.s(mc * 512, mc * 512 + n),
                          vcol(l, "kvn", mc), r, ALU.mult, ALU.mult)
                    P.act(latb.s(mc * TP + tl, mc * TP + tl + n), latf.s(mc * 512, mc * 512 + n), AF.Copy)
                    P.dma("sp", d_lat_out(l, mc, t0, n), latf.s(mc * 512, mc * 512 + n))
            if pi == 1:
                load_cd()
            for (t0, n) in tiles:
                tl = t0 - T0
                samp = t0 >= SEQ
                P.dma("sp", tabc.s(0, n, 64, 96), d_cos[:, t0:t0 + n])
                P.dma("sp", tabs.s(0, n, 64, 96), d_sin[:, t0:t0 + n])
                pa = bank().s(0, n, 0, 96)
                pb = bank().s(0, n, 0, 96)
                P.mm(pa, [(wsl.s(k * 96, k * 96 + 96), ntk(k, tl, n)) for k in range(KC)])
                P.mm(pb, [(wsl.s(KC * 96 + k * 96, KC * 96 + k * 96 + 96), ntk(k, tl, n)) for k in range(KC)])
                t1 = tf[0].s(0, n, 64, 96)
                P.tt(t1, bank_rows(pa, 64, 96), tabc.s(0, n, 64, 96), ALU.mult)
                P.tt(krf.s(0, n, 64, 96), bank_rows(pb, 64, 96), tabs.s(0, n, 64, 96), ALU.mult)
                P.tt(krf.s(0, n, 64, 96), krf.s(0, n, 64, 96), t1, ALU.add)
                P.dma("sp", o_kr[l, :, t0:t0 + n], krf.s(0, n, 64, 96))
                if samp:
                    P.copy(krs.s(0, n, 64, 96), krf.s(0, n, 64, 96), eng="dve")
                else:
                    for h in range(NH):
                        dst = kT.s(h * SEQ + t0, h * SEQ + t0 + n, 64, 96)
                        if h % 2 == 0:
                            P.act(dst, krf.s(0, n, 64, 96), AF.Copy)
                        else:
                            P.copy(dst, krf.s(0, n, 64, 96), eng="dve")
            for (t0, n) in ptiles:
                tl = t0 - T0
                for h in range(NH):
                    ps = bank().s(0, n, 0, 64)
                    P.mm(ps, [(wUK.s((k * NH + h) * 64, (k * NH + h) * 64 + 64), latb.s(k * TP + tl, k * TP + tl + n))
                              for k in range(2)])
                    dst = kT.s(h * SEQ + t0, h * SEQ + t0 + n, 0, 64)
                    if h % 2 == 0:
                        P.copy(dst, ps, eng="dve")
                    else:
                        P.act(dst, ps, AF.Copy)
                for bi in range(n // 128):
                    b = (t0 + bi * 128) // 128
                    ps = bank().all()
                    P.mm(ps, [(latb.s(k * TP + tl + bi * 128, k * TP + tl + bi * 128 + 128), wUV.s(k * 512, k * 512 + 512))
                              for k in range(2)])
                    if bi % 2 == 0:
                        P.act(Vt.s(b * 512, b * 512 + 512), ps, AF.Copy)
                    else:
                        P.copy(Vt.s(b * 512, b * 512 + 512), ps, eng="dve")

            if stop <= 2:
                return
            P.mark("L%d mix%d att" % (l, pi))
            AR.top = att_top
            latb2 = AR.alloc(2 * TP, BF16, "latb")
            krs2 = AR.alloc(DSEQ, BF16, "krs")
            assert latb2.base32 == latb.base32 and krs2.base32 == krs.base32
            wUQ = [AR.alloc(3 * 768, BF16, "wUQ%d" % i) for i in range(2)]
            qT = AR.alloc(NH * 512, BF16, "qT")
            Pt = [AR.at(sq.base32 + i * 256, 512, BF16, "Pt%d" % i) for i in range(4)]
            rden = [tf[2], tf[3]]
            wload(wUQ[0].all(), d_wuq[l, 0])
            wload(wUQ[1].all(), d_wuq[l, 1])

            def load_tabs(t0, n):
                P.dma("sp", tabc.s(0, n, 64, 96), d_cos[:, t0:t0 + n])
                P.dma("sp", tabs.s(0, n, 64, 96), d_sin[:, t0:t0 + n])

            def q_head(t0, n, tl, qdst, qstride, h):
                pa = bank(0, 4).s(0, n, 0, 96)
                pb = bank(0, 4).s(0, n, 0, 96)
                P.mm(pa, [(wUQ[0].s(k * 768 + h * 96, k * 768 + h * 96 + 96), cqn.s(k * TP + tl, k * TP + tl + n))
                          for k in range(3)])
                P.mm(pb, [(wUQ[1].s(k * 768 + h * 96, k * 768 + h * 96 + 96), cqn.s(k * TP + tl, k * TP + tl + n))
                          for k in range(3)])
                P.copy(qdst.s(h * qstride, h * qstride + n, 0, 64), bank_rows(pa, 0, 64), eng="dve")
                t1 = tf[0].s(0, n, 64, 96)
                t2 = tf[1].s(0, n, 64, 96)
                P.tt(t1, bank_rows(pa, 64, 96), tabc.s(0, n, 64, 96), ALU.mult)
                P.tt(t2, bank_rows(pb, 64, 96), tabs.s(0, n, 64, 96), ALU.mult)
                P.tt(qdst.s(h * qstride, h * qstride + n, 64, 96), t1, t2, ALU.add, eng="pool")

            qs = AR.alloc(NH * DSEQ, BF16, "qs") if isB else None
            pcount = {"i": 0}
            for ti, (t0, n) in enumerate(ptiles):
                tl = t0 - T0
                gi = t0 // 512
                if ti == 0:
                    load_tabs(t0, n)
                    for h in range(NH):
                        q_head(t0, n, tl, qT, 512, h)
                if ti + 1 < len(ptiles):
                    nxt = (ptiles[ti + 1][0], ptiles[ti + 1][1], ptiles[ti + 1][0] - T0, qT, 512)
                elif isB:
                    nxt = (stile[0], stile[1], stile[0] - T0, qs, DSEQ)
                else:
                    nxt = None
                if nxt is not None:
                    load_tabs(nxt[0], nxt[1])
                nblk = 4 * gi + 4
                items = [(h, j) for h in range(NH) for j in range(nblk)]
                pts = {}

                def issue_scores(idx):
                    h, j = items[idx]
                    m = j - 4 * gi
                    c0 = 0 if m < 0 else 128 * m
                    ps = bank(0, 4).s(c0, 512)
                    P.mm(ps, [(kT.s(h * SEQ + j * 128, h * SEQ + j * 128 + 128, 0, 96),
                               qT.s(h * 512 + c0, h * 512 + 512, 0, 96))])
                    pt = Pt[pcount["i"] % 4]
                    pcount["i"] += 1
                    P.act(pt.s(c0, 512), ps, AF.Exp, scale=SM_SCALE)
                    if m >= 0:
                        P.memset(pt.s(c0, c0 + 64, 64, 128), 0.0, eng="pool" if gi == 0 else "dve")
                    pts[idx] = (pt, c0)

                def issue_pv(idx):
                    h, j = items[idx]
                    hp, half = h // 2, h % 2
                    acc = banks[4 + (h % 2) * 2]
                    den = banks[5 + (h % 2) * 2]
                    pt, c0 = pts.pop(idx)
                    P.mm(acc.s(c0, 512), [(Vt.s(j * 512 + hp * 128, j * 512 + hp * 128 + 128), pt.s(c0, 512))],
                         start=(j == 0), stop=(j == nblk - 1))
                    P.mm(den.s(c0, 512), [(onesb.all(), pt.s(c0, 512))],
                         start=(j == 0), stop=(j == nblk - 1))
                    if j == nblk - 1:
                        def fin(h=h, hp=hp, half=half, acc=acc, den=den):
                            r0, r1 = half * 64, half * 64 + 64
                            rd = rden[h % 2]
                            recip(rd.s(0, 512, r0, r1), den.s(0, 512, r0, r1))
                            P.tt(aT.s(hp * TP + tl, hp * TP + tl + n, r0, r1), acc.s(0, 512, r0, r1), rd.s(0, 512, r0, r1), ALU.mult)
                        pend.append(fin)
                        if nxt is not None:
                            q_head(nxt[0], nxt[1], nxt[2], nxt[3], nxt[4], h)

                LA = 3
                pend = []
                for idx in range(len(items) + LA):
                    if idx < len(items):
                        issue_scores(idx)
                        if pend:
                            pend.pop(0)()
                    if idx >= LA:
                        issue_pv(idx - LA)
                while pend:
                    pend.pop(0)()

            if stop <= 3:
                return
            if isB:
                t0, n = stile
                tl = t0 - T0
                mark = AR.top
                AR.top = kT.base32
                clatT = AR.alloc(2 * SEQ, BF16, "clatT")
                clat = AR.alloc(16 * 256, BF16, "clat")
                ckr = AR.alloc(SEQ, BF16, "ckr")
                wUKT = AR.alloc(NH * 256, BF16, "wUKT")
                qabs = AR.alloc(2 * 128, BF16, "qabs")
                latn = AR.alloc(256, BF16, "latn")
                olat = AR.alloc(2 * 128, BF16, "olat")
                pts = [AR.alloc(128, BF16, "pts%d" % i) for i in range(4)]
                wUVs = AR.alloc(2 * 512, BF16, "wUVs")
                wload(wUVs.all(), d_wuv2[l])
                assert AR.top <= Vt.base32 + Vt.n32
                wload(clatT.all(), d_clatT[l])
                wload(clat.all(), d_clat[l])
                wload(ckr.s(0, SEQ, 64, 96), d_ckr[l])
                wload(wUKT.s(0, NH * 256, 0, 64), d_wukT[l])
                pq = banks[0]
                for h in range(NH):
                    for kc in range(2):
                        P.mm(pq.s(kc * 128 + h * 16, kc * 128 + h * 16 + 16),
                             [(wUKT.s(h * 256 + kc * 128, h * 256 + kc * 128 + 128, 0, 64), qs.s(h * 16, h * 16 + 16, 0, 64))])
                P.act(qabs.all(), pq.s(0, 256), AF.Copy)
                pl = banks[1]
                for kc in range(2):
                    P.mm(pl.s(kc * 128, kc * 128 + 128, 0, 16),
                         [(latb.s(kc * TP + tl, kc * TP + tl + n), identb.all())])
                P.copy(latn.s(0, 256, 0, 16), pl.s(0, 256, 0, 16), eng="dve")
                ol = [banks[4], banks[5]]
                dn = banks[6]
                for j in range(17):
                    new = j == 16
                    kk = 16 if new else 128
                    ps = banks[2 + j % 2].s(0, 128, 0, kk)
                    if new:
                        pairs = [(latb.s(kc * TP + tl, kc * TP + tl + n), qabs.s(kc * 128, kc * 128 + 128)) for kc in range(2)]
                        pairs.append((krs.s(0, n, 64, 96), qs.s(0, 128, 64, 96)))
                    else:
                        pairs = [(clatT.s(kc * SEQ + j * 128, kc * SEQ + j * 128 + 128), qabs.s(kc * 128, kc * 128 + 128))
                                 for kc in range(2)]
                        pairs.append((ckr.s(j * 128, j * 128 + 128, 64, 96), qs.s(0, 128, 64, 96)))
                    P.mm(ps, pairs)
                    pt = pts[j % 4].s(0, 128, 0, kk)
                    P.act(pt, ps, AF.Exp, scale=SM_SCALE)
                    for kc in range(2):
                        lhs = latn.s(kc * 128, kc * 128 + 128, 0, 16) if new else clat.s(j * 256 + kc * 128, j * 256 + kc * 128 + 128)
                        P.mm(ol[kc].s(0, 128), [(lhs, pt)], start=(j == 0), stop=new)
                    P.mm(dn.s(0, 128), [(onesb.s(0, 128, 0, kk), pt)], start=(j == 0), stop=new)
                for kc in range(2):
                    P.act(olat.s(kc * 128, kc * 128 + 128), ol[kc].s(0, 128), AF.Copy)
                rd = rden[0]
                P.record("dve", (lambda e, o=rd.s(0, 128), i=dn.s(0, 128): e.reciprocal(o.ap, i.ap)),
                         [dn.s(0, 128)], [rd.s(0, 128)])
                for h in range(NH):
                    hp, half = h // 2, h % 2
                    r0, r1 = half * 64, half * 64 + 64
                    ps = banks[h % 2].s(256, 256 + 16)
                    P.mm(ps, [(wUVs.s(kc * 512 + hp * 128, kc * 512 + hp * 128 + 128), olat.s(kc * 128 + h * 16, kc * 128 + h * 16 + 16))
                              for kc in range(2)])
                    P.tt(aT.s(hp * TP + tl, hp * TP + tl + n, r0, r1), bank_rows(ps, r0, r1), rd.s(h * 16, h * 16 + 16, r0, r1), ALU.mult)
                AR.top = mark

            if isB and early_f2 is not None:
                early_f2(sq, rstd)
            if stop <= 4:
                return
            P.mark("L%d mix%d brB" % (l, pi))
            AR.top = stage_top
            branch_b(l, pi, T0, TP, tiles, ptiles, stile, nT, ybT, carryb, sq, rstd, tf)
            if stop <= 5:
                return
            P.mark("L%d mix%d brC" % (l, pi))
            AR.top = stage_top
            branch_c(l, pi, T0, TP, tiles, ptiles, stile, nT, ycT, sq, rstd, tf)
            if stop <= 6:
                return
            P.mark("L%d mix%d brD" % (l, pi))
            AR.top = stage_top
            branch_d(l, pi, T0, TP, tiles, ptiles, stile, nT, ydT, carryd, tf)

            if stop <= 7:
                return
            P.mark("L%d mix%d merge" % (l, pi))
            AR.top = stage_top
            mrg = AR.alloc(KC * TP, BF16, "merged")
            wG = [AR.alloc(KC * 512 + 10 * 128, BF16, "wG0"),
                  AR.at(sq.base32, KC * 512 + 10 * 128, BF16, "wG1")]
            assert sq.base32 + 2688 <= tf[1].base32
            wO = [AR.at(wG[0].base32 + i * 512, KC * 128, BF16, "wO%d" % i) for i in range(5)]
            gsb = [tabc, tabs]
            brs = ((aT, 4), (ybT, 2), (ycT, 2), (ydT, 2))
            for m in range(8):
                w = wG[m % 2]
                wload(w.all(), d_wgm[l, m])
                for (t0, n) in tiles:
                    tl = t0 - T0
                    kb0 = 0
                    for b, (src, nk) in enumerate(brs):
                        pg = bank().s(0, n)
                        pp = bank().s(0, n)
                        P.mm(pg, [(w.s((k * 4 + b) * 128, (k * 4 + b) * 128 + 128), ntk(k, tl, n)) for k in range(KC)])
                        P.mm(pp, [(w.s(KC * 512 + (kb0 + kk) * 128, KC * 512 + (kb0 + kk) * 128 + 128),
                                   src.s(kk * TP + tl, kk * TP + tl + n)) for kk in range(nk)])
                        kb0 += nk
                        g = gsb[b % 2].s(0, n)
                        P.act(g, pg, AF.Sigmoid)
                        if b == 0:
                            P.tt(tf[2].s(0, n), pp, g, ALU.mult)
                        else:
                            P.tt(tf[3].s(0, n), pp, g, ALU.mult)
                            dst = tf[2].s(0, n) if b < 3 else mrg.s(m * TP + tl, m * TP + tl + n)
                            P.tt(dst, tf[2].s(0, n), tf[3].s(0, n), ALU.add)
            P.mark("L%d mix%d wo" % (l, pi))
            for m in range(8):
                w = wO[m % 5]
                wload(w.all(), d_wo[l, m])
                for (t0, n) in tiles:
                    tl = t0 - T0
                    py = bank().s(0, n)
                    P.mm(py, [(w.s(k * 128, k * 128 + 128), mrg.s(k * TP + tl, k * TP + tl + n)) for k in range(KC)])
                    P.act(ytmp.s(m * TP + tl, m * TP + tl + n), py, AF.Copy)
            postnorm_residual(l, "mpost", tiles, T0, TP, ytmp, sq, rstd, False)

        def bank_rows(ref, p0, p1):
            return Ref(ref.ap[p0:p1] if p0 == 0 and False else ref.ap[p0:p1, :], ref.atoms)

        def d_lat_out(l, mc, t0, n):
            return o_lat[l, :, mc * NT + t0: mc * NT + t0 + n]


        def reduce_sum(out, in_):
            P.record("dve", (lambda e, o=out, i=in_: e.tensor_reduce(o.ap, i.ap, mybir.AxisListType.X, ALU.add)),
                     [in_], [out])

        def branch_b(l, pi, T0, TP, tiles, ptiles, stile, nT, ybT, carryb, sq, rstd, tf):
            PTN = sum(n for (_, n) in ptiles)
            W = PADB + PTN
            WS = PADB + DSEQ
            wgl = AR.alloc(KC * 512, BF16, "wgl")
            wload(wgl.all(), d_wglu[l])
            diag = AR.alloc(CBW * 2 * 128, BF16, "diagb")
            xbp = AR.alloc(2 * W, BF16, "xbp")
            xbs = AR.alloc(2 * WS, BF16, "xbs")
            cbfs = [rstd, tf[3]]
            mean = tf[0]
            rs = tf[1]
            cbst = AR.alloc(2 * 2 * PADB, F32, "cbst")
            for j in range(CBW):
                for c in range(2):
                    P.ts(diag.s((j * 2 + c) * 128, (j * 2 + c) * 128 + 128), identf.all(), vcol(l, "cbw", j * 2 + c), ALU.mult)
            for c in range(2):
                if pi == 0:
                    P.memset(xbp.s(c * W, c * W + PADB), 0.0, eng="dve")
                else:
                    P.copy(xbp.s(c * W, c * W + PADB), carryb.s(c * PADB, (c + 1) * PADB), eng="dve")
                    wload(xbs.s(c * WS, c * WS + PADB), d_scb[l][:, c * PADB:(c + 1) * PADB])
                    P.dma("sp", cbst.s(c * 60 + 30, c * 60 + 44), d_scb[l][:, c * PADB + 16:c * PADB + 30])
            for (t0, n) in tiles:
                tl = t0 - T0
                samp = t0 >= SEQ
                for c in range(2):
                    pa = bank().s(0, n)
                    pg = bank().s(0, n)
                    P.mm(pa, [(wgl.s(k * 512 + c * 128, k * 512 + c * 128 + 128), nT.s(k * TP + tl, k * TP + tl + n)) for k in range(KC)])
                    P.mm(pg, [(wgl.s(k * 512 + 256 + c * 128, k * 512 + 256 + c * 128 + 128), nT.s(k * TP + tl, k * TP + tl + n)) for k in range(KC)])
                    sg = tf[c].s(0, n)
                    P.act(sg, pg, AF.Sigmoid)
                    if samp:
                        dst = xbs.s(c * WS + PADB, c * WS + PADB + n)
                        P.tt(cbst.s(c * 60 + 44, c * 60 + 60), pa, sg, ALU.mult)
                    else:
                        dst = xbp.s(c * W + PADB + tl, c * W + PADB + tl + n)
                        if t0 + n == SEQ:
                            P.tt(cbst.s(c * 60, c * 60 + 30), Ref(pa.ap[:, n - 30:n], pa.atoms), tf[c].s(n - 30, n), ALU.mult)
                    P.tt(dst, pa, sg, ALU.mult)
                for c in range(2):
                    ps = bank().s(0, n)
                    if samp:
                        P.mm(ps, [(diag.s((j * 2 + c) * 128, (j * 2 + c) * 128 + 128), xbs.s(c * WS + j, c * WS + j + n)) for j in range(CBW)])
                    else:
                        P.mm(ps, [(diag.s((j * 2 + c) * 128, (j * 2 + c) * 128 + 128), xbp.s(c * W + tl + j, c * W + tl + j + n)) for j in range(CBW)])
                    P.act(cbfs[c].s(0, n), ps, AF.Identity, bias=vcol(l, "cbb", c))
                    P.act(sq.s(c * 512, c * 512 + n), cbfs[c].s(0, n), AF.Copy)
                    P.act(sq.s((2 + c) * 512, (2 + c) * 512 + n), cbfs[c].s(0, n), AF.Square)
                pm = bank().s(0, n)
                pq = bank().s(0, n)
                P.mm(pm, [(onesb.all(), sq.s(c * 512, c * 512 + n)) for c in range(2)])
                P.mm(pq, [(onesb.all(), sq.s((2 + c) * 512, (2 + c) * 512 + n)) for c in range(2)])
                mn = mean.s(0, n)
                r = rs.s(0, n)
                m2 = tf[2].s(0, n)
                P.ts(mn, pm, 1.0 / 256, ALU.mult)
                P.tt(m2, mn, mn, ALU.mult)
                P.stt(r, pq, 1.0 / 256, m2, ALU.mult, ALU.subtract)
                P.act(r, r, AF.Ln, bias=epsb[1.0])
                P.act(r, r, AF.Exp, scale=-0.5)
                for c in range(2):
                    x = cbfs[c].s(0, n)
                    P.tt(x, x, mn, ALU.subtract)
                    P.tt(x, x, r, ALU.mult)
                    P.act(ybT.s(c * TP + tl, c * TP + tl + n), x, AF.Silu, scale=vcol(l, "cblg", c), bias=vcol(l, "cblb", c))
            if pi == 0:
                for c in range(2):
                    P.copy(carryb.s(c * PADB, (c + 1) * PADB), xbp.s(c * W + PTN, c * W + PTN + PADB), eng="dve")
            else:
                P.dma("sp", o_cb[l], cbst.all())

        def branch_c(l, pi, T0, TP, tiles, ptiles, stile, nT, ycT, sq, rstd, tf):
            wuv = AR.alloc(KC * 512, BF16, "wuv")
            wload(wuv.all(), d_wuv[l])
            wsT = AR.alloc(512, BF16, "wsT")
            wload(wsT.all(), d_wsT[l])
            bsr = AR.alloc(512, BF16, "bsr")
            wload(bsr.s(0, 512, 0, 1), d_bs[l])
            for g in range(4):
                P.memset(wsT.s(g * 128, g * 128 + 64, 64, 128), 0.0, eng="dve")
            uT = AR.alloc(2 * TP, BF16, "uT")
            vnT = AR.alloc(2 * 512, BF16, "vnT")
            vtm = [AR.alloc(256, BF16, "vtm%d" % i) for i in range(2)]
            gvst = AR.alloc(2 * DSEQ, F32, "gvst")
            vfs = [rstd, tf[3]]
            mean, rs = tf[0], tf[1]
            bcount = 0
            for (t0, n) in tiles:
                tl = t0 - T0
                samp = t0 >= SEQ
                for c in range(2):
                    pu = bank().s(0, n)
                    P.mm(pu, [(wuv.s(k * 512 + c * 128, k * 512 + c * 128 + 128), nT.s(k * TP + tl, k * TP + tl + n)) for k in range(KC)])
                    P.act(uT.s(c * TP + tl, c * TP + tl + n), pu, AF.Copy)
                for c in range(2):
                    pv = bank().s(0, n)
                    P.mm(pv, [(wuv.s(k * 512 + 256 + c * 128, k * 512 + 256 + c * 128 + 128), nT.s(k * TP + tl, k * TP + tl + n)) for k in range(KC)])
                    P.act(vfs[c].s(0, n), pv, AF.Copy)
                    P.act(sq.s(c * 512, c * 512 + n), vfs[c].s(0, n), AF.Copy)
                    P.act(sq.s((2 + c) * 512, (2 + c) * 512 + n), vfs[c].s(0, n), AF.Square)
                pm = bank().s(0, n)
                pq = bank().s(0, n)
                P.mm(pm, [(onesb.all(), sq.s(c * 512, c * 512 + n)) for c in range(2)])
                P.mm(pq, [(onesb.all(), sq.s((2 + c) * 512, (2 + c) * 512 + n)) for c in range(2)])
                mn = mean.s(0, n)
                r = rs.s(0, n)
                m2 = tf[2].s(0, n)
                P.ts(mn, pm, 1.0 / 256, ALU.mult)
                P.tt(m2, mn, mn, ALU.mult)
                P.stt(r, pq, 1.0 / 256, m2, ALU.mult, ALU.subtract)
                P.act(r, r, AF.Ln, bias=epsb[1.0])
                P.act(r, r, AF.Exp, scale=-0.5)
                for c in range(2):
                    x = vfs[c].s(0, n)
                    P.tt(x, x, mn, ALU.subtract)
                    P.tt(x, x, r, ALU.mult)
                    if samp:
                        P.act(gvst.s(c * DSEQ, (c + 1) * DSEQ), x, AF.Identity, scale=vcol(l, "vng", c), bias=vcol(l, "vnb", c))
                    P.act(vnT.s(c * 512, c * 512 + n), x, AF.Identity, scale=vcol(l, "vng", c), bias=vcol(l, "vnb", c))
                if samp:
                    P.dma("sp", o_gv[l], gvst.all())
                bsz = min(n, 128)
                for bi in range(max(1, n // 128)):
                    c0 = tl + bi * 128
                    v16 = vtm[bcount % 2]
                    bcount += 1
                    pt = bank().s(0, 256, 0, bsz)
                    for ch in range(2):
                        P.mm(Ref(pt.ap[:, ch * 128:(ch + 1) * 128], pt.atoms),
                             [(vnT.s(ch * 512 + bi * 128, ch * 512 + bi * 128 + bsz), identb.all())])
                    P.copy(v16.s(0, 256, 0, bsz), pt, eng="dve")
                    for ch in range(2):
                        for gg in range(2):
                            g = 2 * ch + gg
                            psg = bank().s(0, bsz)
                            P.mm(psg, [(v16.s(ch * 128, ch * 128 + 128, 0, bsz), wsT.s(g * 128, g * 128 + bsz, 0, bsz)),
                                       (onesb.s(0, 128, 0, 1), bsr.s(g * 128, g * 128 + bsz, 0, 1))])
                            r0, r1 = gg * 64, gg * 64 + 64
                            P.tt(ycT.s(ch * TP + c0, ch * TP + c0 + bsz, r0, r1), Ref(psg.ap[r0:r1, :], psg.atoms),
                                 uT.s(ch * TP + c0, ch * TP + c0 + bsz, r0, r1), ALU.mult)

        def branch_d(l, pi, T0, TP, tiles, ptiles, stile, nT, ydT, carryd, tf):
            PTN = sum(n for (_, n) in ptiles)
            W = PADD + PTN
            WS = PADD + DSEQ
            wdp = AR.alloc(KC * 768, BF16, "wdp")
            wload(wdp.all(), d_wdp[l])
            diag = AR.alloc(CDW * 2 * 128, BF16, "diagd")
            xdp = AR.alloc(2 * W, BF16, "xdp")
            xds = AR.alloc(2 * WS, BF16, "xds")
            bgT = AR.alloc(2 * TP, BF16, "bgT")
            cdst = AR.alloc(2 * 2 * PADD, F32, "cdst")
            for j in range(CDW):
                for c in range(2):
                    P.ts(diag.s((j * 2 + c) * 128, (j * 2 + c) * 128 + 128), identf.all(), vcol(l, "cdw", j * 2 + c), ALU.mult)
            for c in range(2):
                if pi == 0:
                    P.memset(xdp.s(c * W, c * W + PADD), 0.0, eng="dve")
                else:
                    P.copy(xdp.s(c * W, c * W + PADD), carryd.s(c * PADD, (c + 1) * PADD), eng="dve")
                    wload(xds.s(c * WS, c * WS + PADD), d_scd[l][:, c * PADD:(c + 1) * PADD])
            for (t0, n) in tiles:
                tl = t0 - T0
                samp = t0 >= SEQ
                for c in range(2):
                    pb = bank().s(0, n)
                    pc = bank().s(0, n)
                    ph = bank().s(0, n)
                    for ps_, off in ((pb, 0), (pc, 256), (ph, 512)):
                        P.mm(ps_, [(wdp.s(k * 768 + off + c * 128, k * 768 + off + c * 128 + 128), nT.s(k * TP + tl, k * TP + tl + n)) for k in range(KC)])
                    P.act(bgT.s(c * TP + tl, c * TP + tl + n), pb, AF.Copy)
                    cg = tf[c].s(0, n)
                    P.act(cg, pc, AF.Copy)
                    if samp:
                        dst = xds.s(c * WS + PADD, c * WS + PADD + n)
                        P.tt(cdst.s(c * 4 + 2, c * 4 + 4), Ref(ph.ap[:, n - 2:n], ph.atoms), tf[c].s(n - 2, n), ALU.mult)
                    else:
                        dst = xdp.s(c * W + PADD + tl, c * W + PADD + tl + n)
                        if t0 + n == SEQ:
                            P.tt(cdst.s(c * 4, c * 4 + 2), Ref(ph.ap[:, n - 2:n], ph.atoms), tf[c].s(n - 2, n), ALU.mult)
                    P.tt(dst, ph, cg, ALU.mult)
                for c in range(2):
                    ps = bank().s(0, n)
                    if samp:
                        P.mm(ps, [(diag.s((j * 2 + c) * 128, (j * 2 + c) * 128 + 128), xds.s(c * WS + j, c * WS + j + n)) for j in range(CDW)])
                    else:
                        P.mm(ps, [(diag.s((j * 2 + c) * 128, (j * 2 + c) * 128 + 128), xdp.s(c * W + tl + j, c * W + tl + j + n)) for j in range(CDW)])
                    P.tt(ydT.s(c * TP + tl, c * TP + tl + n), ps, bgT.s(c * TP + tl, c * TP + tl + n), ALU.mult)
            if pi == 0:
                for c in range(2):
                    P.copy(carryd.s(c * PADD, (c + 1) * PADD), xdp.s(c * W + PTN, c * W + PTN + PADD), eng="dve")
            else:
                P.dma("sp", o_cd[l], cdst.all())

        for l in range(n_layers):
            early = None
            if do_mixer and do_ffn is True:
                def early(sq_, rstd_, l=l):
                    (TA0, TPA, tilesA) = PASSES[0]
                    prenorm(l, "f2pre", tilesA, TA0, TPA, Buf(P, "A", 0, KC * TPA, BF16, "nTA_early"), sq_, rstd_)
            gen = mixer_layer(l, early) if do_mixer else None
            if do_ffn:
                ffn_pair(l, 0, next_pre=(lambda: next(gen)) if gen is not None else None)
            elif gen is not None:
                next(gen)
            if gen is not None:
                for _ in gen:
                    pass
            if do_ffn is True:
                ffn_pair(l, 1, skip_preA=(early is not None))
        for k in range(KC):
            P.dma("sp", o_y[:, k * NT:(k + 1) * NT], xT.s(k * NT, (k + 1) * NT))
        P.barrier_on("sp", [xT.all(), Buf(P, "A", 0, ASIZE, F32).all()])
        P.mark("end")
        P.emit()
        import json as _json
        if os.environ.get("MARKS"):
            _json.dump(P.marks, open(os.environ["MARKS"], "w"))
        print("arena peak", AR.peak, "of", ASIZE, {e: len(P.ops[e]) for e in ENGINES}, "dma sems", len(P.dma_sems))
    return nc


def _slab(W, ms):
    K, M = W.shape
    return np.ascontiguousarray(
        W.reshape(K // 128, 128, M // ms, ms).transpose(2, 1, 0, 3)).reshape(M // ms, 128, (K // 128) * ms)


def _rope_tables():
    half = RD // 2
    inv = np.exp(-math.log(10000.0) * np.arange(half, dtype=np.float32) / half).astype(np.float32)
    pos = np.arange(NT, dtype=np.float32)
    ang = (pos[:, None] * inv[None, :]).astype(np.float32)
    c, s = np.cos(ang).astype(np.float32), np.sin(ang).astype(np.float32)
    cosT = np.concatenate([c, c], axis=1).T
    sinS = np.concatenate([-s, s], axis=1).T
    return np.ascontiguousarray(cosT), np.ascontiguousarray(sinS)


def prep_shared(I):
    f = lambda a: np.asarray(a, dtype=np.float32)
    S = {}
    wgu = np.empty((L, 2, FC, 128, KC * 256), np.float32)
    wdn = np.empty((L, 2, 8, 128, FC * 128), np.float32)
    for l in range(L):
        for fi, (gu, dn) in enumerate((("ffn1_w_gu", "ffn1_w_down"), ("ffn2_w_gu", "ffn2_w_down"))):
            W = f(I[gu][l]).reshape(KC, 128, 2 * DFF)
            G = W[:, :, :DFF].reshape(KC, 128, FC, 128)
            U = W[:, :, DFF:].reshape(KC, 128, FC, 128)
            wgu[l, fi] = np.stack([G, U], axis=3).transpose(2, 1, 0, 3, 4).reshape(FC, 128, KC * 256)
            wdn[l, fi] = _slab(f(I[dn][l]), 128)
    S["wgu"], S["wdn"] = wgu, wdn
    win = f(I["w_in"])
    S["wcq"] = np.stack([_slab(win[l][:, 0:384], 384)[0] for l in range(L)])
    S["wckv"] = np.stack([_slab(win[l][:, 384:640], 256)[0] for l in range(L)])
    wkr = np.zeros((L, 2, 1024, 96), np.float32)
    wkr[:, 0, :, 64:96] = win[:, :, 640:672]
    wkr[:, 1, :, 64:80] = win[:, :, 656:672]
    wkr[:, 1, :, 80:96] = win[:, :, 640:656]
    S["wkr"] = np.stack([np.stack([_slab(wkr[l, i], 96)[0] for i in range(2)]) for l in range(L)])
    S["wglu"] = np.stack([_slab(win[l][:, 672:1184], 512)[0] for l in range(L)])
    S["wuv"] = np.stack([_slab(win[l][:, 1184:1696], 512)[0] for l in range(L)])
    S["wdp"] = np.stack([_slab(win[l][:, 1696:2464], 768)[0] for l in range(L)])
    wgm = np.empty((L, 8, 128, KC * 512 + 10 * 128), np.float32)
    for l in range(L):
        Wg = win[l][:, 2464:].reshape(KC, 128, 4, 8, 128).transpose(3, 1, 0, 2, 4).reshape(8, 128, KC * 512)
        Wbr = np.concatenate([f(I["w_br_a"][l]), f(I["w_br_b"][l]), f(I["w_br_c"][l]), f(I["w_br_d"][l])], axis=0)
        Wb = Wbr.reshape(10, 128, 8, 128).transpose(2, 1, 0, 3).reshape(8, 128, 10 * 128)
        wgm[l] = np.concatenate([Wg, Wb], axis=2)
    S["wgm"] = wgm
    S["wo"] = np.stack([_slab(f(I["w_o"][l]), 128) for l in range(L)])
    wuq = f(I["w_uq"])
    wuq_sw = wuq.reshape(L, QL, NH, 96).copy()
    wuq_sw[..., 64:80] = wuq.reshape(L, QL, NH, 96)[..., 80:96]
    wuq_sw[..., 80:96] = wuq.reshape(L, QL, NH, 96)[..., 64:80]
    wuq_sw = wuq_sw.reshape(L, QL, NH * 96)
    S["wuq"] = np.stack([np.stack([_slab(wuq[l], 768)[0], _slab(wuq_sw[l], 768)[0]]) for l in range(L)])
    wukv = f(I["w_ukv"]).reshape(L, 2, 128, NH, 128)
    S["wuk"] = np.ascontiguousarray(wukv[..., :64].transpose(0, 2, 1, 3, 4)).reshape(L, 128, 2 * NH * 64)
    S["wuv2"] = np.ascontiguousarray(wukv[..., 64:].transpose(0, 2, 1, 3, 4)).reshape(L, 128, 2 * 512)
    w3 = f(I["w_ukv"]).reshape(L, KVL, NH, 128)[..., :64]
    S["wukT"] = np.ascontiguousarray(w3.transpose(0, 3, 2, 1)).reshape(L, 64, NH * 256)
    S["wsT"] = np.ascontiguousarray(f(I["gmlp_w_s"]).transpose(0, 3, 1, 2)).reshape(L, 128, 512)
    S["bsrow"] = np.ascontiguousarray(f(I["gmlp_b_s"])).reshape(L, 1, 512)
    gvb = np.concatenate([f(I["gmlp_vn_g"]), f(I["gmlp_vn_b"])], axis=1)
    S["gvb"] = np.ascontiguousarray(np.broadcast_to(gvb[:, None, :], (L, 128, 512)))
    vecs = np.zeros((128, L * NV), np.float32)

    def put(l, name, v, w):
        v = f(v).reshape(w, 128).T
        vecs[:, l * NV + VOFF[name]: l * NV + VOFF[name] + w] = v
    for l in range(L):
        put(l, "f1pre", I["ffn1_norm_pre"][l], 8)
        put(l, "f1post", I["ffn1_norm_post"][l], 8)
        put(l, "mpre", I["mix_norm_pre"][l], 8)
        put(l, "mpost", I["mix_norm_post"][l], 8)
        put(l, "f2pre", I["ffn2_norm_pre"][l], 8)
        put(l, "f2post", I["ffn2_norm_post"][l], 8)
        put(l, "qn", I["q_norm"][l], 3)
        put(l, "kvn", I["kv_norm"][l], 2)
        put(l, "cbb", I["conv_b_bias"][l], 2)
        put(l, "cblg", I["conv_b_ln_g"][l], 2)
        put(l, "cblb", I["conv_b_ln_b"][l], 2)
        put(l, "cbw", I["conv_b_w"][l], CBW * 2)
        put(l, "cdw", I["conv_d_w"][l], CDW * 2)
        put(l, "vng", I["gmlp_vn_g"][l], 2)
        put(l, "vnb", I["gmlp_vn_b"][l], 2)
    S["vecs"] = vecs
    S["ident"] = np.eye(128, dtype=np.float32)
    S["cosT"], S["sinS"] = _rope_tables()
    return S


def prep_core(I, c):
    f = lambda a: np.asarray(a, dtype=np.float32)
    C = {}
    X = np.concatenate([f(I["x_prompt"][c]), f(I["x_sample"][c])], axis=0)
    C["xT"] = np.ascontiguousarray(X.T.reshape(KC, 128, NT).transpose(1, 0, 2)).reshape(128, KC * NT)
    cl = f(I["cache_kv_latent"][:, c])
    C["clatT"] = np.ascontiguousarray(cl.transpose(0, 2, 1).reshape(L, 2, 128, SEQ).transpose(0, 2, 1, 3)).reshape(L, 128, 2 * SEQ)
    C["clat"] = np.ascontiguousarray(cl.reshape(L, 16, 128, 256).transpose(0, 2, 1, 3)).reshape(L, 128, 16 * 256)
    C["ckrT"] = np.ascontiguousarray(f(I["cache_k_rope"][:, c]).transpose(0, 2, 1))
    sb = f(I["state_conv_b"][:, c])
    C["scbT"] = np.ascontiguousarray(sb.transpose(0, 2, 1).reshape(L, 2, 128, PADB).transpose(0, 2, 1, 3)).reshape(L, 128, 2 * PADB)
    sd = f(I["state_conv_d"][:, c])
    C["scdT"] = np.ascontiguousarray(sd.transpose(0, 2, 1).reshape(L, 2, 128, PADD).transpose(0, 2, 1, 3)).reshape(L, 128, 2 * PADD)
    return C


def assemble(results):
    n = len(results)
    yp = np.empty((n, SEQ, D), np.float32)
    ys = np.empty((n, DSEQ, D), np.float32)
    latp = np.empty((L, n, SEQ, KVL), np.float32)
    lats = np.empty((L, n, DSEQ, KVL), np.float32)
    krp = np.empty((L, n, SEQ, RD), np.float32)
    krs = np.empty((L, n, DSEQ, RD), np.float32)
    cbp = np.empty((L, n, PADB, 256), np.float32)
    cbs = np.empty((L, n, PADB, 256), np.float32)
    cdp = np.empty((L, n, PADD, 256), np.float32)
    cds = np.empty((L, n, PADD, 256), np.float32)
    gv = np.empty((L, n, DSEQ, 256), np.float32)
    for c, r in enumerate(results):
        y = np.asarray(r["yT"]).reshape(128, KC, NT).transpose(2, 1, 0).reshape(NT, D)
        yp[c], ys[c] = y[:SEQ], y[SEQ:]
        lat = np.asarray(r["latT"]).reshape(L, 128, 2, NT).transpose(0, 3, 2, 1).reshape(L, NT, KVL)
        latp[:, c], lats[:, c] = lat[:, :SEQ], lat[:, SEQ:]
        kr = np.asarray(r["krT"]).transpose(0, 2, 1)
        krp[:, c], krs[:, c] = kr[:, :SEQ], kr[:, SEQ:]
        cb = np.asarray(r["cbT"]).reshape(L, 128, 2, 2, PADB).transpose(0, 3, 4, 2, 1).reshape(L, 2, PADB, 256)
        cbp[:, c], cbs[:, c] = cb[:, 0], cb[:, 1]
        cd = np.asarray(r["cdT"]).reshape(L, 128, 2, 2, PADD).transpose(0, 3, 4, 2, 1).reshape(L, 2, PADD, 256)
        cdp[:, c], cds[:, c] = cd[:, 0], cd[:, 1]
        gv[:, c] = np.asarray(r["gvT"]).reshape(L, 128, 2, DSEQ).transpose(0, 3, 2, 1).reshape(L, DSEQ, 256)
    return (yp, ys, latp, krp, cbp, cdp, lats, krs, cbs, gv, cds)


_NC_CACHE = {}


def kernel(**inputs):
    if "nc" not in _NC_CACHE:
        _NC_CACHE["nc"] = build_program()
    nc = _NC_CACHE["nc"]
    S = prep_shared(inputs)
    in_maps = []
    for c in range(8):
        m = dict(S)
        m.update(prep_core(inputs, c))
        in_maps.append(m)
    res = run_bass_kernel_spmd(nc, in_maps, core_ids=list(range(8)))
    return assemble(res.results)
```

```python
import contextlib
import math
import numpy as np
import concourse.bass as bass
import concourse.mybir as mybir
from concourse.bass_utils import run_bass_kernel_spmd

F32 = mybir.dt.float32
BF16 = mybir.dt.bfloat16
AF = mybir.ActivationFunctionType
ALU = mybir.AluOpType

ATOM = 8


class Ref:
    __slots__ = ("ap", "atoms")

    def __init__(self, ap, atoms):
        self.ap = ap
        self.atoms = atoms


class Buf:
    def __init__(self, prog, space, base32, ncols, dt, name=""):
        self.prog = prog
        self.space = space
        self.base32 = base32
        self.ncols = ncols
        self.dt = dt
        self.name = name
        self.esz = 2 if dt == BF16 else 4
        n32 = (ncols * self.esz + 3) // 4
        self.n32 = n32
        t = prog.tensors[space]
        ap = t[:, base32:base32 + n32]
        if dt != F32:
            ap = ap.bitcast(dt)
        self.full = ap

    def s(self, c0, c1, p0=0, p1=128, step=None):
        assert 0 <= c0 < c1 <= self.ncols, (self.name, c0, c1, self.ncols)
        if step is None:
            ap = self.full[p0:p1, c0:c1]
        else:
            ap = self.full[p0:p1, c0:c1:step]
        asz = (512 if self.space == "PS" else ATOM) * 4
        a0 = (self.base32 * 4 + c0 * self.esz) // asz
        a1 = (self.base32 * 4 + c1 * self.esz - 1) // asz
        return Ref(ap, [(self.space, a) for a in range(a0, a1 + 1)])

    def all(self):
        return self.s(0, self.ncols)


class Op:
    __slots__ = ("eng", "fn", "waits", "need_sig", "sigval", "dma_sem", "dma_val", "eidx")

    def __init__(self, eng, fn):
        self.eng = eng
        self.fn = fn
        self.waits = []
        self.need_sig = False
        self.sigval = None
        self.dma_sem = None
        self.dma_val = None
        self.eidx = None


ENGINES = ("pe", "act", "dve", "pool", "sp")


class Prog:
    def __init__(self, nc):
        self.nc = nc
        self.tensors = {}
        self.ops = {e: [] for e in ENGINES}
        self.last_w = {}
        self.readers = {}
        self.seen = {e: {} for e in ENGINES}
        self.dma_sems = {}
        self.n_dma_sems = 0
        self.rr = 0
        self.n_mm = 0
        self.marks = []

    def _add_wait(self, op, dep, raw):
        if dep is op:
            return
        e = op.eng
        if dep.dma_sem is not None:
            key = ("dma", dep.dma_sem)
            if self.seen[e].get(key, 0) >= dep.dma_val:
                return
            self.seen[e][key] = dep.dma_val
            op.waits.append(("dma", dep.dma_sem, dep.dma_val))
            return
        if dep.eng == e and op.dma_sem is None:
            if e == "pe":
                return
        key = ("eng", dep.eng)
        if self.seen[e].get(key, -1) >= dep.eidx:
            return
        self.seen[e][key] = dep.eidx
        dep.need_sig = True
        op.waits.append(("op", dep))

    def record(self, eng, fn, reads=(), writes=(), dma_key=None):
        op = Op(eng, fn)
        op.eidx = len(self.ops[eng])
        if dma_key is not None:
            v = self.dma_sems.get(dma_key, 0) + 16
            self.dma_sems[dma_key] = v
            op.dma_sem = dma_key
            op.dma_val = v
        for r in reads:
            for a in r.atoms:
                w = self.last_w.get(a)
                if w is not None:
                    self._add_wait(op, w, True)
        for r in writes:
            for a in r.atoms:
                w = self.last_w.get(a)
                if w is not None:
                    self._add_wait(op, w, False)
                for rd in self.readers.get(a, ()):
                    self._add_wait(op, rd, False)
        for r in reads:
            for a in r.atoms:
                lst = self.readers.setdefault(a, [])
                if not lst or lst[-1] is not op:
                    lst.append(op)
        for r in writes:
            for a in r.atoms:
                self.last_w[a] = op
                self.readers[a] = []
        self.ops[eng].append(op)
        return op

    def mm(self, out, pairs, start=True, stop=True):
        reads = []
        for l, r in pairs:
            reads.append(l)
            reads.append(r)
        n = len(pairs)
        self.n_mm += n

        def fn(eng, out=out, pairs=pairs):
            ins = None
            for i, (l, r) in enumerate(pairs):
                ins = eng.matmul(out.ap, l.ap, r.ap, start=(start and i == 0),
                                 stop=(stop and i == n - 1))
            return ins
        return self.record("pe", fn, reads, [out])

    def act(self, out, in_, func, scale=1.0, bias=0.0, accum=None, eng="act"):
        reads = [in_]
        sc = scale.ap if isinstance(scale, Ref) else scale
        bi = bias.ap if isinstance(bias, Ref) else bias
        if isinstance(scale, Ref):
            reads.append(scale)
        if isinstance(bias, Ref):
            reads.append(bias)
        writes = [out]
        if accum is not None:
            writes.append(accum)

        def fn(eng, out=out, in_=in_):
            kw = {}
            if accum is not None:
                kw["accum_out"] = accum.ap
            return eng.activation(out.ap, in_.ap, func, bias=bi, scale=sc, **kw)
        return self.record("act", fn, reads, writes)

    def tt(self, out, a, b, op, eng="dve"):
        def fn(e, out=out, a=a, b=b):
            return e.tensor_tensor(out.ap, a.ap, b.ap, op)
        return self.record(eng, fn, [a, b], [out])

    def stt(self, out, in0, scalar, in1, op0, op1, eng="dve"):
        reads = [in0, in1]
        sc = scalar.ap if isinstance(scalar, Ref) else scalar
        if isinstance(scalar, Ref):
            reads.append(scalar)

        def fn(e, out=out, in0=in0, in1=in1):
            return e.scalar_tensor_tensor(out.ap, in0.ap, sc, in1.ap, op0, op1)
        return self.record(eng, fn, reads, [out])

    def ts(self, out, in0, s1, op0, s2=None, op1=None, eng="dve"):
        reads = [in0]
        a1 = s1.ap if isinstance(s1, Ref) else s1
        a2 = s2.ap if isinstance(s2, Ref) else s2
        if isinstance(s1, Ref):
            reads.append(s1)
        if isinstance(s2, Ref):
            reads.append(s2)

        def fn(e, out=out, in0=in0):
            if op1 is None:
                return e.tensor_scalar(out.ap, in0.ap, a1, None, op0)
            return e.tensor_scalar(out.ap, in0.ap, a1, a2, op0, op1)
        return self.record(eng, fn, reads, [out])

    def copy(self, out, in_, eng="dve"):
        def fn(e, out=out, in_=in_):
            return e.tensor_copy(out.ap, in_.ap)
        return self.record(eng, fn, [in_], [out])

    def memset(self, out, val, eng="pool"):
        def fn(e, out=out):
            return e.memset(out.ap, val)
        return self.record(eng, fn, [], [out])

    def dma(self, queue, out, in_, key=None):
        reads = [in_] if isinstance(in_, Ref) else []
        writes = [out] if isinstance(out, Ref) else []
        oap = out.ap if isinstance(out, Ref) else out
        iap = in_.ap if isinstance(in_, Ref) else in_
        if key is None:
            key = ("auto", self.rr % 24)
            self.rr += 1

        def fn(e):
            return e.dma_start(out=oap, in_=iap)
        prev = self.dma_sems.get(key, 0)
        op = self.record(queue, fn, reads, writes, dma_key=key)
        if prev > 0 and self.seen[queue].get(("dma", key), 0) < prev:
            self.seen[queue][("dma", key)] = prev
            op.waits.append(("dma", key, prev))
        return op

    def mark(self, label):
        self.marks.append((label, self.n_mm))

    def barrier_on(self, eng, refs):
        return self.record(eng, None, refs, refs, dma_key=None)

    def emit(self):
        nc = self.nc
        with contextlib.ExitStack() as st:
            esem = {e: st.enter_context(nc.semaphore("sem_" + e)) for e in ENGINES}
            dsem = {}
            for k in self.dma_sems:
                dsem[k] = st.enter_context(nc.semaphore("dsem%d" % len(dsem)))
            for e in ENGINES:
                c = 0
                for op in self.ops[e]:
                    if op.need_sig:
                        c += 1
                        op.sigval = c
            block = st.enter_context(nc.Block())
            bmap = {"pe": block.tensor, "act": block.scalar, "dve": block.vector,
                    "pool": block.gpsimd, "sp": block.sync}

            def make(ename):
                def body(eng):
                    for op in self.ops[ename]:
                        for w in op.waits:
                            if w[0] == "dma":
                                eng.wait_ge(dsem[w[1]], w[2])
                            else:
                                d = w[1]
                                eng.wait_ge(esem[d.eng], d.sigval)
                        if op.fn is None:
                            continue
                        ins = op.fn(eng)
                        if op.dma_sem is not None:
                            ins.then_inc(dsem[op.dma_sem], 16)
                        elif op.need_sig:
                            ins.then_inc(esem[ename], 1)
                return body
            for e in ENGINES:
                if self.ops[e]:
                    bmap[e](make(e))


D = 1024
KC = 8
SEQ = 2048
DSEQ = 16
NT = SEQ + DSEQ
L = 4
DFF = 2816
FC = DFF // 128
QL, KVL, RD = 384, 256, 32
NH = 8
EPS = 1e-6
SM_SCALE = (64 + 32) ** -0.5
CBW, CDW = 31, 3
PADB, PADD = CBW - 1, CDW - 1

VOFF = {}
_o = 0
for _n, _w in (("f1pre", 8), ("f1post", 8), ("mpre", 8), ("mpost", 8), ("f2pre", 8), ("f2post", 8),
               ("qn", 3), ("kvn", 2), ("cbb", 2), ("cblg", 2), ("cblb", 2), ("cbw", CBW * 2), ("cdw", CDW * 2),
               ("vng", 2), ("vnb", 2)):
    VOFF[_n] = _o
    _o += _w
NV = _o

PASSES = (
    (0, 1024, ((0, 512), (512, 512))),
    (1024, 1040, ((1024, 512), (1536, 512), (2048, 16))),
)


class Arena:
    def __init__(self, prog, space, size):
        self.prog, self.space, self.size = prog, space, size
        self.top = 0
        self.peak = 0

    def alloc(self, ncols, dt, name=""):
        esz = 2 if dt == BF16 else 4
        n32 = (ncols * esz + 3) // 4
        n32 = (n32 + 127) // 128 * 128
        assert self.top + n32 <= self.size, ("arena overflow", name, self.top, n32, self.size)
        b = Buf(self.prog, self.space, self.top, ncols, dt, name)
        self.top += n32
        self.peak = max(self.peak, self.top)
        return b

    def at(self, base32, ncols, dt, name=""):
        return Buf(self.prog, self.space, base32, ncols, dt, name)


import os


def build_program(n_layers=L, do_mixer=True, do_ffn=True, stop=99):
    nc = bass.Bass("TRN2", target_bir_lowering=False)

    def din(name, shape):
        return nc.dram_tensor(name, list(shape), F32, kind="ExternalInput").ap()

    def dout(name, shape):
        return nc.dram_tensor(name, list(shape), F32, kind="ExternalOutput").ap()

    d_x = din("xT", (128, KC * NT))
    d_vecs = din("vecs", (128, L * NV))
    d_ident = din("ident", (128, 128))
    d_cos = din("cosT", (32, NT))
    d_sin = din("sinS", (32, NT))
    d_wgu = din("wgu", (L, 2, FC, 128, KC * 256))
    d_wdn = din("wdn", (L, 2, 8, 128, FC * 128))
    d_wcq = din("wcq", (L, 128, KC * QL))
    d_wckv = din("wckv", (L, 128, KC * KVL))
    d_wkr = din("wkr", (L, 2, 128, KC * 96))
    d_wglu = din("wglu", (L, 128, KC * 512))
    d_wuv = din("wuv", (L, 128, KC * 512))
    d_wdp = din("wdp", (L, 128, KC * 768))
    d_wgm = din("wgm", (L, 8, 128, KC * 512 + 10 * 128))
    d_wo = din("wo", (L, 8, 128, KC * 128))
    d_wuq = din("wuq", (L, 2, 128, 3 * 768))
    d_wuk = din("wuk", (L, 128, 2 * NH * 64))
    d_wuv2 = din("wuv2", (L, 128, 2 * 512))
    d_wukT = din("wukT", (L, 64, NH * 256))
    d_wsT = din("wsT", (L, 128, 512))
    d_bs = din("bsrow", (L, 1, 512))
    d_gvb = din("gvb", (L, 128, 512))
    d_clatT = din("clatT", (L, 128, 2 * SEQ))
    d_clat = din("clat", (L, 128, 16 * 256))
    d_ckr = din("ckrT", (L, 32, SEQ))
    d_scb = din("scbT", (L, 128, 2 * PADB))
    d_scd = din("scdT", (L, 128, 2 * PADD))

    o_y = dout("yT", (128, KC * NT))
    o_lat = dout("latT", (L, 128, 2 * NT))
    o_kr = dout("krT", (L, 32, NT))
    o_cb = dout("cbT", (L, 128, 2 * 2 * PADB))
    o_cd = dout("cdT", (L, 128, 2 * 2 * PADD))
    o_gv = dout("gvT", (L, 128, 2 * DSEQ))

    with contextlib.ExitStack() as st:
        XCOLS = KC * NT
        CCOLS = L * NV + 128 + 64 + 64 + 2 * ATOM + 8
        CCOLS = (CCOLS + 127) // 128 * 128
        ASIZE = 35456
        t_x = st.enter_context(nc.sbuf_tensor("xres", [128, XCOLS], F32))
        t_c = st.enter_context(nc.sbuf_tensor("consts", [128, CCOLS], F32))
        t_a = st.enter_context(nc.sbuf_tensor("arena", [128, ASIZE], F32))
        t_p = st.enter_context(nc.psum_tensor("psum", [128, 8 * 512], F32))
        P = Prog(nc)
        P.tensors.update({"X": t_x, "C": t_c, "A": t_a, "PS": t_p})
        xT = Buf(P, "X", 0, XCOLS, F32, "xT")
        vecs = Buf(P, "C", 0, L * NV, F32, "vecs")
        identf = Buf(P, "C", L * NV, 128, F32, "identf")
        onesb = Buf(P, "C", L * NV + 128, 128, BF16, "ones")
        identb = Buf(P, "C", L * NV + 192, 128, BF16, "identb")
        banks = [Buf(P, "PS", i * 512, 512, F32, "bank%d" % i) for i in range(8)]
        AR = Arena(P, "A", ASIZE)
        state = {"bank": 0}

        def bank(lo=0, hi=8):
            b = banks[lo + state["bank"] % (hi - lo)]
            state["bank"] += 1
            return b

        def vcol(l, name, i=0):
            c = l * NV + VOFF[name] + i
            return vecs.s(c, c + 1)

        def xs(k, t0, n):
            return xT.s(k * NT + t0, k * NT + t0 + n)

        for k in range(KC):
            P.dma("sp", xT.s(k * NT, (k + 1) * NT), d_x[:, k * NT:(k + 1) * NT])
        P.dma("sp", vecs.all(), d_vecs)
        P.dma("sp", identf.all(), d_ident)
        epsb = {1.0: Buf(P, "C", L * NV + 256, 1, F32, "eps1"), 4.0: Buf(P, "C", L * NV + 256 + ATOM, 1, F32, "eps4")}
        for f2_, b_ in epsb.items():
            P.memset(b_.all(), EPS * f2_, eng="dve")
        epsb = {k_: v_.all() for k_, v_ in epsb.items()}
        P.memset(onesb.all(), 1.0, eng="dve")
        P.copy(identb.all(), identf.all(), eng="dve")

        wq = {"i": 0}

        def wload(dst, src):
            P.dma("pool", dst, src, key=("w", wq["i"] % 6))
            wq["i"] += 1

        def recip(out, in_):
            P.record("dve", (lambda e, o=out, i=in_: e.reciprocal(o.ap, i.ap)), [in_], [out])

        def rstd_from(srcs, n, Dn, out, sq, post=1.0):
            sqs = []
            for i, r in enumerate(srcs):
                q = sq.s(i * 512, i * 512 + n)
                P.act(q, r, AF.Square)
                sqs.append(q)
            ps = bank().s(0, n)
            P.mm(ps, [(onesb.all(), q) for q in sqs])
            f2 = 1.0 / (post * post)
            P.act(out, ps, AF.Ln, scale=f2 / Dn, bias=epsb[f2])
            P.act(out, out, AF.Exp, scale=-0.5)

        def prenorm(l, gname, tiles, T0, TP, nT, sq, rstd):
            for (t0, n) in tiles:
                tl = t0 - T0
                r = rstd.s(0, n)
                rstd_from([xs(k, t0, n) for k in range(KC)], n, D, r, sq)
                for k in range(KC):
                    P.stt(nT.s(k * TP + tl, k * TP + tl + n), xs(k, t0, n), vcol(l, gname, k), r,
                          ALU.mult, ALU.mult)

        def postnorm_residual(l, gname, tiles, T0, TP, ytmp, sq, rstd, half):
            for (t0, n) in tiles:
                tl = t0 - T0
                r = rstd.s(0, n)
                ys = [ytmp.s(k * TP + tl, k * TP + tl + n) for k in range(KC)]
                rstd_from(ys, n, D, r, sq, post=0.5 if half else 1.0)
                for k in range(KC):
                    P.stt(ys[k], ys[k], vcol(l, gname, k), r, ALU.mult, ALU.mult)
                    P.tt(xs(k, t0, n), xs(k, t0, n), ys[k], ALU.add)

        def ffn_pair(l, f, next_pre=None, skip_preA=False):
            pre, post = ("f1pre", "f1post") if f == 0 else ("f2pre", "f2post")
            (TA0, TPA, tilesA), (TB0, TPB, tilesB) = PASSES
            AR.top = 0
            ytmp = AR.alloc(KC * TPB, F32, "ytmp")
            nTA = AR.at(ytmp.base32, KC * TPA, BF16, "nTA")
            wgu = [AR.alloc(KC * 256, BF16, "wgu%d" % i) for i in range(3)]
            assert AR.top <= 12288
            nTB = AR.alloc(KC * TPB, BF16, "nTB")
            hT = AR.alloc(FC * TPB, BF16, "hT")
            wdn = [AR.alloc(FC * 128, BF16, "wdn%d" % i) for i in range(3)]
            sq = AR.alloc(KC * 512, BF16, "sq")
            rstd = AR.alloc(512, F32, "rstd")
            sg = [AR.alloc(512, F32, "sg%d" % i) for i in range(2)]
            cnt = {"s": 0}

            def up(T0, TP, tiles, nT, hook=None):
                for c in range(FC):
                    w = wgu[c % 3]
                    wload(w.all(), d_wgu[l, f, c])
                    for (t0, n) in tiles:
                        tl = t0 - T0
                        pg = bank().s(0, n)
                        pu = bank().s(0, n)
                        P.mm(pg, [(w.s(k * 256, k * 256 + 128), nT.s(k * TP + tl, k * TP + tl + n)) for k in range(KC)])
                        P.mm(pu, [(w.s(k * 256 + 128, k * 256 + 256), nT.s(k * TP + tl, k * TP + tl + n)) for k in range(KC)])
                        s_ = sg[cnt["s"] % 2].s(0, n)
                        cnt["s"] += 1
                        P.act(s_, pg, AF.Silu)
                        P.tt(hT.s(c * TP + tl, c * TP + tl + n), pu, s_, ALU.mult)
                    if hook is not None:
                        hook(c)

            def down(T0, TP, tiles):
                for m in range(8):
                    w = wdn[m % 3]
                    wload(w.all(), d_wdn[l, f, m])
                    for (t0, n) in tiles:
                        tl = t0 - T0
                        py = bank().s(0, n)
                        P.mm(py, [(w.s(k * 128, k * 128 + 128), hT.s(k * TP + tl, k * TP + tl + n)) for k in range(FC)])
                        P.act(ytmp.s(m * TP + tl, m * TP + tl + n), py, AF.Copy)

            def hookA(c):
                if c % 4 == 2 and c // 4 < len(tilesA):
                    postnorm_residual(l, post, [tilesA[c // 4]], TA0, TPA, ytmp, sq, rstd, True)

            P.mark("L%d ffn%d upA" % (l, f))
            if not skip_preA:
                prenorm(l, pre, tilesA, TA0, TPA, nTA, sq, rstd)
            up(TA0, TPA, tilesA, nTA)
            P.mark("L%d ffn%d downA" % (l, f))
            prenorm(l, pre, tilesB, TB0, TPB, nTB, sq, rstd)
            down(TA0, TPA, tilesA)
            P.mark("L%d ffn%d upB" % (l, f))
            up(TB0, TPB, tilesB, nTB, hook=hookA)
            P.mark("L%d ffn%d downB" % (l, f))
            down(TB0, TPB, tilesB)
            P.mark("L%d ffn%d end" % (l, f))
            if next_pre is not None:
                next_pre()
            postnorm_residual(l, post, tilesB, TB0, TPB, ytmp, sq, rstd, True)

        def mixer_layer(l, early_f2=None):
            AR.top = 0
            kT = AR.alloc(NH * SEQ, BF16, "kT")
            Vt = AR.alloc(16 * 512, BF16, "V")
            carryb = AR.alloc(2 * PADB, BF16, "carryb")
            carryd = AR.alloc(2 * PADD, BF16, "carryd")
            base_top = AR.top
            for pi, (T0, TP, tiles) in enumerate(PASSES):
                AR.top = base_top
                for _ in mixer_pass(l, pi, T0, TP, tiles, kT, Vt, carryb, carryd, early_f2):
                    if pi == 0:
                        yield

        def mixer_pass(l, pi, T0, TP, tiles, kT, Vt, carryb, carryd, early_f2=None):
            isB = pi == 1
            ptiles = [t for t in tiles if t[0] < SEQ]
            stile = (SEQ, DSEQ) if isB else None
            PT0 = T0
            PTN = sum(n for (_, n) in ptiles)
            ybase = AR.top
            nT = AR.alloc(KC * TP, BF16, "nT")
            aT = AR.alloc(4 * TP, BF16, "aT")
            ybT = AR.alloc(2 * TP, BF16, "ybT")
            ycT = AR.alloc(2 * TP, BF16, "ycT")
            ydT = AR.alloc(2 * TP, BF16, "ydT")
            need = (KC * TP * 4 - (AR.top - ybase) * 4)
            if need > 0:
                AR.alloc(need // 4, F32, "ypad")
            ytmp = AR.at(ybase, KC * TP, F32, "ytmp")
            sq = AR.alloc(KC * 512, BF16, "sq")
            rstd = AR.alloc(512, F32, "rstd")
            tf = [AR.alloc(512, F32, "tf%d" % i) for i in range(4)]
            tabc = AR.alloc(512, F32, "tabc")
            tabs = AR.alloc(512, F32, "tabs")
            stage_top = AR.top

            def ntk(k, tl, n):
                return nT.s(k * TP + tl, k * TP + tl + n)

            prenorm(l, "mpre", tiles, T0, TP, nT, sq, rstd)
            yield
            P.mark("L%d mix%d st2" % (l, pi))

            cqn = AR.alloc(3 * TP, BF16, "cqn")
            att_top = AR.top
            latb = AR.alloc(2 * TP, BF16, "latb")
            krs = AR.alloc(DSEQ, BF16, "krs")
            wsl = AR.alloc(2 * KC * 96, BF16, "wslot")
            wsl2 = AR.alloc(KC * KVL, BF16, "wslot2")
            latf = AR.alloc(2 * 512, F32, "latf")
            krf = AR.alloc(512, F32, "krf")
            cqf = AR.at(tf[1].base32, 3 * 512, F32, "cqf")
            r = None
            wUK = AR.at(wsl2.base32, 2 * NH * 64, BF16, "wUK")
            wUV = AR.at(wsl2.base32 + 512, 2 * 512, BF16, "wUV")

            def load_cd():
                wload(wsl.s(0, KC * 96), d_wkr[l, 0])
                wload(wsl.s(KC * 96, 2 * KC * 96), d_wkr[l, 1])
                wload(wUK.all(), d_wuk[l])
                wload(wUV.all(), d_wuv2[l])
            assert wsl2.base32 == wsl.base32 + 768
            if pi == 0:
                wslA = AR.at(Vt.base32 + 128, KC * QL, BF16, "wslA")
                wslB = AR.at(Vt.base32 + 128 + 1536, KC * KVL, BF16, "wslB")
            else:
                wslA = AR.at(wsl.base32, KC * QL, BF16, "wslA")
                wslB = AR.at(wsl.base32, KC * KVL, BF16, "wslB")
            wload(wslA.all(), d_wcq[l])
            if pi == 0:
                wload(wslB.all(), d_wckv[l])
                load_cd()
            for (t0, n) in tiles:
                tl = t0 - T0
                for mc in range(3):
                    ps = bank().s(0, n)
                    P.mm(ps, [(wslA.s(k * QL + mc * 128, k * QL + mc * 128 + 128), ntk(k, tl, n)) for k in range(KC)])
                    P.act(cqf.s(mc * 512, mc * 512 + n), ps, AF.Copy)
                r = rstd.s(0, n)
                rstd_from([cqf.s(mc * 512, mc * 512 + n) for mc in range(3)], n, QL, r, sq)
                for mc in range(3):
                    P.stt(cqn.s(mc * TP + tl, mc * TP + tl + n), cqf.s(mc * 512, mc * 512 + n),
                          vcol(l, "qn", mc), r, ALU.mult, ALU.mult)
            if pi == 1:
                wload(wslB.all(), d_wckv[l])
            for (t0, n) in tiles:
                tl = t0 - T0
                r = rstd.s(0, n)
                for mc in range(2):
                    ps = bank().s(0, n)
                    P.mm(ps, [(wslB.s(k * KVL + mc * 128, k * KVL + mc * 128 + 128), ntk(k, tl, n)) for k in range(KC)])
                    P.act(latf.s(mc * 512, mc * 512 + n), ps, AF.Copy)
                rstd_from([latf.s(mc * 512, mc * 512 + n) for mc in range(2)], n, KVL, r, sq)
                for mc in range(2):
                    P.stt(latf.s(mc * 512, mc * 512 + n), latf.s(mc * 512, mc * 512 + n),
                          vcol(l, "kvn", mc), r, ALU.mult, ALU.mult)
                    P.act(latb.s(mc * TP + tl, mc * TP + tl + n), latf.s(mc * 512, mc * 512 + n), AF.Copy)
                    P.dma("sp", d_lat_out(l, mc, t0, n), latf.s(mc * 512, mc * 512 + n))
            if pi == 1:
                load_cd()
            for (t0, n) in tiles:
                tl = t0 - T0
                samp = t0 >= SEQ
                P.dma("sp", tabc.s(0, n, 64, 96), d_cos[:, t0:t0 + n])
                P.dma("sp", tabs.s(0, n, 64, 96), d_sin[:, t0:t0 + n])
                pa = bank().s(0, n, 0, 96)
                pb = bank().s(0, n, 0, 96)
                P.mm(pa, [(wsl.s(k * 96, k * 96 + 96), ntk(k, tl, n)) for k in range(KC)])
                P.mm(pb, [(wsl.s(KC * 96 + k * 96, KC * 96 + k * 96 + 96), ntk(k, tl, n)) for k in range(KC)])
                t1 = tf[0].s(0, n, 64, 96)
                P.tt(t1, bank_rows(pa, 64, 96), tabc.s(0, n, 64, 96), ALU.mult)
                P.tt(krf.s(0, n, 64, 96), bank_rows(pb, 64, 96), tabs.s(0, n, 64, 96), ALU.mult)
                P.tt(krf.s(0, n, 64, 96), krf.s(0, n, 64, 96), t1, ALU.add)
                P.dma("sp", o_kr[l, :, t0:t0 + n], krf.s(0, n, 64, 96))
                if samp:
                    P.copy(krs.s(0, n, 64, 96), krf.s(0, n, 64, 96), eng="dve")
                else:
                    for h in range(NH):
                        dst = kT.s(h * SEQ + t0, h * SEQ + t0 + n, 64, 96)
                        if h % 2 == 0:
                            P.act(dst, krf.s(0, n, 64, 96), AF.Copy)
                        else:
                            P.copy(dst, krf.s(0, n, 64, 96), eng="dve")
            for (t0, n) in ptiles:
                tl = t0 - T0
                for h in range(NH):
                    ps = bank().s(0, n, 0, 64)
                    P.mm(ps, [(wUK.s((k * NH + h) * 64, (k * NH + h) * 64 + 64), latb.s(k * TP + tl, k * TP + tl + n))
                              for k in range(2)])
                    dst = kT.s(h * SEQ + t0, h * SEQ + t0 + n, 0, 64)
                    if h % 2 == 0:
                        P.copy(dst, ps, eng="dve")
                    else:
                        P.act(dst, ps, AF.Copy)
                for bi in range(n // 128):
                    b = (t0 + bi * 128) // 128
                    ps = bank().all()
                    P.mm(ps, [(latb.s(k * TP + tl + bi * 128, k * TP + tl + bi * 128 + 128), wUV.s(k * 512, k * 512 + 512))
                              for k in range(2)])
                    if bi % 2 == 0:
                        P.act(Vt.s(b * 512, b * 512 + 512), ps, AF.Copy)
                    else:
                        P.copy(Vt.s(b * 512, b * 512 + 512), ps, eng="dve")

            if stop <= 2:
                return
            P.mark("L%d mix%d att" % (l, pi))
            AR.top = att_top
            latb2 = AR.alloc(2 * TP, BF16, "latb")
            krs2 = AR.alloc(DSEQ, BF16, "krs")
            assert latb2.base32 == latb.base32 and krs2.base32 == krs.base32
            wUQ = [AR.alloc(3 * 768, BF16, "wUQ%d" % i) for i in range(2)]
            qT = AR.alloc(NH * 512, BF16, "qT")
            Pt = [AR.at(sq.base32 + i * 256, 512, BF16, "Pt%d" % i) for i in range(4)]
            rden = [tf[2], tf[3]]
            wload(wUQ[0].all(), d_wuq[l, 0])
            wload(wUQ[1].all(), d_wuq[l, 1])

            def load_tabs(t0, n):
                P.dma("sp", tabc.s(0, n, 64, 96), d_cos[:, t0:t0 + n])
                P.dma("sp", tabs.s(0, n, 64, 96), d_sin[:, t0:t0 + n])

            def q_head(t0, n, tl, qdst, qstride, h):
                pa = bank(0, 4).s(0, n, 0, 96)
                pb = bank(0, 4).s(0, n, 0, 96)
                P.mm(pa, [(wUQ[0].s(k * 768 + h * 96, k * 768 + h * 96 + 96), cqn.s(k * TP + tl, k * TP + tl + n))
                          for k in range(3)])
                P.mm(pb, [(wUQ[1].s(k * 768 + h * 96, k * 768 + h * 96 + 96), cqn.s(k * TP + tl, k * TP + tl + n))
                          for k in range(3)])
                P.copy(qdst.s(h * qstride, h * qstride + n, 0, 64), bank_rows(pa, 0, 64), eng="dve")
                t1 = tf[0].s(0, n, 64, 96)
                t2 = tf[1].s(0, n, 64, 96)
                P.tt(t1, bank_rows(pa, 64, 96), tabc.s(0, n, 64, 96), ALU.mult)
                P.tt(t2, bank_rows(pb, 64, 96), tabs.s(0, n, 64, 96), ALU.mult)
                P.tt(qdst.s(h * qstride, h * qstride + n, 64, 96), t1, t2, ALU.add, eng="pool")

            qs = AR.alloc(NH * DSEQ, BF16, "qs") if isB else None
            pcount = {"i": 0}
            for ti, (t0, n) in enumerate(ptiles):
                tl = t0 - T0
                gi = t0 // 512
                if ti == 0:
                    load_tabs(t0, n)
                    for h in range(NH):
                        q_head(t0, n, tl, qT, 512, h)
                if ti + 1 < len(ptiles):
                    nxt = (ptiles[ti + 1][0], ptiles[ti + 1][1], ptiles[ti + 1][0] - T0, qT, 512)
                elif isB:
                    nxt = (stile[0], stile[1], stile[0] - T0, qs, DSEQ)
                else:
                    nxt = None
                if nxt is not None:
                    load_tabs(nxt[0], nxt[1])
                nblk = 4 * gi + 4
                items = [(h, j) for h in range(NH) for j in range(nblk)]
                pts = {}

                def issue_scores(idx):
                    h, j = items[idx]
                    m = j - 4 * gi
                    c0 = 0 if m < 0 else 128 * m
                    ps = bank(0, 4).s(c0, 512)
                    P.mm(ps, [(kT.s(h * SEQ + j * 128, h * SEQ + j * 128 + 128, 0, 96),
                               qT.s(h * 512 + c0, h * 512 + 512, 0, 96))])
                    pt = Pt[pcount["i"] % 4]
                    pcount["i"] += 1
                    P.act(pt.s(c0, 512), ps, AF.Exp, scale=SM_SCALE)
                    if m >= 0:
                        P.memset(pt.s(c0, c0 + 64, 64, 128), 0.0, eng="pool" if gi == 0 else "dve")
                    pts[idx] = (pt, c0)

                def issue_pv(idx):
                    h, j = items[idx]
                    hp, half = h // 2, h % 2
                    acc = banks[4 + (h % 2) * 2]
                    den = banks[5 + (h % 2) * 2]
                    pt, c0 = pts.pop(idx)
                    P.mm(acc.s(c0, 512), [(Vt.s(j * 512 + hp * 128, j * 512 + hp * 128 + 128), pt.s(c0, 512))],
                         start=(j == 0), stop=(j == nblk - 1))
                    P.mm(den.s(c0, 512), [(onesb.all(), pt.s(c0, 512))],
                         start=(j == 0), stop=(j == nblk - 1))
                    if j == nblk - 1:
                        def fin(h=h, hp=hp, half=half, acc=acc, den=den):
                            r0, r1 = half * 64, half * 64 + 64
                            rd = rden[h % 2]
                            recip(rd.s(0, 512, r0, r1), den.s(0, 512, r0, r1))
                            P.tt(aT.s(hp * TP + tl, hp * TP + tl + n, r0, r1), acc.s(0, 512, r0, r1), rd.s(0, 512, r0, r1), ALU.mult)
                        pend.append(fin)
                        if nxt is not None:
                            q_head(nxt[0], nxt[1], nxt[2], nxt[3], nxt[4], h)

                LA = 3
                pend = []
                for idx in range(len(items) + LA):
                    if idx < len(items):
                        issue_scores(idx)
                        if pend:
                            pend.pop(0)()
                    if idx >= LA:
                        issue_pv(idx - LA)
                while pend:
                    pend.pop(0)()

            if stop <= 3:
                return
            if isB:
                t0, n = stile
                tl = t0 - T0
                mark = AR.top
                AR.top = kT.base32
                clatT = AR.alloc(2 * SEQ, BF16, "clatT")
                clat = AR.alloc(16 * 256, BF16, "clat")
                ckr = AR.alloc(SEQ, BF16, "ckr")
                wUKT = AR.alloc(NH * 256, BF16, "wUKT")
                qabs = AR.alloc(2 * 128, BF16, "qabs")
                latn = AR.alloc(256, BF16, "latn")
                olat = AR.alloc(2 * 128, BF16, "olat")
                pts = [AR.alloc(128, BF16, "pts%d" % i) for i in range(4)]
                wUVs = AR.alloc(2 * 512, BF16, "wUVs")
                wload(wUVs.all(), d_wuv2[l])
                assert AR.top <= Vt.base32 + Vt.n32
                wload(clatT.all(), d_clatT[l])
                wload(clat.all(), d_clat[l])
                wload(ckr.s(0, SEQ, 64, 96), d_ckr[l])
                wload(wUKT.s(0, NH * 256, 0, 64), d_wukT[l])
                pq = banks[0]
                for h in range(NH):
                    for kc in range(2):
                        P.mm(pq.s(kc * 128 + h * 16, kc * 128 + h * 16 + 16),
                             [(wUKT.s(h * 256 + kc * 128, h * 256 + kc * 128 + 128, 0, 64), qs.s(h * 16, h * 16 + 16, 0, 64))])
                P.act(qabs.all(), pq.s(0, 256), AF.Copy)
                pl = banks[1]
                for kc in range(2):
                    P.mm(pl.s(kc * 128, kc * 128 + 128, 0, 16),
                         [(latb.s(kc * TP + tl, kc * TP + tl + n), identb.all())])
                P.copy(latn.s(0, 256, 0, 16), pl.s(0, 256, 0, 16), eng="dve")
                ol = [banks[4], banks[5]]
                dn = banks[6]
                for j in range(17):
                    new = j == 16
                    kk = 16 if new else 128
                    ps = banks[2 + j % 2].s(0, 128, 0, kk)
                    if new:
                        pairs = [(latb.s(kc * TP + tl, kc * TP + tl + n), qabs.s(kc * 128, kc * 128 + 128)) for kc in range(2)]
                        pairs.append((krs.s(0, n, 64, 96), qs.s(0, 128, 64, 96)))
                    else:
                        pairs = [(clatT.s(kc * SEQ + j * 128, kc * SEQ + j * 128 + 128), qabs.s(kc * 128, kc * 128 + 128))
                                 for kc in range(2)]
                        pairs.append((ckr.s(j * 128, j * 128 + 128, 64, 96), qs.s(0, 128, 64, 96)))
                    P.mm(ps, pairs)
                    pt = pts[j % 4].s(0, 128, 0, kk)
                    P.act(pt, ps, AF.Exp, scale=SM_SCALE)
                    for kc in range(2):
                        lhs = latn.s(kc * 128, kc * 128 + 128, 0, 16) if new else clat.s(j * 256 + kc * 128, j * 256 + kc * 128 + 128)
                        P.mm(ol[kc].s(0, 128), [(lhs, pt)], start=(j == 0), stop=new)
                    P.mm(dn.s(0, 128), [(onesb.s(0, 128, 0, kk), pt)], start=(j == 0), stop=new)
                for kc in range(2):
                    P.act(olat.s(kc * 128, kc * 128 + 128), ol[kc].s(0, 128), AF.Copy)
                rd = rden[0]
                P.record("dve", (lambda e, o=rd.s(0, 128), i=dn.s(0, 128): e.reciprocal(o.ap, i.ap)),
                         [dn.s(0, 128)], [rd.s(0, 128)])
                for h in range(NH):
                    hp, half = h // 2, h % 2
                    r0, r1 = half * 64, half * 64 + 64
                    ps = banks[h % 2].s(256, 256 + 16)
                    P.mm(ps, [(wUVs.s(kc * 512 + hp * 128, kc * 512 + hp * 128 + 128), olat.s(kc * 128 + h * 16, kc * 128 + h * 16 + 16))
                              for kc in range(2)])
                    P.tt(aT.s(hp * TP + tl, hp * TP + tl + n, r0, r1), bank_rows(ps, r0, r1), rd.s(h * 16, h * 16 + 16, r0, r1), ALU.mult)
                AR.top = mark

            if isB and early_f2 is not None:
                early_f2(sq, rstd)
            if stop <= 4:
                return
            P.mark("L%d mix%d brB" % (l, pi))
            AR.top = stage_top
            branch_b(l, pi, T0, TP, tiles, ptiles, stile, nT, ybT, carryb, sq, rstd, tf)
            if stop <= 5:
                return
            P.mark("L%d mix%d brC" % (l, pi))
            AR.top = stage_top
            branch_c(l, pi, T0, TP, tiles, ptiles, stile, nT, ycT, sq, rstd, tf)
            if stop <= 6:
                return
            P.mark("L%d mix%d brD" % (l, pi))
            AR.top = stage_top
            branch_d(l, pi, T0, TP, tiles, ptiles, stile, nT, ydT, carryd, tf)

            if stop <= 7:
                return
            P.mark("L%d mix%d merge" % (l, pi))
            AR.top = stage_top
            mrg = AR.alloc(KC * TP, BF16, "merged")
            wG = [AR.alloc(KC * 512 + 10 * 128, BF16, "wG0"),
                  AR.at(sq.base32, KC * 512 + 10 * 128, BF16, "wG1")]
            assert sq.base32 + 2688 <= tf[1].base32
            wO = [AR.at(wG[0].base32 + i * 512, KC * 128, BF16, "wO%d" % i) for i in range(5)]
            gsb = [tabc, tabs]
            brs = ((aT, 4), (ybT, 2), (ycT, 2), (ydT, 2))
            for m in range(8):
                w = wG[m % 2]
                wload(w.all(), d_wgm[l, m])
                for (t0, n) in tiles:
                    tl = t0 - T0
                    kb0 = 0
                    for b, (src, nk) in enumerate(brs):
                        pg = bank().s(0, n)
                        pp = bank().s(0, n)
                        P.mm(pg, [(w.s((k * 4 + b) * 128, (k * 4 + b) * 128 + 128), ntk(k, tl, n)) for k in range(KC)])
                        P.mm(pp, [(w.s(KC * 512 + (kb0 + kk) * 128, KC * 512 + (kb0 + kk) * 128 + 128),
                                   src.s(kk * TP + tl, kk * TP + tl + n)) for kk in range(nk)])
                        kb0 += nk
                        g = gsb[b % 2].s(0, n)
                        P.act(g, pg, AF.Sigmoid)
                        if b == 0:
                            P.tt(tf[2].s(0, n), pp, g, ALU.mult)
                        else:
                            P.tt(tf[3].s(0, n), pp, g, ALU.mult)
                            dst = tf[2].s(0, n) if b < 3 else mrg.s(m * TP + tl, m * TP + tl + n)
                            P.tt(dst, tf[2].s(0, n), tf[3].s(0, n), ALU.add)
            P.mark("L%d mix%d wo" % (l, pi))
            for m in range(8):
                w = wO[m % 5]
                wload(w.all(), d_wo[l, m])
                for (t0, n) in tiles:
                    tl = t0 - T0
                    py = bank().s(0, n)
                    P.mm(py, [(w.s(k * 128, k * 128 + 128), mrg.s(k * TP + tl, k * TP + tl + n)) for k in range(KC)])
                    P.act(ytmp.s(m * TP + tl, m * TP + tl + n), py, AF.Copy)
            postnorm_residual(l, "mpost", tiles, T0, TP, ytmp, sq, rstd, False)

        def bank_rows(ref, p0, p1):
            return Ref(ref.ap[p0:p1] if p0 == 0 and False else ref.ap[p0:p1, :], ref.atoms)

        def d_lat_out(l, mc, t0, n):
            return o_lat[l, :, mc * NT + t0: mc * NT + t0 + n]


        def reduce_sum(out, in_):
            P.record("dve", (lambda e, o=out, i=in_: e.tensor_reduce(o.ap, i.ap, mybir.AxisListType.X, ALU.add)),
                     [in_], [out])

        def branch_b(l, pi, T0, TP, tiles, ptiles, stile, nT, ybT, carryb, sq, rstd, tf):
            PTN = sum(n for (_, n) in ptiles)
            W = PADB + PTN
            WS = PADB + DSEQ
            wgl = AR.alloc(KC * 512, BF16, "wgl")
            wload(wgl.all(), d_wglu[l])
            diag = AR.alloc(CBW * 2 * 128, BF16, "diagb")
            xbp = AR.alloc(2 * W, BF16, "xbp")
            xbs = AR.alloc(2 * WS, BF16, "xbs")
            cbfs = [rstd, tf[3]]
            mean = tf[0]
            rs = tf[1]
            cbst = AR.alloc(2 * 2 * PADB, F32, "cbst")
            for j in range(CBW):
                for c in range(2):
                    P.ts(diag.s((j * 2 + c) * 128, (j * 2 + c) * 128 + 128), identf.all(), vcol(l, "cbw", j * 2 + c), ALU.mult)
            for c in range(2):
                if pi == 0:
                    P.memset(xbp.s(c * W, c * W + PADB), 0.0, eng="dve")
                else:
                    P.copy(xbp.s(c * W, c * W + PADB), carryb.s(c * PADB, (c + 1) * PADB), eng="dve")
                    wload(xbs.s(c * WS, c * WS + PADB), d_scb[l][:, c * PADB:(c + 1) * PADB])
                    P.dma("sp", cbst.s(c * 60 + 30, c * 60 + 44), d_scb[l][:, c * PADB + 16:c * PADB + 30])
            for (t0, n) in tiles:
                tl = t0 - T0
                samp = t0 >= SEQ
                for c in range(2):
                    pa = bank().s(0, n)
                    pg = bank().s(0, n)
                    P.mm(pa, [(wgl.s(k * 512 + c * 128, k * 512 + c * 128 + 128), nT.s(k * TP + tl, k * TP + tl + n)) for k in range(KC)])
                    P.mm(pg, [(wgl.s(k * 512 + 256 + c * 128, k * 512 + 256 + c * 128 + 128), nT.s(k * TP + tl, k * TP + tl + n)) for k in range(KC)])
                    sg = tf[c].s(0, n)
                    P.act(sg, pg, AF.Sigmoid)
                    if samp:
                        dst = xbs.s(c * WS + PADB, c * WS + PADB + n)
                        P.tt(cbst.s(c * 60 + 44, c * 60 + 60), pa, sg, ALU.mult)
                    else:
                        dst = xbp.s(c * W + PADB + tl, c * W + PADB + tl + n)
                        if t0 + n == SEQ:
                            P.tt(cbst.s(c * 60, c * 60 + 30), Ref(pa.ap[:, n - 30:n], pa.atoms), tf[c].s(n - 30, n), ALU.mult)
                    P.tt(dst, pa, sg, ALU.mult)
                for c in range(2):
                    ps = bank().s(0, n)
                    if samp:
                        P.mm(ps, [(diag.s((j * 2 + c) * 128, (j * 2 + c) * 128 + 128), xbs.s(c * WS + j, c * WS + j + n)) for j in range(CBW)])
                    else:
                        P.mm(ps, [(diag.s((j * 2 + c) * 128, (j * 2 + c) * 128 + 128), xbp.s(c * W + tl + j, c * W + tl + j + n)) for j in range(CBW)])
                    P.act(cbfs[c].s(0, n), ps, AF.Identity, bias=vcol(l, "cbb", c))
                    P.act(sq.s(c * 512, c * 512 + n), cbfs[c].s(0, n), AF.Copy)
                    P.act(sq.s((2 + c) * 512, (2 + c) * 512 + n), cbfs[c].s(0, n), AF.Square)
                pm = bank().s(0, n)
                pq = bank().s(0, n)
                P.mm(pm, [(onesb.all(), sq.s(c * 512, c * 512 + n)) for c in range(2)])
                P.mm(pq, [(onesb.all(), sq.s((2 + c) * 512, (2 + c) * 512 + n)) for c in range(2)])
                mn = mean.s(0, n)
                r = rs.s(0, n)
                m2 = tf[2].s(0, n)
                P.ts(mn, pm, 1.0 / 256, ALU.mult)
                P.tt(m2, mn, mn, ALU.mult)
                P.stt(r, pq, 1.0 / 256, m2, ALU.mult, ALU.subtract)
                P.act(r, r, AF.Ln, bias=epsb[1.0])
                P.act(r, r, AF.Exp, scale=-0.5)
                for c in range(2):
                    x = cbfs[c].s(0, n)
                    P.tt(x, x, mn, ALU.subtract)
                    P.tt(x, x, r, ALU.mult)
                    P.act(ybT.s(c * TP + tl, c * TP + tl + n), x, AF.Silu, scale=vcol(l, "cblg", c), bias=vcol(l, "cblb", c))
            if pi == 0:
                for c in range(2):
                    P.copy(carryb.s(c * PADB, (c + 1) * PADB), xbp.s(c * W + PTN, c * W + PTN + PADB), eng="dve")
            else:
                P.dma("sp", o_cb[l], cbst.all())

        def branch_c(l, pi, T0, TP, tiles, ptiles, stile, nT, ycT, sq, rstd, tf):
            wuv = AR.alloc(KC * 512, BF16, "wuv")
            wload(wuv.all(), d_wuv[l])
            wsT = AR.alloc(512, BF16, "wsT")
            wload(wsT.all(), d_wsT[l])
            bsr = AR.alloc(512, BF16, "bsr")
            wload(bsr.s(0, 512, 0, 1), d_bs[l])
            for g in range(4):
                P.memset(wsT.s(g * 128, g * 128 + 64, 64, 128), 0.0, eng="dve")
            uT = AR.alloc(2 * TP, BF16, "uT")
            vnT = AR.alloc(2 * 512, BF16, "vnT")
            vtm = [AR.alloc(256, BF16, "vtm%d" % i) for i in range(2)]
            gvst = AR.alloc(2 * DSEQ, F32, "gvst")
            vfs = [rstd, tf[3]]
            mean, rs = tf[0], tf[1]
            bcount = 0
            for (t0, n) in tiles:
                tl = t0 - T0
                samp = t0 >= SEQ
                for c in range(2):
                    pu = bank().s(0, n)
                    P.mm(pu, [(wuv.s(k * 512 + c * 128, k * 512 + c * 128 + 128), nT.s(k * TP + tl, k * TP + tl + n)) for k in range(KC)])
                    P.act(uT.s(c * TP + tl, c * TP + tl + n), pu, AF.Copy)
                for c in range(2):
                    pv = bank().s(0, n)
                    P.mm(pv, [(wuv.s(k * 512 + 256 + c * 128, k * 512 + 256 + c * 128 + 128), nT.s(k * TP + tl, k * TP + tl + n)) for k in range(KC)])
                    P.act(vfs[c].s(0, n), pv, AF.Copy)
                    P.act(sq.s(c * 512, c * 512 + n), vfs[c].s(0, n), AF.Copy)
                    P.act(sq.s((2 + c) * 512, (2 + c) * 512 + n), vfs[c].s(0, n), AF.Square)
                pm = bank().s(0, n)
                pq = bank().s(0, n)
                P.mm(pm, [(onesb.all(), sq.s(c * 512, c * 512 + n)) for c in range(2)])
                P.mm(pq, [(onesb.all(), sq.s((2 + c) * 512, (2 + c) * 512 + n)) for c in range(2)])
                mn = mean.s(0, n)
                r = rs.s(0, n)
                m2 = tf[2].s(0, n)
                P.ts(mn, pm, 1.0 / 256, ALU.mult)
                P.tt(m2, mn, mn, ALU.mult)
                P.stt(r, pq, 1.0 / 256, m2, ALU.mult, ALU.subtract)
                P.act(r, r, AF.Ln, bias=epsb[1.0])
                P.act(r, r, AF.Exp, scale=-0.5)
                for c in range(2):
                    x = vfs[c].s(0, n)
                    P.tt(x, x, mn, ALU.subtract)
                    P.tt(x, x, r, ALU.mult)
                    if samp:
                        P.act(gvst.s(c * DSEQ, (c + 1) * DSEQ), x, AF.Identity, scale=vcol(l, "vng", c), bias=vcol(l, "vnb", c))
                    P.act(vnT.s(c * 512, c * 512 + n), x, AF.Identity, scale=vcol(l, "vng", c), bias=vcol(l, "vnb", c))
                if samp:
                    P.dma("sp", o_gv[l], gvst.all())
                bsz = min(n, 128)
                for bi in range(max(1, n // 128)):
                    c0 = tl + bi * 128
                    v16 = vtm[bcount % 2]
                    bcount += 1
                    pt = bank().s(0, 256, 0, bsz)
                    for ch in range(2):
                        P.mm(Ref(pt.ap[:, ch * 128:(ch + 1) * 128], pt.atoms),
                             [(vnT.s(ch * 512 + bi * 128, ch * 512 + bi * 128 + bsz), identb.all())])
                    P.copy(v16.s(0, 256, 0, bsz), pt, eng="dve")
                    for ch in range(2):
                        for gg in range(2):
                            g = 2 * ch + gg
                            psg = bank().s(0, bsz)
                            P.mm(psg, [(v16.s(ch * 128, ch * 128 + 128, 0, bsz), wsT.s(g * 128, g * 128 + bsz, 0, bsz)),
                                       (onesb.s(0, 128, 0, 1), bsr.s(g * 128, g * 128 + bsz, 0, 1))])
                            r0, r1 = gg * 64, gg * 64 + 64
                            P.tt(ycT.s(ch * TP + c0, ch * TP + c0 + bsz, r0, r1), Ref(psg.ap[r0:r1, :], psg.atoms),
                                 uT.s(ch * TP + c0, ch * TP + c0 + bsz, r0, r1), ALU.mult)

        def branch_d(l, pi, T0, TP, tiles, ptiles, stile, nT, ydT, carryd, tf):
            PTN = sum(n for (_, n) in ptiles)
            W = PADD + PTN
            WS = PADD + DSEQ
            wdp = AR.alloc(KC * 768, BF16, "wdp")
            wload(wdp.all(), d_wdp[l])
            diag = AR.alloc(CDW * 2 * 128, BF16, "diagd")
            xdp = AR.alloc(2 * W, BF16, "xdp")
            xds = AR.alloc(2 * WS, BF16, "xds")
            bgT = AR.alloc(2 * TP, BF16, "bgT")
            cdst = AR.alloc(2 * 2 * PADD, F32, "cdst")
            for j in range(CDW):
                for c in range(2):
                    P.ts(diag.s((j * 2 + c) * 128, (j * 2 + c) * 128 + 128), identf.all(), vcol(l, "cdw", j * 2 + c), ALU.mult)
            for c in range(2):
                if pi == 0:
                    P.memset(xdp.s(c * W, c * W + PADD), 0.0, eng="dve")
                else:
                    P.copy(xdp.s(c * W, c * W + PADD), carryd.s(c * PADD, (c + 1) * PADD), eng="dve")
                    wload(xds.s(c * WS, c * WS + PADD), d_scd[l][:, c * PADD:(c + 1) * PADD])
            for (t0, n) in tiles:
                tl = t0 - T0
                samp = t0 >= SEQ
                for c in range(2):
                    pb = bank().s(0, n)
                    pc = bank().s(0, n)
                    ph = bank().s(0, n)
                    for ps_, off in ((pb, 0), (pc, 256), (ph, 512)):
                        P.mm(ps_, [(wdp.s(k * 768 + off + c * 128, k * 768 + off + c * 128 + 128), nT.s(k * TP + tl, k * TP + tl + n)) for k in range(KC)])
                    P.act(bgT.s(c * TP + tl, c * TP + tl + n), pb, AF.Copy)
                    cg = tf[c].s(0, n)
                    P.act(cg, pc, AF.Copy)
                    if samp:
                        dst = xds.s(c * WS + PADD, c * WS + PADD + n)
                        P.tt(cdst.s(c * 4 + 2, c * 4 + 4), Ref(ph.ap[:, n - 2:n], ph.atoms), tf[c].s(n - 2, n), ALU.mult)
                    else:
                        dst = xdp.s(c * W + PADD + tl, c * W + PADD + tl + n)
                        if t0 + n == SEQ:
                            P.tt(cdst.s(c * 4, c * 4 + 2), Ref(ph.ap[:, n - 2:n], ph.atoms), tf[c].s(n - 2, n), ALU.mult)
                    P.tt(dst, ph, cg, ALU.mult)
                for c in range(2):
                    ps = bank().s(0, n)
                    if samp:
                        P.mm(ps, [(diag.s((j * 2 + c) * 128, (j * 2 + c) * 128 + 128), xds.s(c * WS + j, c * WS + j + n)) for j in range(CDW)])
                    else:
                        P.mm(ps, [(diag.s((j * 2 + c) * 128, (j * 2 + c) * 128 + 128), xdp.s(c * W + tl + j, c * W + tl + j + n)) for j in range(CDW)])
                    P.tt(ydT.s(c * TP + tl, c * TP + tl + n), ps, bgT.s(c * TP + tl, c * TP + tl + n), ALU.mult)
            if pi == 0:
                for c in range(2):
                    P.copy(carryd.s(c * PADD, (c + 1) * PADD), xdp.s(c * W + PTN, c * W + PTN + PADD), eng="dve")
            else:
                P.dma("sp", o_cd[l], cdst.all())

        for l in range(n_layers):
            early = None
            if do_mixer and do_ffn is True:
                def early(sq_, rstd_, l=l):
                    (TA0, TPA, tilesA) = PASSES[0]
                    prenorm(l, "f2pre", tilesA, TA0, TPA, Buf(P, "A", 0, KC * TPA, BF16, "nTA_early"), sq_, rstd_)
            gen = mixer_layer(l, early) if do_mixer else None
            if do_ffn:
                ffn_pair(l, 0, next_pre=(lambda: next(gen)) if gen is not None else None)
            elif gen is not None:
                next(gen)
            if gen is not None:
                for _ in gen:
                    pass
            if do_ffn is True:
                ffn_pair(l, 1, skip_preA=(early is not None))
        for k in range(KC):
            P.dma("sp", o_y[:, k * NT:(k + 1) * NT], xT.s(k * NT, (k + 1) * NT))
        P.barrier_on("sp", [xT.all(), Buf(P, "A", 0, ASIZE, F32).all()])
        P.mark("end")
        P.emit()
        import json as _json
        if os.environ.get("MARKS"):
            _json.dump(P.marks, open(os.environ["MARKS"], "w"))
        print("arena peak", AR.peak, "of", ASIZE, {e: len(P.ops[e]) for e in ENGINES}, "dma sems", len(P.dma_sems))
    return nc


def _slab(W, ms):
    K, M = W.shape
    return np.ascontiguousarray(
        W.reshape(K // 128, 128, M // ms, ms).transpose(2, 1, 0, 3)).reshape(M // ms, 128, (K // 128) * ms)


def _rope_tables():
    half = RD // 2
    inv = np.exp(-math.log(10000.0) * np.arange(half, dtype=np.float32) / half).astype(np.float32)
    pos = np.arange(NT, dtype=np.float32)
    ang = (pos[:, None] * inv[None, :]).astype(np.float32)
    c, s = np.cos(ang).astype(np.float32), np.sin(ang).astype(np.float32)
    cosT = np.concatenate([c, c], axis=1).T
    sinS = np.concatenate([-s, s], axis=1).T
    return np.ascontiguousarray(cosT), np.ascontiguousarray(sinS)


def prep_shared(I):
    f = lambda a: np.asarray(a, dtype=np.float32)
    S = {}
    wgu = np.empty((L, 2, FC, 128, KC * 256), np.float32)
    wdn = np.empty((L, 2, 8, 128, FC * 128), np.float32)
    for l in range(L):
        for fi, (gu, dn) in enumerate((("ffn1_w_gu", "ffn1_w_down"), ("ffn2_w_gu", "ffn2_w_down"))):
            W = f(I[gu][l]).reshape(KC, 128, 2 * DFF)
            G = W[:, :, :DFF].reshape(KC, 128, FC, 128)
            U = W[:, :, DFF:].reshape(KC, 128, FC, 128)
            wgu[l, fi] = np.stack([G, U], axis=3).transpose(2, 1, 0, 3, 4).reshape(FC, 128, KC * 256)
            wdn[l, fi] = _slab(f(I[dn][l]), 128)
    S["wgu"], S["wdn"] = wgu, wdn
    win = f(I["w_in"])
    S["wcq"] = np.stack([_slab(win[l][:, 0:384], 384)[0] for l in range(L)])
    S["wckv"] = np.stack([_slab(win[l][:, 384:640], 256)[0] for l in range(L)])
    wkr = np.zeros((L, 2, 1024, 96), np.float32)
    wkr[:, 0, :, 64:96] = win[:, :, 640:672]
    wkr[:, 1, :, 64:80] = win[:, :, 656:672]
    wkr[:, 1, :, 80:96] = win[:, :, 640:656]
    S["wkr"] = np.stack([np.stack([_slab(wkr[l, i], 96)[0] for i in range(2)]) for l in range(L)])
    S["wglu"] = np.stack([_slab(win[l][:, 672:1184], 512)[0] for l in range(L)])
    S["wuv"] = np.stack([_slab(win[l][:, 1184:1696], 512)[0] for l in range(L)])
    S["wdp"] = np.stack([_slab(win[l][:, 1696:2464], 768)[0] for l in range(L)])
    wgm = np.empty((L, 8, 128, KC * 512 + 10 * 128), np.float32)
    for l in range(L):
        Wg = win[l][:, 2464:].reshape(KC, 128, 4, 8, 128).transpose(3, 1, 0, 2, 4).reshape(8, 128, KC * 512)
        Wbr = np.concatenate([f(I["w_br_a"][l]), f(I["w_br_b"][l]), f(I["w_br_c"][l]), f(I["w_br_d"][l])], axis=0)
        Wb = Wbr.reshape(10, 128, 8, 128).transpose(2, 1, 0, 3).reshape(8, 128, 10 * 128)
        wgm[l] = np.concatenate([Wg, Wb], axis=2)
    S["wgm"] = wgm
    S["wo"] = np.stack([_slab(f(I["w_o"][l]), 128) for l in range(L)])
    wuq = f(I["w_uq"])
    wuq_sw = wuq.reshape(L, QL, NH, 96).copy()
    wuq_sw[..., 64:80] = wuq.reshape(L, QL, NH, 96)[..., 80:96]
    wuq_sw[..., 80:96] = wuq.reshape(L, QL, NH, 96)[..., 64:80]
    wuq_sw = wuq_sw.reshape(L, QL, NH * 96)
    S["wuq"] = np.stack([np.stack([_slab(wuq[l], 768)[0], _slab(wuq_sw[l], 768)[0]]) for l in range(L)])
    wukv = f(I["w_ukv"]).reshape(L, 2, 128, NH, 128)
    S["wuk"] = np.ascontiguousarray(wukv[..., :64].transpose(0, 2, 1, 3, 4)).reshape(L, 128, 2 * NH * 64)
    S["wuv2"] = np.ascontiguousarray(wukv[..., 64:].transpose(0, 2, 1, 3, 4)).reshape(L, 128, 2 * 512)
    w3 = f(I["w_ukv"]).reshape(L, KVL, NH, 128)[..., :64]
    S["wukT"] = np.ascontiguousarray(w3.transpose(0, 3, 2, 1)).reshape(L, 64, NH * 256)
    S["wsT"] = np.ascontiguousarray(f(I["gmlp_w_s"]).transpose(0, 3, 1, 2)).reshape(L, 128, 512)
    S["bsrow"] = np.ascontiguousarray(f(I["gmlp_b_s"])).reshape(L, 1, 512)
    gvb = np.concatenate([f(I["gmlp_vn_g"]), f(I["gmlp_vn_b"])], axis=1)
    S["gvb"] = np.ascontiguousarray(np.broadcast_to(gvb[:, None, :], (L, 128, 512)))
    vecs = np.zeros((128, L * NV), np.float32)

    def put(l, name, v, w):
        v = f(v).reshape(w, 128).T
        vecs[:, l * NV + VOFF[name]: l * NV + VOFF[name] + w] = v
    for l in range(L):
        put(l, "f1pre", I["ffn1_norm_pre"][l], 8)
        put(l, "f1post", I["ffn1_norm_post"][l], 8)
        put(l, "mpre", I["mix_norm_pre"][l], 8)
        put(l, "mpost", I["mix_norm_post"][l], 8)
        put(l, "f2pre", I["ffn2_norm_pre"][l], 8)
        put(l, "f2post", I["ffn2_norm_post"][l], 8)
        put(l, "qn", I["q_norm"][l], 3)
        put(l, "kvn", I["kv_norm"][l], 2)
        put(l, "cbb", I["conv_b_bias"][l], 2)
        put(l, "cblg", I["conv_b_ln_g"][l], 2)
        put(l, "cblb", I["conv_b_ln_b"][l], 2)
        put(l, "cbw", I["conv_b_w"][l], CBW * 2)
        put(l, "cdw", I["conv_d_w"][l], CDW * 2)
        put(l, "vng", I["gmlp_vn_g"][l], 2)
        put(l, "vnb", I["gmlp_vn_b"][l], 2)
    S["vecs"] = vecs
    S["ident"] = np.eye(128, dtype=np.float32)
    S["cosT"], S["sinS"] = _rope_tables()
    return S


def prep_core(I, c):
    f = lambda a: np.asarray(a, dtype=np.float32)
    C = {}
    X = np.concatenate([f(I["x_prompt"][c]), f(I["x_sample"][c])], axis=0)
    C["xT"] = np.ascontiguousarray(X.T.reshape(KC, 128, NT).transpose(1, 0, 2)).reshape(128, KC * NT)
    cl = f(I["cache_kv_latent"][:, c])
    C["clatT"] = np.ascontiguousarray(cl.transpose(0, 2, 1).reshape(L, 2, 128, SEQ).transpose(0, 2, 1, 3)).reshape(L, 128, 2 * SEQ)
    C["clat"] = np.ascontiguousarray(cl.reshape(L, 16, 128, 256).transpose(0, 2, 1, 3)).reshape(L, 128, 16 * 256)
    C["ckrT"] = np.ascontiguousarray(f(I["cache_k_rope"][:, c]).transpose(0, 2, 1))
    sb = f(I["state_conv_b"][:, c])
    C["scbT"] = np.ascontiguousarray(sb.transpose(0, 2, 1).reshape(L, 2, 128, PADB).transpose(0, 2, 1, 3)).reshape(L, 128, 2 * PADB)
    sd = f(I["state_conv_d"][:, c])
    C["scdT"] = np.ascontiguousarray(sd.transpose(0, 2, 1).reshape(L, 2, 128, PADD).transpose(0, 2, 1, 3)).reshape(L, 128, 2 * PADD)
    return C


def assemble(results):
    n = len(results)
    yp = np.empty((n, SEQ, D), np.float32)
    ys = np.empty((n, DSEQ, D), np.float32)
    latp = np.empty((L, n, SEQ, KVL), np.float32)
    lats = np.empty((L, n, DSEQ, KVL), np.float32)
    krp = np.empty((L, n, SEQ, RD), np.float32)
    krs = np.empty((L, n, DSEQ, RD), np.float32)
    cbp = np.empty((L, n, PADB, 256), np.float32)
    cbs = np.empty((L, n, PADB, 256), np.float32)
    cdp = np.empty((L, n, PADD, 256), np.float32)
    cds = np.empty((L, n, PADD, 256), np.float32)
    gv = np.empty((L, n, DSEQ, 256), np.float32)
    for c, r in enumerate(results):
        y = np.asarray(r["yT"]).reshape(128, KC, NT).transpose(2, 1, 0).reshape(NT, D)
        yp[c], ys[c] = y[:SEQ], y[SEQ:]
        lat = np.asarray(r["latT"]).reshape(L, 128, 2, NT).transpose(0, 3, 2, 1).reshape(L, NT, KVL)
        latp[:, c], lats[:, c] = lat[:, :SEQ], lat[:, SEQ:]
        kr = np.asarray(r["krT"]).transpose(0, 2, 1)
        krp[:, c], krs[:, c] = kr[:, :SEQ], kr[:, SEQ:]
        cb = np.asarray(r["cbT"]).reshape(L, 128, 2, 2, PADB).transpose(0, 3, 4, 2, 1).reshape(L, 2, PADB, 256)
        cbp[:, c], cbs[:, c] = cb[:, 0], cb[:, 1]
        cd = np.asarray(r["cdT"]).reshape(L, 128, 2, 2, PADD).transpose(0, 3, 4, 2, 1).reshape(L, 2, PADD, 256)
        cdp[:, c], cds[:, c] = cd[:, 0], cd[:, 1]
        gv[:, c] = np.asarray(r["gvT"]).reshape(L, 128, 2, DSEQ).transpose(0, 3, 2, 1).reshape(L, DSEQ, 256)
    return (yp, ys, latp, krp, cbp, cdp, lats, krs, cbs, gv, cds)


_NC_CACHE = {}


def kernel(**inputs):
    if "nc" not in _NC_CACHE:
        _NC_CACHE["nc"] = build_program()
    nc = _NC_CACHE["nc"]
    S = prep_shared(inputs)
    in_maps = []
    for c in range(8):
        m = dict(S)
        m.update(prep_core(inputs, c))
        in_maps.append(m)
    res = run_bass_kernel_spmd(nc, in_maps, core_ids=list(range(8)))
    return assemble(res.results)
```

```python
import contextlib
import math
import numpy as np
import concourse.bass as bass
import concourse.mybir as mybir
from concourse.bass_utils import run_bass_kernel_spmd

F32 = mybir.dt.float32
BF16 = mybir.dt.bfloat16
AF = mybir.ActivationFunctionType
ALU = mybir.AluOpType

ATOM = 8


class Ref:
    __slots__ = ("ap", "atoms")

    def __init__(self, ap, atoms):
        self.ap = ap
        self.atoms = atoms


class Buf:
    def __init__(self, prog, space, base32, ncols, dt, name=""):
        self.prog = prog
        self.space = space
        self.base32 = base32
        self.ncols = ncols
        self.dt = dt
        self.name = name
        self.esz = 2 if dt == BF16 else 4
        n32 = (ncols * self.esz + 3) // 4
        self.n32 = n32
        t = prog.tensors[space]
        ap = t[:, base32:base32 + n32]
        if dt != F32:
            ap = ap.bitcast(dt)
        self.full = ap

    def s(self, c0, c1, p0=0, p1=128, step=None):
        assert 0 <= c0 < c1 <= self.ncols, (self.name, c0, c1, self.ncols)
        if step is None:
            ap = self.full[p0:p1, c0:c1]
        else:
            ap = self.full[p0:p1, c0:c1:step]
        asz = (512 if self.space == "PS" else ATOM) * 4
        a0 = (self.base32 * 4 + c0 * self.esz) // asz
        a1 = (self.base32 * 4 + c1 * self.esz - 1) // asz
        return Ref(ap, [(self.space, a) for a in range(a0, a1 + 1)])

    def all(self):
        return self.s(0, self.ncols)


class Op:
    __slots__ = ("eng", "fn", "waits", "need_sig", "sigval", "dma_sem", "dma_val", "eidx")

    def __init__(self, eng, fn):
        self.eng = eng
        self.fn = fn
        self.waits = []
        self.need_sig = False
        self.sigval = None
        self.dma_sem = None
        self.dma_val = None
        self.eidx = None


ENGINES = ("pe", "act", "dve", "pool", "sp")


class Prog:
    def __init__(self, nc):
        self.nc = nc
        self.tensors = {}
        self.ops = {e: [] for e in ENGINES}
        self.last_w = {}
        self.readers = {}
        self.seen = {e: {} for e in ENGINES}
        self.dma_sems = {}
        self.n_dma_sems = 0
        self.rr = 0
        self.n_mm = 0
        self.marks = []

    def _add_wait(self, op, dep, raw):
        if dep is op:
            return
        e = op.eng
        if dep.dma_sem is not None:
            key = ("dma", dep.dma_sem)
            if self.seen[e].get(key, 0) >= dep.dma_val:
                return
            self.seen[e][key] = dep.dma_val
            op.waits.append(("dma", dep.dma_sem, dep.dma_val))
            return
        if dep.eng == e and op.dma_sem is None:
            if e == "pe":
                return
        key = ("eng", dep.eng)
        if self.seen[e].get(key, -1) >= dep.eidx:
            return
        self.seen[e][key] = dep.eidx
        dep.need_sig = True
        op.waits.append(("op", dep))

    def record(self, eng, fn, reads=(), writes=(), dma_key=None):
        op = Op(eng, fn)
        op.eidx = len(self.ops[eng])
        if dma_key is not None:
            v = self.dma_sems.get(dma_key, 0) + 16
            self.dma_sems[dma_key] = v
            op.dma_sem = dma_key
            op.dma_val = v
        for r in reads:
            for a in r.atoms:
                w = self.last_w.get(a)
                if w is not None:
                    self._add_wait(op, w, True)
        for r in writes:
            for a in r.atoms:
                w = self.last_w.get(a)
                if w is not None:
                    self._add_wait(op, w, False)
                for rd in self.readers.get(a, ()):
                    self._add_wait(op, rd, False)
        for r in reads:
            for a in r.atoms:
                lst = self.readers.setdefault(a, [])
                if not lst or lst[-1] is not op:
                    lst.append(op)
        for r in writes:
            for a in r.atoms:
                self.last_w[a] = op
                self.readers[a] = []
        self.ops[eng].append(op)
        return op

    def mm(self, out, pairs, start=True, stop=True):
        reads = []
        for l, r in pairs:
            reads.append(l)
            reads.append(r)
        n = len(pairs)
        self.n_mm += n

        def fn(eng, out=out, pairs=pairs):
            ins = None
            for i, (l, r) in enumerate(pairs):
                ins = eng.matmul(out.ap, l.ap, r.ap, start=(start and i == 0),
                                 stop=(stop and i == n - 1))
            return ins
        return self.record("pe", fn, reads, [out])

    def act(self, out, in_, func, scale=1.0, bias=0.0, accum=None, eng="act"):
        reads = [in_]
        sc = scale.ap if isinstance(scale, Ref) else scale
        bi = bias.ap if isinstance(bias, Ref) else bias
        if isinstance(scale, Ref):
            reads.append(scale)
        if isinstance(bias, Ref):
            reads.append(bias)
        writes = [out]
        if accum is not None:
            writes.append(accum)

        def fn(eng, out=out, in_=in_):
            kw = {}
            if accum is not None:
                kw["accum_out"] = accum.ap
            return eng.activation(out.ap, in_.ap, func, bias=bi, scale=sc, **kw)
        return self.record("act", fn, reads, writes)

    def tt(self, out, a, b, op, eng="dve"):
        def fn(e, out=out, a=a, b=b):
            return e.tensor_tensor(out.ap, a.ap, b.ap, op)
        return self.record(eng, fn, [a, b], [out])

    def stt(self, out, in0, scalar, in1, op0, op1, eng="dve"):
        reads = [in0, in1]
        sc = scalar.ap if isinstance(scalar, Ref) else scalar
        if isinstance(scalar, Ref):
            reads.append(scalar)

        def fn(e, out=out, in0=in0, in1=in1):
            return e.scalar_tensor_tensor(out.ap, in0.ap, sc, in1.ap, op0, op1)
        return self.record(eng, fn, reads, [out])

    def ts(self, out, in0, s1, op0, s2=None, op1=None, eng="dve"):
        reads = [in0]
        a1 = s1.ap if isinstance(s1, Ref) else s1
        a2 = s2.ap if isinstance(s2, Ref) else s2
        if isinstance(s1, Ref):
            reads.append(s1)
        if isinstance(s2, Ref):
            reads.append(s2)

        def fn(e, out=out, in0=in0):
            if op1 is None:
                return e.tensor_scalar(out.ap, in0.ap, a1, None, op0)
            return e.tensor_scalar(out.ap, in0.ap, a1, a2, op0, op1)
        return self.record(eng, fn, reads, [out])

    def copy(self, out, in_, eng="dve"):
        def fn(e, out=out, in_=in_):
            return e.tensor_copy(out.ap, in_.ap)
        return self.record(eng, fn, [in_], [out])

    def memset(self, out, val, eng="pool"):
        def fn(e, out=out):
            return e.memset(out.ap, val)
        return self.record(eng, fn, [], [out])

    def dma(self, queue, out, in_, key=None):
        reads = [in_] if isinstance(in_, Ref) else []
        writes = [out] if isinstance(out, Ref) else []
        oap = out.ap if isinstance(out, Ref) else out
        iap = in_.ap if isinstance(in_, Ref) else in_
        if key is None:
            key = ("auto", self.rr % 24)
            self.rr += 1

        def fn(e):
            return e.dma_start(out=oap, in_=iap)
        prev = self.dma_sems.get(key, 0)
        op = self.record(queue, fn, reads, writes, dma_key=key)
        if prev > 0 and self.seen[queue].get(("dma", key), 0) < prev:
            self.seen[queue][("dma", key)] = prev
            op.waits.append(("dma", key, prev))
        return op

    def mark(self, label):
        self.marks.append((label, self.n_mm))

    def barrier_on(self, eng, refs):
        return self.record(eng, None, refs, refs, dma_key=None)

    def emit(self):
        nc = self.nc
        with contextlib.ExitStack() as st:
            esem = {e: st.enter_context(nc.semaphore("sem_" + e)) for e in ENGINES}
            dsem = {}
            for k in self.dma_sems:
                dsem[k] = st.enter_context(nc.semaphore("dsem%d" % len(dsem)))
            for e in ENGINES:
                c = 0
                for op in self.ops[e]:
                    if op.need_sig:
                        c += 1
                        op.sigval = c
            block = st.enter_context(nc.Block())
            bmap = {"pe": block.tensor, "act": block.scalar, "dve": block.vector,
                    "pool": block.gpsimd, "sp": block.sync}

            def make(ename):
                def body(eng):
                    for op in self.ops[ename]:
                        for w in op.waits:
                            if w[0] == "dma":
                                eng.wait_ge(dsem[w[1]], w[2])
                            else:
                                d = w[1]
                                eng.wait_ge(esem[d.eng], d.sigval)
                        if op.fn is None:
                            continue
                        ins = op.fn(eng)
                        if op.dma_sem is not None:
                            ins.then_inc(dsem[op.dma_sem], 16)
                        elif op.need_sig:
                            ins.then_inc(esem[ename], 1)
                return body
            for e in ENGINES:
                if self.ops[e]:
                    bmap[e](make(e))


D = 1024
KC = 8
SEQ = 2048
DSEQ = 16
NT = SEQ + DSEQ
L = 4
DFF = 2816
FC = DFF // 128
QL, KVL, RD = 384, 256, 32
NH = 8
EPS = 1e-6
SM_SCALE = (64 + 32) ** -0.5
CBW, CDW = 31, 3
PADB, PADD = CBW - 1, CDW - 1

VOFF = {}
_o = 0
for _n, _w in (("f1pre", 8), ("f1post", 8), ("mpre", 8), ("mpost", 8), ("f2pre", 8), ("f2post", 8),
               ("qn", 3), ("kvn", 2), ("cbb", 2), ("cblg", 2), ("cblb", 2), ("cbw", CBW * 2), ("cdw", CDW * 2),
               ("vng", 2), ("vnb", 2)):
    VOFF[_n] = _o
    _o += _w
NV = _o

PASSES = (
    (0, 1024, ((0, 512), (512, 512))),
    (1024, 1040, ((1024, 512), (1536, 512), (2048, 16))),
)


class Arena:
    def __init__(self, prog, space, size):
        self.prog, self.space, self.size = prog, space, size
        self.top = 0
        self.peak = 0

    def alloc(self, ncols, dt, name=""):
        esz = 2 if dt == BF16 else 4
        n32 = (ncols * esz + 3) // 4
        n32 = (n32 + 127) // 128 * 128
        assert self.top + n32 <= self.size, ("arena overflow", name, self.top, n32, self.size)
        b = Buf(self.prog, self.space, self.top, ncols, dt, name)
        self.top += n32
        self.peak = max(self.peak, self.top)
        return b

    def at(self, base32, ncols, dt, name=""):
        return Buf(self.prog, self.space, base32, ncols, dt, name)


import os


def build_program(n_layers=L, do_mixer=True, do_ffn=True, stop=99):
    nc = bass.Bass("TRN2", target_bir_lowering=False)

    def din(name, shape):
        return nc.dram_tensor(name, list(shape), F32, kind="ExternalInput").ap()

    def dout(name, shape):
        return nc.dram_tensor(name, list(shape), F32, kind="ExternalOutput").ap()

    d_x = din("xT", (128, KC * NT))
    d_vecs = din("vecs", (128, L * NV))
    d_ident = din("ident", (128, 128))
    d_cos = din("cosT", (32, NT))
    d_sin = din("sinS", (32, NT))
    d_wgu = din("wgu", (L, 2, FC, 128, KC * 256))
    d_wdn = din("wdn", (L, 2, 8, 128, FC * 128))
    d_wcq = din("wcq", (L, 128, KC * QL))
    d_wckv = din("wckv", (L, 128, KC * KVL))
    d_wkr = din("wkr", (L, 2, 128, KC * 96))
    d_wglu = din("wglu", (L, 128, KC * 512))
    d_wuv = din("wuv", (L, 128, KC * 512))
    d_wdp = din("wdp", (L, 128, KC * 768))
    d_wgm = din("wgm", (L, 8, 128, KC * 512 + 10 * 128))
    d_wo = din("wo", (L, 8, 128, KC * 128))
    d_wuq = din("wuq", (L, 2, 128, 3 * 768))
    d_wuk = din("wuk", (L, 128, 2 * NH * 64))
    d_wuv2 = din("wuv2", (L, 128, 2 * 512))
    d_wukT = din("wukT", (L, 64, NH * 256))
    d_wsT = din("wsT", (L, 128, 512))
    d_bs = din("bsrow", (L, 1, 512))
    d_gvb = din("gvb", (L, 128, 512))
    d_clatT = din("clatT", (L, 128, 2 * SEQ))
    d_clat = din("clat", (L, 128, 16 * 256))
    d_ckr = din("ckrT", (L, 32, SEQ))
    d_scb = din("scbT", (L, 128, 2 * PADB))
    d_scd = din("scdT", (L, 128, 2 * PADD))

    o_y = dout("yT", (128, KC * NT))
    o_lat = dout("latT", (L, 128, 2 * NT))
    o_kr = dout("krT", (L, 32, NT))
    o_cb = dout("cbT", (L, 128, 2 * 2 * PADB))
    o_cd = dout("cdT", (L, 128, 2 * 2 * PADD))
    o_gv = dout("gvT", (L, 128, 2 * DSEQ))

    with contextlib.ExitStack() as st:
        XCOLS = KC * NT
        CCOLS = L * NV + 128 + 64 + 64 + 2 * ATOM + 8
        CCOLS = (CCOLS + 127) // 128 * 128
        ASIZE = 35456
        t_x = st.enter_context(nc.sbuf_tensor("xres", [128, XCOLS], F32))
        t_c = st.enter_context(nc.sbuf_tensor("consts", [128, CCOLS], F32))
        t_a = st.enter_context(nc.sbuf_tensor("arena", [128, ASIZE], F32))
        t_p = st.enter_context(nc.psum_tensor("psum", [128, 8 * 512], F32))
        P = Prog(nc)
        P.tensors.update({"X": t_x, "C": t_c, "A": t_a, "PS": t_p})
        xT = Buf(P, "X", 0, XCOLS, F32, "xT")
        vecs = Buf(P, "C", 0, L * NV, F32, "vecs")
        identf = Buf(P, "C", L * NV, 128, F32, "identf")
        onesb = Buf(P, "C", L * NV + 128, 128, BF16, "ones")
        identb = Buf(P, "C", L * NV + 192, 128, BF16, "identb")
        banks = [Buf(P, "PS", i * 512, 512, F32, "bank%d" % i) for i in range(8)]
        AR = Arena(P, "A", ASIZE)
        state = {"bank": 0}

        def bank(lo=0, hi=8):
            b = banks[lo + state["bank"] % (hi - lo)]
            state["bank"] += 1
            return b

        def vcol(l, name, i=0):
            c = l * NV + VOFF[name] + i
            return vecs.s(c, c + 1)

        def xs(k, t0, n):
            return xT.s(k * NT + t0, k * NT + t0 + n)

        for k in range(KC):
            P.dma("sp", xT.s(k * NT, (k + 1) * NT), d_x[:, k * NT:(k + 1) * NT])
        P.dma("sp", vecs.all(), d_vecs)
        P.dma("sp", identf.all(), d_ident)
        epsb = {1.0: Buf(P, "C", L * NV + 256, 1, F32, "eps1"), 4.0: Buf(P, "C", L * NV + 256 + ATOM, 1, F32, "eps4")}
        for f2_, b_ in epsb.items():
            P.memset(b_.all(), EPS * f2_, eng="dve")
        epsb = {k_: v_.all() for k_, v_ in epsb.items()}
        P.memset(onesb.all(), 1.0, eng="dve")
        P.copy(identb.all(), identf.all(), eng="dve")

        wq = {"i": 0}

        def wload(dst, src):
            P.dma("pool", dst, src, key=("w", wq["i"] % 6))
            wq["i"] += 1

        def recip(out, in_):
            P.record("dve", (lambda e, o=out, i=in_: e.reciprocal(o.ap, i.ap)), [in_], [out])

        def rstd_from(srcs, n, Dn, out, sq, post=1.0):
            sqs = []
            for i, r in enumerate(srcs):
                q = sq.s(i * 512, i * 512 + n)
                P.act(q, r, AF.Square)
                sqs.append(q)
            ps = bank().s(0, n)
            P.mm(ps, [(onesb.all(), q) for q in sqs])
            f2 = 1.0 / (post * post)
            P.act(out, ps, AF.Ln, scale=f2 / Dn, bias=epsb[f2])
            P.act(out, out, AF.Exp, scale=-0.5)

        def prenorm(l, gname, tiles, T0, TP, nT, sq, rstd):
            for (t0, n) in tiles:
                tl = t0 - T0
                r = rstd.s(0, n)
                rstd_from([xs(k, t0, n) for k in range(KC)], n, D, r, sq)
                for k in range(KC):
                    P.stt(nT.s(k * TP + tl, k * TP + tl + n), xs(k, t0, n), vcol(l, gname, k), r,
                          ALU.mult, ALU.mult)

        def postnorm_residual(l, gname, tiles, T0, TP, ytmp, sq, rstd, half):
            for (t0, n) in tiles:
                tl = t0 - T0
                r = rstd.s(0, n)
                ys = [ytmp.s(k * TP + tl, k * TP + tl + n) for k in range(KC)]
                rstd_from(ys, n, D, r, sq, post=0.5 if half else 1.0)
                for k in range(KC):
                    P.stt(ys[k], ys[k], vcol(l, gname, k), r, ALU.mult, ALU.mult)
                    P.tt(xs(k, t0, n), xs(k, t0, n), ys[k], ALU.add)

        def ffn_pair(l, f, next_pre=None, skip_preA=False):
            pre, post = ("f1pre", "f1post") if f == 0 else ("f2pre", "f2post")
            (TA0, TPA, tilesA), (TB0, TPB, tilesB) = PASSES
            AR.top = 0
            ytmp = AR.alloc(KC * TPB, F32, "ytmp")
            nTA = AR.at(ytmp.base32, KC * TPA, BF16, "nTA")
            wgu = [AR.alloc(KC * 256, BF16, "wgu%d" % i) for i in range(3)]
            assert AR.top <= 12288
            nTB = AR.alloc(KC * TPB, BF16, "nTB")
            hT = AR.alloc(FC * TPB, BF16, "hT")
            wdn = [AR.alloc(FC * 128, BF16, "wdn%d" % i) for i in range(3)]
            sq = AR.alloc(KC * 512, BF16, "sq")
            rstd = AR.alloc(512, F32, "rstd")
            sg = [AR.alloc(512, F32, "sg%d" % i) for i in range(2)]
            cnt = {"s": 0}

            def up(T0, TP, tiles, nT, hook=None):
                for c in range(FC):
                    w = wgu[c % 3]
                    wload(w.all(), d_wgu[l, f, c])
                    for (t0, n) in tiles:
                        tl = t0 - T0
                        pg = bank().s(0, n)
                        pu = bank().s(0, n)
                        P.mm(pg, [(w.s(k * 256, k * 256 + 128), nT.s(k * TP + tl, k * TP + tl + n)) for k in range(KC)])
                        P.mm(pu, [(w.s(k * 256 + 128, k * 256 + 256), nT.s(k * TP + tl, k * TP + tl + n)) for k in range(KC)])
                        s_ = sg[cnt["s"] % 2].s(0, n)
                        cnt["s"] += 1
                        P.act(s_, pg, AF.Silu)
                        P.tt(hT.s(c * TP + tl, c * TP + tl + n), pu, s_, ALU.mult)
                    if hook is not None:
                        hook(c)

            def down(T0, TP, tiles):
                for m in range(8):
                    w = wdn[m % 3]
                    wload(w.all(), d_wdn[l, f, m])
                    for (t0, n) in tiles:
                        tl = t0 - T0
                        py = bank().s(0, n)
                        P.mm(py, [(w.s(k * 128, k * 128 + 128), hT.s(k * TP + tl, k * TP + tl + n)) for k in range(FC)])
                        P.act(ytmp.s(m * TP + tl, m * TP + tl + n), py, AF.Copy)

            def hookA(c):
                if c % 4 == 2 and c // 4 < len(tilesA):
                    postnorm_residual(l, post, [tilesA[c // 4]], TA0, TPA, ytmp, sq, rstd, True)

            P.mark("L%d ffn%d upA" % (l, f))
            if not skip_preA:
                prenorm(l, pre, tilesA, TA0, TPA, nTA, sq, rstd)
            up(TA0, TPA, tilesA, nTA)
            P.mark("L%d ffn%d downA" % (l, f))
            prenorm(l, pre, tilesB, TB0, TPB, nTB, sq, rstd)
            down(TA0, TPA, tilesA)
            P.mark("L%d ffn%d upB" % (l, f))
            up(TB0, TPB, tilesB, nTB, hook=hookA)
            P.mark("L%d ffn%d downB" % (l, f))
            down(TB0, TPB, tilesB)
            P.mark("L%d ffn%d end" % (l, f))
            if next_pre is not None:
                next_pre()
            postnorm_residual(l, post, tilesB, TB0, TPB, ytmp, sq, rstd, True)

        def mixer_layer(l, early_f2=None):
            AR.top = 0
            kT = AR.alloc(NH * SEQ, BF16, "kT")
            Vt = AR.alloc(16 * 512, BF16, "V")
            carryb = AR.alloc(2 * PADB, BF16, "carryb")
            carryd = AR.alloc(2 * PADD, BF16, "carryd")
            base_top = AR.top
            for pi, (T0, TP, tiles) in enumerate(PASSES):
                AR.top = base_top
                for _ in mixer_pass(l, pi, T0, TP, tiles, kT, Vt, carryb, carryd, early_f2):
                    if pi == 0:
                        yield

        def mixer_pass(l, pi, T0, TP, tiles, kT, Vt, carryb, carryd, early_f2=None):
            isB = pi == 1
            ptiles = [t for t in tiles if t[0] < SEQ]
            stile = (SEQ, DSEQ) if isB else None
            PT0 = T0
            PTN = sum(n for (_, n) in ptiles)
            ybase = AR.top
            nT = AR.alloc(KC * TP, BF16, "nT")
            aT = AR.alloc(4 * TP, BF16, "aT")
            ybT = AR.alloc(2 * TP, BF16, "ybT")
            ycT = AR.alloc(2 * TP, BF16, "ycT")
            ydT = AR.alloc(2 * TP, BF16, "ydT")
            need = (KC * TP * 4 - (AR.top - ybase) * 4)
            if need > 0:
                AR.alloc(need // 4, F32, "ypad")
            ytmp = AR.at(ybase, KC * TP, F32, "ytmp")
            sq = AR.alloc(KC * 512, BF16, "sq")
            rstd = AR.alloc(512, F32, "rstd")
            tf = [AR.alloc(512, F32, "tf%d" % i) for i in range(4)]
            tabc = AR.alloc(512, F32, "tabc")
            tabs = AR.alloc(512, F32, "tabs")
            stage_top = AR.top

            def ntk(k, tl, n):
                return nT.s(k * TP + tl, k * TP + tl + n)

            prenorm(l, "mpre", tiles, T0, TP, nT, sq, rstd)
            yield
            P.mark("L%d mix%d st2" % (l, pi))

            cqn = AR.alloc(3 * TP, BF16, "cqn")
            att_top = AR.top
            latb = AR.alloc(2 * TP, BF16, "latb")
            krs = AR.alloc(DSEQ, BF16, "krs")
            wsl = AR.alloc(2 * KC * 96, BF16, "wslot")
            wsl2 = AR.alloc(KC * KVL, BF16, "wslot2")
            latf = AR.alloc(2 * 512, F32, "latf")
            krf = AR.alloc(512, F32, "krf")
            cqf = AR.at(tf[1].base32, 3 * 512, F32, "cqf")
            r = None
            wUK = AR.at(wsl2.base32, 2 * NH * 64, BF16, "wUK")
            wUV = AR.at(wsl2.base32 + 512, 2 * 512, BF16, "wUV")

            def load_cd():
                wload(wsl.s(0, KC * 96), d_wkr[l, 0])
                wload(wsl.s(KC * 96, 2 * KC * 96), d_wkr[l, 1])
                wload(wUK.all(), d_wuk[l])
                wload(wUV.all(), d_wuv2[l])
            assert wsl2.base32 == wsl.base32 + 768
            if pi == 0:
                wslA = AR.at(Vt.base32 + 128, KC * QL, BF16, "wslA")
                wslB = AR.at(Vt.base32 + 128 + 1536, KC * KVL, BF16, "wslB")
            else:
                wslA = AR.at(wsl.base32, KC * QL, BF16, "wslA")
                wslB = AR.at(wsl.base32, KC * KVL, BF16, "wslB")
            wload(wslA.all(), d_wcq[l])
            if pi == 0:
                wload(wslB.all(), d_wckv[l])
                load_cd()
            for (t0, n) in tiles:
                tl = t0 - T0
                for mc in range(3):
                    ps = bank().s(0, n)
                    P.mm(ps, [(wslA.s(k * QL + mc * 128, k * QL + mc * 128 + 128), ntk(k, tl, n)) for k in range(KC)])
                    P.act(cqf.s(mc * 512, mc * 512 + n), ps, AF.Copy)
                r = rstd.s(0, n)
                rstd_from([cqf.s(mc * 512, mc * 512 + n) for mc in range(3)], n, QL, r, sq)
                for mc in range(3):
                    P.stt(cqn.s(mc * TP + tl, mc * TP + tl + n), cqf.s(mc * 512, mc * 512 + n),
                          vcol(l, "qn", mc), r, ALU.mult, ALU.mult)
            if pi == 1:
                wload(wslB.all(), d_wckv[l])
            for (t0, n) in tiles:
                tl = t0 - T0
                r = rstd.s(0, n)
                for mc in range(2):
                    ps = bank().s(0, n)
                    P.mm(ps, [(wslB.s(k * KVL + mc * 128, k * KVL + mc * 128 + 128), ntk(k, tl, n)) for k in range(KC)])
                    P.act(latf.s(mc * 512, mc * 512 + n), ps, AF.Copy)
                rstd_from([latf.s(mc * 512, mc * 512 + n) for mc in range(2)], n, KVL, r, sq)
                for mc in range(2):
                    P.stt(latf.s(mc * 512, mc * 512 + n), latf.s(mc * 512, mc * 512 + n),
                          vcol(l, "kvn", mc), r, ALU.mult, ALU.mult)
                    P.act(latb.s(mc * TP + tl, mc * TP + tl + n), latf.s(mc * 512, mc * 512 + n), AF.Copy)
                    P.dma("sp", d_lat_out(l, mc, t0, n), latf.s(mc * 512, mc * 512 + n))
            if pi == 1:
                load_cd()
            for (t0, n) in tiles:
                tl = t0 - T0
                samp = t0 >= SEQ
                P.dma("sp", tabc.s(0, n, 64, 96), d_cos[:, t0:t0 + n])
                P.dma("sp", tabs.s(0, n, 64, 96), d_sin[:, t0:t0 + n])
                pa = bank().s(0, n, 0, 96)
                pb = bank().s(0, n, 0, 96)
                P.mm(pa, [(wsl.s(k * 96, k * 96 + 96), ntk(k, tl, n)) for k in range(KC)])
                P.mm(pb, [(wsl.s(KC * 96 + k * 96, KC * 96 + k * 96 + 96), ntk(k, tl, n)) for k in range(KC)])
                t1 = tf[0].s(0, n, 64, 96)
                P.tt(t1, bank_rows(pa, 64, 96), tabc.s(0, n, 64, 96), ALU.mult)
                P.tt(krf.s(0, n, 64, 96), bank_rows(pb, 64, 96), tabs.s(0, n, 64, 96), ALU.mult)
                P.tt(krf.s(0, n, 64, 96), krf.s(0, n, 64, 96), t1, ALU.add)
                P.dma("sp", o_kr[l, :, t0:t0 + n], krf.s(0, n, 64, 96))
                if samp:
                    P.copy(krs.s(0, n, 64, 96), krf.s(0, n, 64, 96), eng="dve")
                else:
                    for h in range(NH):
                        dst = kT.s(h * SEQ + t0, h * SEQ + t0 + n, 64, 96)
                        if h % 2 == 0:
                            P.act(dst, krf.s(0, n, 64, 96), AF.Copy)
                        else:
                            P.copy(dst, krf.s(0, n, 64, 96), eng="dve")
            for (t0, n) in ptiles:
                tl = t0 - T0
                for h in range(NH):
                    ps = bank().s(0, n, 0, 64)
                    P.mm(ps, [(wUK.s((k * NH + h) * 64, (k * NH + h) * 64 + 64), latb.s(k * TP + tl, k * TP + tl + n))
                              for k in range(2)])
                    dst = kT.s(h * SEQ + t0, h * SEQ + t0 + n, 0, 64)
                    if h % 2 == 0:
                        P.copy(dst, ps, eng="dve")
                    else:
                        P.act(dst, ps, AF.Copy)
                for bi in range(n // 128):
                    b = (t0 + bi * 128) // 128
                    ps = bank().all()
                    P.mm(ps, [(latb.s(k * TP + tl + bi * 128, k * TP + tl + bi * 128 + 128), wUV.s(k * 512, k * 512 + 512))
                              for k in range(2)])
                    if bi % 2 == 0:
                        P.act(Vt.s(b * 512, b * 512 + 512), ps, AF.Copy)
                    else:
                        P.copy(Vt.s(b * 512, b * 512 + 512), ps, eng="dve")

            if stop <= 2:
                return
            P.mark("L%d mix%d att" % (l, pi))
            AR.top = att_top
            latb2 = AR.alloc(2 * TP, BF16, "latb")
            krs2 = AR.alloc(DSEQ, BF16, "krs")
            assert latb2.base32 == latb.base32 and krs2.base32 == krs.base32
            wUQ = [AR.alloc(3 * 768, BF16, "wUQ%d" % i) for i in range(2)]
            qT = AR.alloc(NH * 512, BF16, "qT")
            Pt = [AR.at(sq.base32 + i * 256, 512, BF16, "Pt%d" % i) for i in range(4)]
            rden = [tf[2], tf[3]]
            wload(wUQ[0].all(), d_wuq[l, 0])
            wload(wUQ[1].all(), d_wuq[l, 1])

            def load_tabs(t0, n):
                P.dma("sp", tabc.s(0, n, 64, 96), d_cos[:, t0:t0 + n])
                P.dma("sp", tabs.s(0, n, 64, 96), d_sin[:, t0:t0 + n])

            def q_head(t0, n, tl, qdst, qstride, h):
                pa = bank(0, 4).s(0, n, 0, 96)
                pb = bank(0, 4).s(0, n, 0, 96)
                P.mm(pa, [(wUQ[0].s(k * 768 + h * 96, k * 768 + h * 96 + 96), cqn.s(k * TP + tl, k * TP + tl + n))
                          for k in range(3)])
                P.mm(pb, [(wUQ[1].s(k * 768 + h * 96, k * 768 + h * 96 + 96), cqn.s(k * TP + tl, k * TP + tl + n))
                          for k in range(3)])
                P.copy(qdst.s(h * qstride, h * qstride + n, 0, 64), bank_rows(pa, 0, 64), eng="dve")
                t1 = tf[0].s(0, n, 64, 96)
                t2 = tf[1].s(0, n, 64, 96)
                P.tt(t1, bank_rows(pa, 64, 96), tabc.s(0, n, 64, 96), ALU.mult)
                P.tt(t2, bank_rows(pb, 64, 96), tabs.s(0, n, 64, 96), ALU.mult)
                P.tt(qdst.s(h * qstride, h * qstride + n, 64, 96), t1, t2, ALU.add, eng="pool")

            qs = AR.alloc(NH * DSEQ, BF16, "qs") if isB else None
            pcount = {"i": 0}
            for ti, (t0, n) in enumerate(ptiles):
                tl = t0 - T0
                gi = t0 // 512
                if ti == 0:
                    load_tabs(t0, n)
                    for h in range(NH):
                        q_head(t0, n, tl, qT, 512, h)
                if ti + 1 < len(ptiles):
                    nxt = (ptiles[ti + 1][0], ptiles[ti + 1][1], ptiles[ti + 1][0] - T0, qT, 512)
                elif isB:
                    nxt = (stile[0], stile[1], stile[0] - T0, qs, DSEQ)
                else:
                    nxt = None
                if nxt is not None:
                    load_tabs(nxt[0], nxt[1])
                nblk = 4 * gi + 4
                items = [(h, j) for h in range(NH) for j in range(nblk)]
                pts = {}

                def issue_scores(idx):
                    h, j = items[idx]
                    m = j - 4 * gi
                    c0 = 0 if m < 0 else 128 * m
                    ps = bank(0, 4).s(c0, 512)
                    P.mm(ps, [(kT.s(h * SEQ + j * 128, h * SEQ + j * 128 + 128, 0, 96),
                               qT.s(h * 512 + c0, h * 512 + 512, 0, 96))])
                    pt = Pt[pcount["i"] % 4]
                    pcount["i"] += 1
                    P.act(pt.s(c0, 512), ps, AF.Exp, scale=SM_SCALE)
                    if m >= 0:
                        P.memset(pt.s(c0, c0 + 64, 64, 128), 0.0, eng="pool" if gi == 0 else "dve")
                    pts[idx] = (pt, c0)

                def issue_pv(idx):
                    h, j = items[idx]
                    hp, half = h // 2, h % 2
                    acc = banks[4 + (h % 2) * 2]
                    den = banks[5 + (h % 2) * 2]
                    pt, c0 = pts.pop(idx)
                    P.mm(acc.s(c0, 512), [(Vt.s(j * 512 + hp * 128, j * 512 + hp * 128 + 128), pt.s(c0, 512))],
                         start=(j == 0), stop=(j == nblk - 1))
                    P.mm(den.s(c0, 512), [(onesb.all(), pt.s(c0, 512))],
                         start=(j == 0), stop=(j == nblk - 1))
                    if j == nblk - 1:
                        def fin(h=h, hp=hp, half=half, acc=acc, den=den):
                            r0, r1 = half * 64, half * 64 + 64
                            rd = rden[h % 2]
                            recip(rd.s(0, 512, r0, r1), den.s(0, 512, r0, r1))
                            P.tt(aT.s(hp * TP + tl, hp * TP + tl + n, r0, r1), acc.s(0, 512, r0, r1), rd.s(0, 512, r0, r1), ALU.mult)
                        pend.append(fin)
                        if nxt is not None:
                            q_head(nxt[0], nxt[1], nxt[2], nxt[3], nxt[4], h)

                LA = 3
                pend = []
                for idx in range(len(items) + LA):
                    if idx < len(items):
                        issue_scores(idx)
                        if pend:
                            pend.pop(0)()
                    if idx >= LA:
                        issue_pv(idx - LA)
                while pend:
                    pend.pop(0)()

            if stop <= 3:
                return
            if isB:
                t0, n = stile
                tl = t0 - T0
                mark = AR.top
                AR.top = kT.base32
                clatT = AR.alloc(2 * SEQ, BF16, "clatT")
                clat = AR.alloc(16 * 256, BF16, "clat")
                ckr = AR.alloc(SEQ, BF16, "ckr")
                wUKT = AR.alloc(NH * 256, BF16, "wUKT")
                qabs = AR.alloc(2 * 128, BF16, "qabs")
                latn = AR.alloc(256, BF16, "latn")
                olat = AR.alloc(2 * 128, BF16, "olat")
                pts = [AR.alloc(128, BF16, "pts%d" % i) for i in range(4)]
                wUVs = AR.alloc(2 * 512, BF16, "wUVs")
                wload(wUVs.all(), d_wuv2[l])
                assert AR.top <= Vt.base32 + Vt.n32
                wload(clatT.all(), d_clatT[l])
                wload(clat.all(), d_clat[l])
                wload(ckr.s(0, SEQ, 64, 96), d_ckr[l])
                wload(wUKT.s(0, NH * 256, 0, 64), d_wukT[l])
                pq = banks[0]
                for h in range(NH):
                    for kc in range(2):
                        P.mm(pq.s(kc * 128 + h * 16, kc * 128 + h * 16 + 16),
                             [(wUKT.s(h * 256 + kc * 128, h * 256 + kc * 128 + 128, 0, 64), qs.s(h * 16, h * 16 + 16, 0, 64))])
                P.act(qabs.all(), pq.s(0, 256), AF.Copy)
                pl = banks[1]
                for kc in range(2):
                    P.mm(pl.s(kc * 128, kc * 128 + 128, 0, 16),
                         [(latb.s(kc * TP + tl, kc * TP + tl + n), identb.all())])
                P.copy(latn.s(0, 256, 0, 16), pl.s(0, 256, 0, 16), eng="dve")
                ol = [banks[4], banks[5]]
                dn = banks[6]
                for j in range(17):
                    new = j == 16
                    kk = 16 if new else 128
                    ps = banks[2 + j % 2].s(0, 128, 0, kk)
                    if new:
                        pairs = [(latb.s(kc * TP + tl, kc * TP + tl + n), qabs.s(kc * 128, kc * 128 + 128)) for kc in range(2)]
                        pairs.append((krs.s(0, n, 64, 96), qs.s(0, 128, 64, 96)))
                    else:
                        pairs = [(clatT.s(kc * SEQ + j * 128, kc * SEQ + j * 128 + 128), qabs.s(kc * 128, kc * 128 + 128))
                                 for kc in range(2)]
                        pairs.append((ckr.s(j * 128, j * 128 + 128, 64, 96), qs.s(0, 128, 64, 96)))
                    P.mm(ps, pairs)
                    pt = pts[j % 4].s(0, 128, 0, kk)
                    P.act(pt, ps, AF.Exp, scale=SM_SCALE)
                    for kc in range(2):
                        lhs = latn.s(kc * 128, kc * 128 + 128, 0, 16) if new else clat.s(j * 256 + kc * 128, j * 256 + kc * 128 + 128)
                        P.mm(ol[kc].s(0, 128), [(lhs, pt)], start=(j == 0), stop=new)
                    P.mm(dn.s(0, 128), [(onesb.s(0, 128, 0, kk), pt)], start=(j == 0), stop=new)
                for kc in range(2):
                    P.act(olat.s(kc * 128, kc * 128 + 128), ol[kc].s(0, 128), AF.Copy)
                rd = rden[0]
                P.record("dve", (lambda e, o=rd.s(0, 128), i=dn.s(0, 128): e.reciprocal(o.ap, i.ap)),
                         [dn.s(0, 128)], [rd.s(0, 128)])
                for h in range(NH):
                    hp, half = h // 2, h % 2
                    r0, r1 = half * 64, half * 64 + 64
                    ps = banks[h % 2].s(256, 256 + 16)
                    P.mm(ps, [(wUVs.s(kc * 512 + hp * 128, kc * 512 + hp * 128 + 128), olat.s(kc * 128 + h * 16, kc * 128 + h * 16 + 16))
                              for kc in range(2)])
                    P.tt(aT.s(hp * TP + tl, hp * TP + tl + n, r0, r1), bank_rows(ps, r0, r1), rd.s(h * 16, h * 16 + 16, r0, r1), ALU.mult)
                AR.top = mark

            if isB and early_f2 is not None:
                early_f2(sq, rstd)
            if stop <= 4:
                return
            P.mark("L%d mix%d brB" % (l, pi))
            AR.top = stage_top
            branch_b(l, pi, T0, TP, tiles, ptiles, stile, nT, ybT, carryb, sq, rstd, tf + [tabc, tabs])
            if stop <= 5:
                return
            P.mark("L%d mix%d brC" % (l, pi))
            AR.top = stage_top
            branch_c(l, pi, T0, TP, tiles, ptiles, stile, nT, ycT, sq, rstd, tf + [tabc, tabs])
            if stop <= 6:
                return
            P.mark("L%d mix%d brD" % (l, pi))
            AR.top = stage_top
            branch_d(l, pi, T0, TP, tiles, ptiles, stile, nT, ydT, carryd, tf)

            if stop <= 7:
                return
            P.mark("L%d mix%d merge" % (l, pi))
            AR.top = stage_top
            mrg = AR.alloc(KC * TP, BF16, "merged")
            wG = [AR.alloc(KC * 512 + 10 * 128, BF16, "wG0"),
                  AR.at(sq.base32, KC * 512 + 10 * 128, BF16, "wG1")]
            assert sq.base32 + 2688 <= tf[1].base32
            wO = [AR.at(wG[0].base32 + i * 512, KC * 128, BF16, "wO%d" % i) for i in range(5)]
            gsb = [tabc, tabs]
            brs = ((aT, 4), (ybT, 2), (ycT, 2), (ydT, 2))
            for m in range(8):
                w = wG[m % 2]
                wload(w.all(), d_wgm[l, m])
                for (t0, n) in tiles:
                    tl = t0 - T0
                    kb0 = 0
                    for b, (src, nk) in enumerate(brs):
                        pg = bank().s(0, n)
                        pp = bank().s(0, n)
                        P.mm(pg, [(w.s((k * 4 + b) * 128, (k * 4 + b) * 128 + 128), ntk(k, tl, n)) for k in range(KC)])
                        P.mm(pp, [(w.s(KC * 512 + (kb0 + kk) * 128, KC * 512 + (kb0 + kk) * 128 + 128),
                                   src.s(kk * TP + tl, kk * TP + tl + n)) for kk in range(nk)])
                        kb0 += nk
                        g = gsb[b % 2].s(0, n)
                        P.act(g, pg, AF.Sigmoid)
                        if b == 0:
                            P.tt(tf[2].s(0, n), pp, g, ALU.mult)
                        else:
                            P.tt(tf[3].s(0, n), pp, g, ALU.mult)
                            dst = tf[2].s(0, n) if b < 3 else mrg.s(m * TP + tl, m * TP + tl + n)
                            P.tt(dst, tf[2].s(0, n), tf[3].s(0, n), ALU.add)
            P.mark("L%d mix%d wo" % (l, pi))
            for m in range(8):
                w = wO[m % 5]
                wload(w.all(), d_wo[l, m])
                for (t0, n) in tiles:
                    tl = t0 - T0
                    py = bank().s(0, n)
                    P.mm(py, [(w.s(k * 128, k * 128 + 128), mrg.s(k * TP + tl, k * TP + tl + n)) for k in range(KC)])
                    P.act(ytmp.s(m * TP + tl, m * TP + tl + n), py, AF.Copy)
            postnorm_residual(l, "mpost", tiles, T0, TP, ytmp, sq, rstd, False)

        def bank_rows(ref, p0, p1):
            return Ref(ref.ap[p0:p1] if p0 == 0 and False else ref.ap[p0:p1, :], ref.atoms)

        def d_lat_out(l, mc, t0, n):
            return o_lat[l, :, mc * NT + t0: mc * NT + t0 + n]


        def reduce_sum(out, in_):
            P.record("dve", (lambda e, o=out, i=in_: e.tensor_reduce(o.ap, i.ap, mybir.AxisListType.X, ALU.add)),
                     [in_], [out])

        def branch_b(l, pi, T0, TP, tiles, ptiles, stile, nT, ybT, carryb, sq, rstd, tf):
            PTN = sum(n for (_, n) in ptiles)
            W = PADB + PTN
            WS = PADB + DSEQ
            wgl = AR.alloc(KC * 512, BF16, "wgl")
            wload(wgl.all(), d_wglu[l])
            diag = AR.alloc(CBW * 2 * 128, BF16, "diagb")
            xbp = AR.alloc(2 * W, BF16, "xbp")
            xbs = AR.alloc(2 * WS, BF16, "xbs")
            cbfs = [rstd, tf[3]]
            mean = tf[4]
            rs = tf[5]
            cbst = AR.alloc(2 * 2 * PADB, F32, "cbst")
            for j in range(CBW):
                for c in range(2):
                    P.ts(diag.s((j * 2 + c) * 128, (j * 2 + c) * 128 + 128), identf.all(), vcol(l, "cbw", j * 2 + c), ALU.mult)
            for c in range(2):
                if pi == 0:
                    P.memset(xbp.s(c * W, c * W + PADB), 0.0, eng="dve")
                else:
                    P.copy(xbp.s(c * W, c * W + PADB), carryb.s(c * PADB, (c + 1) * PADB), eng="dve")
                    wload(xbs.s(c * WS, c * WS + PADB), d_scb[l][:, c * PADB:(c + 1) * PADB])
                    P.dma("sp", cbst.s(c * 60 + 30, c * 60 + 44), d_scb[l][:, c * PADB + 16:c * PADB + 30])
            def phA(t0, n):
                tl = t0 - T0
                samp = t0 >= SEQ
                for c in range(2):
                    pa = bank().s(0, n)
                    pg = bank().s(0, n)
                    P.mm(pa, [(wgl.s(k * 512 + c * 128, k * 512 + c * 128 + 128), nT.s(k * TP + tl, k * TP + tl + n)) for k in range(KC)])
                    P.mm(pg, [(wgl.s(k * 512 + 256 + c * 128, k * 512 + 256 + c * 128 + 128), nT.s(k * TP + tl, k * TP + tl + n)) for k in range(KC)])
                    sg = tf[c].s(0, n)
                    P.act(sg, pg, AF.Sigmoid)
                    if samp:
                        dst = xbs.s(c * WS + PADB, c * WS + PADB + n)
                        P.tt(cbst.s(c * 60 + 44, c * 60 + 60), pa, sg, ALU.mult)
                    else:
                        dst = xbp.s(c * W + PADB + tl, c * W + PADB + tl + n)
                        if t0 + n == SEQ:
                            P.tt(cbst.s(c * 60, c * 60 + 30), Ref(pa.ap[:, n - 30:n], pa.atoms), tf[c].s(n - 30, n), ALU.mult)
                    P.tt(dst, pa, sg, ALU.mult)

            def phB(t0, n):
                tl = t0 - T0
                samp = t0 >= SEQ
                for c in range(2):
                    ps = bank().s(0, n)
                    if samp:
                        P.mm(ps, [(diag.s((j * 2 + c) * 128, (j * 2 + c) * 128 + 128), xbs.s(c * WS + j, c * WS + j + n)) for j in range(CBW)])
                    else:
                        P.mm(ps, [(diag.s((j * 2 + c) * 128, (j * 2 + c) * 128 + 128), xbp.s(c * W + tl + j, c * W + tl + j + n)) for j in range(CBW)])
                    P.act(cbfs[c].s(0, n), ps, AF.Identity, bias=vcol(l, "cbb", c))
                    P.act(sq.s(c * 512, c * 512 + n), cbfs[c].s(0, n), AF.Copy)
                    P.act(sq.s((2 + c) * 512, (2 + c) * 512 + n), cbfs[c].s(0, n), AF.Square)

            def phC(t0, n):
                tl = t0 - T0
                pm = bank().s(0, n)
                pq = bank().s(0, n)
                P.mm(pm, [(onesb.all(), sq.s(c * 512, c * 512 + n)) for c in range(2)])
                P.mm(pq, [(onesb.all(), sq.s((2 + c) * 512, (2 + c) * 512 + n)) for c in range(2)])
                mn = mean.s(0, n)
                r = rs.s(0, n)
                m2 = tf[2].s(0, n)
                P.ts(mn, pm, 1.0 / 256, ALU.mult)
                P.tt(m2, mn, mn, ALU.mult)
                P.stt(r, pq, 1.0 / 256, m2, ALU.mult, ALU.subtract)
                P.act(r, r, AF.Ln, bias=epsb[1.0])
                P.act(r, r, AF.Exp, scale=-0.5)
                for c in range(2):
                    x = cbfs[c].s(0, n)
                    P.tt(x, x, mn, ALU.subtract)
                    P.tt(x, x, r, ALU.mult)
                    P.act(ybT.s(c * TP + tl, c * TP + tl + n), x, AF.Silu, scale=vcol(l, "cblg", c), bias=vcol(l, "cblb", c))

            phA(*tiles[0])
            phB(*tiles[0])
            for ti in range(1, len(tiles)):
                phA(*tiles[ti])
                phC(*tiles[ti - 1])
                phB(*tiles[ti])
            phC(*tiles[-1])
            if pi == 0:
                for c in range(2):
                    P.copy(carryb.s(c * PADB, (c + 1) * PADB), xbp.s(c * W + PTN, c * W + PTN + PADB), eng="dve")
            else:
                P.dma("sp", o_cb[l], cbst.all())

        def branch_c(l, pi, T0, TP, tiles, ptiles, stile, nT, ycT, sq, rstd, tf):
            wuv = AR.alloc(KC * 512, BF16, "wuv")
            wload(wuv.all(), d_wuv[l])
            wsT = AR.alloc(512, BF16, "wsT")
            wload(wsT.all(), d_wsT[l])
            bsr = AR.alloc(512, BF16, "bsr")
            wload(bsr.s(0, 512, 0, 1), d_bs[l])
            for g in range(4):
                P.memset(wsT.s(g * 128, g * 128 + 64, 64, 128), 0.0, eng="dve")
            uT = AR.alloc(2 * TP, BF16, "uT")
            vnT = AR.alloc(2 * 512, BF16, "vnT")
            vtm = [AR.alloc(256, BF16, "vtm%d" % i) for i in range(2)]
            gvst = AR.alloc(2 * DSEQ, F32, "gvst")
            vfs = [rstd, tf[3]]
            mean, rs = tf[0], tf[1]
            bcount = 0
            def phA(ti, t0, n):
                tl = t0 - T0
                vf_ = vfsets[ti % 2]
                so = (ti % 2) * 4
                for c in range(2):
                    pu = bank().s(0, n)
                    P.mm(pu, [(wuv.s(k * 512 + c * 128, k * 512 + c * 128 + 128), nT.s(k * TP + tl, k * TP + tl + n)) for k in range(KC)])
                    P.act(uT.s(c * TP + tl, c * TP + tl + n), pu, AF.Copy)
                for c in range(2):
                    pv = bank().s(0, n)
                    P.mm(pv, [(wuv.s(k * 512 + 256 + c * 128, k * 512 + 256 + c * 128 + 128), nT.s(k * TP + tl, k * TP + tl + n)) for k in range(KC)])
                    P.act(vf_[c].s(0, n), pv, AF.Copy)
                    P.act(sq.s((so + c) * 512, (so + c) * 512 + n), vf_[c].s(0, n), AF.Copy)
                    P.act(sq.s((so + 2 + c) * 512, (so + 2 + c) * 512 + n), vf_[c].s(0, n), AF.Square)

            def phC(ti, t0, n):
                tl = t0 - T0
                samp = t0 >= SEQ
                vf_ = vfsets[ti % 2]
                so = (ti % 2) * 4
                pm = bank().s(0, n)
                pq = bank().s(0, n)
                P.mm(pm, [(onesb.all(), sq.s((so + c) * 512, (so + c) * 512 + n)) for c in range(2)])
                P.mm(pq, [(onesb.all(), sq.s((so + 2 + c) * 512, (so + 2 + c) * 512 + n)) for c in range(2)])
                mn = mean.s(0, n)
                r = rs.s(0, n)
                m2 = tf[2].s(0, n)
                P.ts(mn, pm, 1.0 / 256, ALU.mult)
                P.tt(m2, mn, mn, ALU.mult)
                P.stt(r, pq, 1.0 / 256, m2, ALU.mult, ALU.subtract)
                P.act(r, r, AF.Ln, bias=epsb[1.0])
                P.act(r, r, AF.Exp, scale=-0.5)
                for c in range(2):
                    x = vf_[c].s(0, n)
                    P.tt(x, x, mn, ALU.subtract)
                    P.tt(x, x, r, ALU.mult)
                    if samp:
                        P.act(gvst.s(c * DSEQ, (c + 1) * DSEQ), x, AF.Identity, scale=vcol(l, "vng", c), bias=vcol(l, "vnb", c))
                    P.act(vnT.s(c * 512, c * 512 + n), x, AF.Identity, scale=vcol(l, "vng", c), bias=vcol(l, "vnb", c))
                if samp:
                    P.dma("sp", o_gv[l], gvst.all())
                bsz = min(n, 128)
                for bi in range(max(1, n // 128)):
                    c0 = tl + bi * 128
                    v16 = vtm[bc["i"] % 2]
                    bc["i"] += 1
                    pt = bank().s(0, 256, 0, bsz)
                    for ch in range(2):
                        P.mm(Ref(pt.ap[:, ch * 128:(ch + 1) * 128], pt.atoms),
                             [(vnT.s(ch * 512 + bi * 128, ch * 512 + bi * 128 + bsz), identb.all())])
                    P.copy(v16.s(0, 256, 0, bsz), pt, eng="dve")
                    for ch in range(2):
                        for gg in range(2):
                            g = 2 * ch + gg
                            psg = bank().s(0, bsz)
                            P.mm(psg, [(v16.s(ch * 128, ch * 128 + 128, 0, bsz), wsT.s(g * 128, g * 128 + bsz, 0, bsz)),
                                       (onesb.s(0, 128, 0, 1), bsr.s(g * 128, g * 128 + bsz, 0, 1))])
                            r0, r1 = gg * 64, gg * 64 + 64
                            P.tt(ycT.s(ch * TP + c0, ch * TP + c0 + bsz, r0, r1), Ref(psg.ap[r0:r1, :], psg.atoms),
                                 uT.s(ch * TP + c0, ch * TP + c0 + bsz, r0, r1), ALU.mult)

            bc = {"i": 0}
            vfsets = [[rstd, tf[3]], [tf[4], tf[5]]]
            phA(0, *tiles[0])
            for ti in range(1, len(tiles)):
                phA(ti, *tiles[ti])
                phC(ti - 1, *tiles[ti - 1])
            phC(len(tiles) - 1, *tiles[-1])

        def branch_d(l, pi, T0, TP, tiles, ptiles, stile, nT, ydT, carryd, tf):
            PTN = sum(n for (_, n) in ptiles)
            W = PADD + PTN
            WS = PADD + DSEQ
            wdp = AR.alloc(KC * 768, BF16, "wdp")
            wload(wdp.all(), d_wdp[l])
            diag = AR.alloc(CDW * 2 * 128, BF16, "diagd")
            xdp = AR.alloc(2 * W, BF16, "xdp")
            xds = AR.alloc(2 * WS, BF16, "xds")
            bgT = AR.alloc(2 * TP, BF16, "bgT")
            cdst = AR.alloc(2 * 2 * PADD, F32, "cdst")
            for j in range(CDW):
                for c in range(2):
                    P.ts(diag.s((j * 2 + c) * 128, (j * 2 + c) * 128 + 128), identf.all(), vcol(l, "cdw", j * 2 + c), ALU.mult)
            for c in range(2):
                if pi == 0:
                    P.memset(xdp.s(c * W, c * W + PADD), 0.0, eng="dve")
                else:
                    P.copy(xdp.s(c * W, c * W + PADD), carryd.s(c * PADD, (c + 1) * PADD), eng="dve")
                    wload(xds.s(c * WS, c * WS + PADD), d_scd[l][:, c * PADD:(c + 1) * PADD])
            for (t0, n) in tiles:
                tl = t0 - T0
                samp = t0 >= SEQ
                for c in range(2):
                    pb = bank().s(0, n)
                    pc = bank().s(0, n)
                    ph = bank().s(0, n)
                    for ps_, off in ((pb, 0), (pc, 256), (ph, 512)):
                        P.mm(ps_, [(wdp.s(k * 768 + off + c * 128, k * 768 + off + c * 128 + 128), nT.s(k * TP + tl, k * TP + tl + n)) for k in range(KC)])
                    P.act(bgT.s(c * TP + tl, c * TP + tl + n), pb, AF.Copy)
                    cg = tf[c].s(0, n)
                    P.act(cg, pc, AF.Copy)
                    if samp:
                        dst = xds.s(c * WS + PADD, c * WS + PADD + n)
                        P.tt(cdst.s(c * 4 + 2, c * 4 + 4), Ref(ph.ap[:, n - 2:n], ph.atoms), tf[c].s(n - 2, n), ALU.mult)
                    else:
                        dst = xdp.s(c * W + PADD + tl, c * W + PADD + tl + n)
                        if t0 + n == SEQ:
                            P.tt(cdst.s(c * 4, c * 4 + 2), Ref(ph.ap[:, n - 2:n], ph.atoms), tf[c].s(n - 2, n), ALU.mult)
                    P.tt(dst, ph, cg, ALU.mult)
                for c in range(2):
                    ps = bank().s(0, n)
                    if samp:
                        P.mm(ps, [(diag.s((j * 2 + c) * 128, (j * 2 + c) * 128 + 128), xds.s(c * WS + j, c * WS + j + n)) for j in range(CDW)])
                    else:
                        P.mm(ps, [(diag.s((j * 2 + c) * 128, (j * 2 + c) * 128 + 128), xdp.s(c * W + tl + j, c * W + tl + j + n)) for j in range(CDW)])
                    P.tt(ydT.s(c * TP + tl, c * TP + tl + n), ps, bgT.s(c * TP + tl, c * TP + tl + n), ALU.mult)
            if pi == 0:
                for c in range(2):
                    P.copy(carryd.s(c * PADD, (c + 1) * PADD), xdp.s(c * W + PTN, c * W + PTN + PADD), eng="dve")
            else:
                P.dma("sp", o_cd[l], cdst.all())

        for l in range(n_layers):
            early = None
            if do_mixer and do_ffn is True:
                def early(sq_, rstd_, l=l):
                    (TA0, TPA, tilesA) = PASSES[0]
                    prenorm(l, "f2pre", tilesA, TA0, TPA, Buf(P, "A", 0, KC * TPA, BF16, "nTA_early"), sq_, rstd_)
            gen = mixer_layer(l, early) if do_mixer else None
            if do_ffn:
                ffn_pair(l, 0, next_pre=(lambda: next(gen)) if gen is not None else None)
            elif gen is not None:
                next(gen)
            if gen is not None:
                for _ in gen:
                    pass
            if do_ffn is True:
                ffn_pair(l, 1, skip_preA=(early is not None))
        for k in range(KC):
            P.dma("sp", o_y[:, k * NT:(k + 1) * NT], xT.s(k * NT, (k + 1) * NT))
        P.barrier_on("sp", [xT.all(), Buf(P, "A", 0, ASIZE, F32).all()])
        P.mark("end")
        P.emit()
        import json as _json
        if os.environ.get("MARKS"):
            _json.dump(P.marks, open(os.environ["MARKS"], "w"))
        print("arena peak", AR.peak, "of", ASIZE, {e: len(P.ops[e]) for e in ENGINES}, "dma sems", len(P.dma_sems))
    return nc


def _slab(W, ms):
    K, M = W.shape
    return np.ascontiguousarray(
        W.reshape(K // 128, 128, M // ms, ms).transpose(2, 1, 0, 3)).reshape(M // ms, 128, (K // 128) * ms)


def _rope_tables():
    half = RD // 2
    inv = np.exp(-math.log(10000.0) * np.arange(half, dtype=np.float32) / half).astype(np.float32)
    pos = np.arange(NT, dtype=np.float32)
    ang = (pos[:, None] * inv[None, :]).astype(np.float32)
    c, s = np.cos(ang).astype(np.float32), np.sin(ang).astype(np.float32)
    cosT = np.concatenate([c, c], axis=1).T
    sinS = np.concatenate([-s, s], axis=1).T
    return np.ascontiguousarray(cosT), np.ascontiguousarray(sinS)


def prep_shared(I):
    f = lambda a: np.asarray(a, dtype=np.float32)
    S = {}
    wgu = np.empty((L, 2, FC, 128, KC * 256), np.float32)
    wdn = np.empty((L, 2, 8, 128, FC * 128), np.float32)
    for l in range(L):
        for fi, (gu, dn) in enumerate((("ffn1_w_gu", "ffn1_w_down"), ("ffn2_w_gu", "ffn2_w_down"))):
            W = f(I[gu][l]).reshape(KC, 128, 2 * DFF)
            G = W[:, :, :DFF].reshape(KC, 128, FC, 128)
            U = W[:, :, DFF:].reshape(KC, 128, FC, 128)
            wgu[l, fi] = np.stack([G, U], axis=3).transpose(2, 1, 0, 3, 4).reshape(FC, 128, KC * 256)
            wdn[l, fi] = _slab(f(I[dn][l]), 128)
    S["wgu"], S["wdn"] = wgu, wdn
    win = f(I["w_in"])
    S["wcq"] = np.stack([_slab(win[l][:, 0:384], 384)[0] for l in range(L)])
    S["wckv"] = np.stack([_slab(win[l][:, 384:640], 256)[0] for l in range(L)])
    wkr = np.zeros((L, 2, 1024, 96), np.float32)
    wkr[:, 0, :, 64:96] = win[:, :, 640:672]
    wkr[:, 1, :, 64:80] = win[:, :, 656:672]
    wkr[:, 1, :, 80:96] = win[:, :, 640:656]
    S["wkr"] = np.stack([np.stack([_slab(wkr[l, i], 96)[0] for i in range(2)]) for l in range(L)])
    S["wglu"] = np.stack([_slab(win[l][:, 672:1184], 512)[0] for l in range(L)])
    S["wuv"] = np.stack([_slab(win[l][:, 1184:1696], 512)[0] for l in range(L)])
    S["wdp"] = np.stack([_slab(win[l][:, 1696:2464], 768)[0] for l in range(L)])
    wgm = np.empty((L, 8, 128, KC * 512 + 10 * 128), np.float32)
    for l in range(L):
        Wg = win[l][:, 2464:].reshape(KC, 128, 4, 8, 128).transpose(3, 1, 0, 2, 4).reshape(8, 128, KC * 512)
        Wbr = np.concatenate([f(I["w_br_a"][l]), f(I["w_br_b"][l]), f(I["w_br_c"][l]), f(I["w_br_d"][l])], axis=0)
        Wb = Wbr.reshape(10, 128, 8, 128).transpose(2, 1, 0, 3).reshape(8, 128, 10 * 128)
        wgm[l] = np.concatenate([Wg, Wb], axis=2)
    S["wgm"] = wgm
    S["wo"] = np.stack([_slab(f(I["w_o"][l]), 128) for l in range(L)])
    wuq = f(I["w_uq"])
    wuq_sw = wuq.reshape(L, QL, NH, 96).copy()
    wuq_sw[..., 64:80] = wuq.reshape(L, QL, NH, 96)[..., 80:96]
    wuq_sw[..., 80:96] = wuq.reshape(L, QL, NH, 96)[..., 64:80]
    wuq_sw = wuq_sw.reshape(L, QL, NH * 96)
    S["wuq"] = np.stack([np.stack([_slab(wuq[l], 768)[0], _slab(wuq_sw[l], 768)[0]]) for l in range(L)])
    wukv = f(I["w_ukv"]).reshape(L, 2, 128, NH, 128)
    S["wuk"] = np.ascontiguousarray(wukv[..., :64].transpose(0, 2, 1, 3, 4)).reshape(L, 128, 2 * NH * 64)
    S["wuv2"] = np.ascontiguousarray(wukv[..., 64:].transpose(0, 2, 1, 3, 4)).reshape(L, 128, 2 * 512)
    w3 = f(I["w_ukv"]).reshape(L, KVL, NH, 128)[..., :64]
    S["wukT"] = np.ascontiguousarray(w3.transpose(0, 3, 2, 1)).reshape(L, 64, NH * 256)
    S["wsT"] = np.ascontiguousarray(f(I["gmlp_w_s"]).transpose(0, 3, 1, 2)).reshape(L, 128, 512)
    S["bsrow"] = np.ascontiguousarray(f(I["gmlp_b_s"])).reshape(L, 1, 512)
    gvb = np.concatenate([f(I["gmlp_vn_g"]), f(I["gmlp_vn_b"])], axis=1)
    S["gvb"] = np.ascontiguousarray(np.broadcast_to(gvb[:, None, :], (L, 128, 512)))
    vecs = np.zeros((128, L * NV), np.float32)

    def put(l, name, v, w):
        v = f(v).reshape(w, 128).T
        vecs[:, l * NV + VOFF[name]: l * NV + VOFF[name] + w] = v
    for l in range(L):
        put(l, "f1pre", I["ffn1_norm_pre"][l], 8)
        put(l, "f1post", I["ffn1_norm_post"][l], 8)
        put(l, "mpre", I["mix_norm_pre"][l], 8)
        put(l, "mpost", I["mix_norm_post"][l], 8)
        put(l, "f2pre", I["ffn2_norm_pre"][l], 8)
        put(l, "f2post", I["ffn2_norm_post"][l], 8)
        put(l, "qn", I["q_norm"][l], 3)
        put(l, "kvn", I["kv_norm"][l], 2)
        put(l, "cbb", I["conv_b_bias"][l], 2)
        put(l, "cblg", I["conv_b_ln_g"][l], 2)
        put(l, "cblb", I["conv_b_ln_b"][l], 2)
        put(l, "cbw", I["conv_b_w"][l], CBW * 2)
        put(l, "cdw", I["conv_d_w"][l], CDW * 2)
        put(l, "vng", I["gmlp_vn_g"][l], 2)
        put(l, "vnb", I["gmlp_vn_b"][l], 2)
    S["vecs"] = vecs
    S["ident"] = np.eye(128, dtype=np.float32)
    S["cosT"], S["sinS"] = _rope_tables()
    return S


def prep_core(I, c):
    f = lambda a: np.asarray(a, dtype=np.float32)
    C = {}
    X = np.concatenate([f(I["x_prompt"][c]), f(I["x_sample"][c])], axis=0)
    C["xT"] = np.ascontiguousarray(X.T.reshape(KC, 128, NT).transpose(1, 0, 2)).reshape(128, KC * NT)
    cl = f(I["cache_kv_latent"][:, c])
    C["clatT"] = np.ascontiguousarray(cl.transpose(0, 2, 1).reshape(L, 2, 128, SEQ).transpose(0, 2, 1, 3)).reshape(L, 128, 2 * SEQ)
    C["clat"] = np.ascontiguousarray(cl.reshape(L, 16, 128, 256).transpose(0, 2, 1, 3)).reshape(L, 128, 16 * 256)
    C["ckrT"] = np.ascontiguousarray(f(I["cache_k_rope"][:, c]).transpose(0, 2, 1))
    sb = f(I["state_conv_b"][:, c])
    C["scbT"] = np.ascontiguousarray(sb.transpose(0, 2, 1).reshape(L, 2, 128, PADB).transpose(0, 2, 1, 3)).reshape(L, 128, 2 * PADB)
    sd = f(I["state_conv_d"][:, c])
    C["scdT"] = np.ascontiguousarray(sd.transpose(0, 2, 1).reshape(L, 2, 128, PADD).transpose(0, 2, 1, 3)).reshape(L, 128, 2 * PADD)
    return C


def assemble(results):
    n = len(results)
    yp = np.empty((n, SEQ, D), np.float32)
    ys = np.empty((n, DSEQ, D), np.float32)
    latp = np.empty((L, n, SEQ, KVL), np.float32)
    lats = np.empty((L, n, DSEQ, KVL), np.float32)
    krp = np.empty((L, n, SEQ, RD), np.float32)
    krs = np.empty((L, n, DSEQ, RD), np.float32)
    cbp = np.empty((L, n, PADB, 256), np.float32)
    cbs = np.empty((L, n, PADB, 256), np.float32)
    cdp = np.empty((L, n, PADD, 256), np.float32)
    cds = np.empty((L, n, PADD, 256), np.float32)
    gv = np.empty((L, n, DSEQ, 256), np.float32)
    for c, r in enumerate(results):
        y = np.asarray(r["yT"]).reshape(128, KC, NT).transpose(2, 1, 0).reshape(NT, D)
        yp[c], ys[c] = y[:SEQ], y[SEQ:]
        lat = np.asarray(r["latT"]).reshape(L, 128, 2, NT).transpose(0, 3, 2, 1).reshape(L, NT, KVL)
        latp[:, c], lats[:, c] = lat[:, :SEQ], lat[:, SEQ:]
        kr = np.asarray(r["krT"]).transpose(0, 2, 1)
        krp[:, c], krs[:, c] = kr[:, :SEQ], kr[:, SEQ:]
        cb = np.asarray(r["cbT"]).reshape(L, 128, 2, 2, PADB).transpose(0, 3, 4, 2, 1).reshape(L, 2, PADB, 256)
        cbp[:, c], cbs[:, c] = cb[:, 0], cb[:, 1]
        cd = np.asarray(r["cdT"]).reshape(L, 128, 2, 2, PADD).transpose(0, 3, 4, 2, 1).reshape(L, 2, PADD, 256)
        cdp[:, c], cds[:, c] = cd[:, 0], cd[:, 1]
        gv[:, c] = np.asarray(r["gvT"]).reshape(L, 128, 2, DSEQ).transpose(0, 3, 2, 1).reshape(L, DSEQ, 256)
    return (yp, ys, latp, krp, cbp, cdp, lats, krs, cbs, gv, cds)


_NC_CACHE = {}


def kernel(**inputs):
    if "nc" not in _NC_CACHE:
        _NC_CACHE["nc"] = build_program()
    nc = _NC_CACHE["nc"]
    S = prep_shared(inputs)
    in_maps = []
    for c in range(8):
        m = dict(S)
        m.update(prep_core(inputs, c))
        in_maps.append(m)
    res = run_bass_kernel_spmd(nc, in_maps, core_ids=list(range(8)))
    return assemble(res.results)
```

```python
import contextlib
import math
import numpy as np
import concourse.bass as bass
import concourse.mybir as mybir
from concourse.bass_utils import run_bass_kernel_spmd

F32 = mybir.dt.float32
BF16 = mybir.dt.bfloat16
AF = mybir.ActivationFunctionType
ALU = mybir.AluOpType

ATOM = 8


class Ref:
    __slots__ = ("ap", "atoms")

    def __init__(self, ap, atoms):
        self.ap = ap
        self.atoms = atoms


class Buf:
    def __init__(self, prog, space, base32, ncols, dt, name=""):
        self.prog = prog
        self.space = space
        self.base32 = base32
        self.ncols = ncols
        self.dt = dt
        self.name = name
        self.esz = 2 if dt == BF16 else 4
        n32 = (ncols * self.esz + 3) // 4
        self.n32 = n32
        t = prog.tensors[space]
        ap = t[:, base32:base32 + n32]
        if dt != F32:
            ap = ap.bitcast(dt)
        self.full = ap

    def s(self, c0, c1, p0=0, p1=128, step=None):
        assert 0 <= c0 < c1 <= self.ncols, (self.name, c0, c1, self.ncols)
        if step is None:
            ap = self.full[p0:p1, c0:c1]
        else:
            ap = self.full[p0:p1, c0:c1:step]
        asz = (512 if self.space == "PS" else ATOM) * 4
        a0 = (self.base32 * 4 + c0 * self.esz) // asz
        a1 = (self.base32 * 4 + c1 * self.esz - 1) // asz
        return Ref(ap, [(self.space, a) for a in range(a0, a1 + 1)])

    def all(self):
        return self.s(0, self.ncols)


class Op:
    __slots__ = ("eng", "fn", "waits", "need_sig", "sigval", "dma_sem", "dma_val", "eidx")

    def __init__(self, eng, fn):
        self.eng = eng
        self.fn = fn
        self.waits = []
        self.need_sig = False
        self.sigval = None
        self.dma_sem = None
        self.dma_val = None
        self.eidx = None


ENGINES = ("pe", "act", "dve", "pool", "sp")


class Prog:
    def __init__(self, nc):
        self.nc = nc
        self.tensors = {}
        self.ops = {e: [] for e in ENGINES}
        self.last_w = {}
        self.readers = {}
        self.seen = {e: {} for e in ENGINES}
        self.dma_sems = {}
        self.n_dma_sems = 0
        self.rr = 0
        self.n_mm = 0
        self.marks = []

    def _add_wait(self, op, dep, raw):
        if dep is op:
            return
        e = op.eng
        if dep.dma_sem is not None:
            key = ("dma", dep.dma_sem)
            if self.seen[e].get(key, 0) >= dep.dma_val:
                return
            self.seen[e][key] = dep.dma_val
            op.waits.append(("dma", dep.dma_sem, dep.dma_val))
            return
        if dep.eng == e and op.dma_sem is None:
            if e == "pe":
                return
        key = ("eng", dep.eng)
        if self.seen[e].get(key, -1) >= dep.eidx:
            return
        self.seen[e][key] = dep.eidx
        dep.need_sig = True
        op.waits.append(("op", dep))

    def record(self, eng, fn, reads=(), writes=(), dma_key=None):
        op = Op(eng, fn)
        op.eidx = len(self.ops[eng])
        if dma_key is not None:
            v = self.dma_sems.get(dma_key, 0) + 16
            self.dma_sems[dma_key] = v
            op.dma_sem = dma_key
            op.dma_val = v
        for r in reads:
            for a in r.atoms:
                w = self.last_w.get(a)
                if w is not None:
                    self._add_wait(op, w, True)
        for r in writes:
            for a in r.atoms:
                w = self.last_w.get(a)
                if w is not None:
                    self._add_wait(op, w, False)
                for rd in self.readers.get(a, ()):
                    self._add_wait(op, rd, False)
        for r in reads:
            for a in r.atoms:
                lst = self.readers.setdefault(a, [])
                if not lst or lst[-1] is not op:
                    lst.append(op)
        for r in writes:
            for a in r.atoms:
                self.last_w[a] = op
                self.readers[a] = []
        self.ops[eng].append(op)
        return op

    def mm(self, out, pairs, start=True, stop=True):
        reads = []
        for l, r in pairs:
            reads.append(l)
            reads.append(r)
        n = len(pairs)
        self.n_mm += n

        def fn(eng, out=out, pairs=pairs):
            ins = None
            for i, (l, r) in enumerate(pairs):
                ins = eng.matmul(out.ap, l.ap, r.ap, start=(start and i == 0),
                                 stop=(stop and i == n - 1))
            return ins
        return self.record("pe", fn, reads, [out])

    def act(self, out, in_, func, scale=1.0, bias=0.0, accum=None, eng="act"):
        reads = [in_]
        sc = scale.ap if isinstance(scale, Ref) else scale
        bi = bias.ap if isinstance(bias, Ref) else bias
        if isinstance(scale, Ref):
            reads.append(scale)
        if isinstance(bias, Ref):
            reads.append(bias)
        writes = [out]
        if accum is not None:
            writes.append(accum)

        def fn(eng, out=out, in_=in_):
            kw = {}
            if accum is not None:
                kw["accum_out"] = accum.ap
            return eng.activation(out.ap, in_.ap, func, bias=bi, scale=sc, **kw)
        return self.record("act", fn, reads, writes)

    def tt(self, out, a, b, op, eng="dve"):
        def fn(e, out=out, a=a, b=b):
            return e.tensor_tensor(out.ap, a.ap, b.ap, op)
        return self.record(eng, fn, [a, b], [out])

    def stt(self, out, in0, scalar, in1, op0, op1, eng="dve"):
        reads = [in0, in1]
        sc = scalar.ap if isinstance(scalar, Ref) else scalar
        if isinstance(scalar, Ref):
            reads.append(scalar)

        def fn(e, out=out, in0=in0, in1=in1):
            return e.scalar_tensor_tensor(out.ap, in0.ap, sc, in1.ap, op0, op1)
        return self.record(eng, fn, reads, [out])

    def ts(self, out, in0, s1, op0, s2=None, op1=None, eng="dve"):
        reads = [in0]
        a1 = s1.ap if isinstance(s1, Ref) else s1
        a2 = s2.ap if isinstance(s2, Ref) else s2
        if isinstance(s1, Ref):
            reads.append(s1)
        if isinstance(s2, Ref):
            reads.append(s2)

        def fn(e, out=out, in0=in0):
            if op1 is None:
                return e.tensor_scalar(out.ap, in0.ap, a1, None, op0)
            return e.tensor_scalar(out.ap, in0.ap, a1, a2, op0, op1)
        return self.record(eng, fn, reads, [out])

    def copy(self, out, in_, eng="dve"):
        def fn(e, out=out, in_=in_):
            return e.tensor_copy(out.ap, in_.ap)
        return self.record(eng, fn, [in_], [out])

    def memset(self, out, val, eng="pool"):
        def fn(e, out=out):
            return e.memset(out.ap, val)
        return self.record(eng, fn, [], [out])

    def dma(self, queue, out, in_, key=None):
        reads = [in_] if isinstance(in_, Ref) else []
        writes = [out] if isinstance(out, Ref) else []
        oap = out.ap if isinstance(out, Ref) else out
        iap = in_.ap if isinstance(in_, Ref) else in_
        if key is None:
            key = ("auto", self.rr % 24)
            self.rr += 1

        def fn(e):
            return e.dma_start(out=oap, in_=iap)
        prev = self.dma_sems.get(key, 0)
        op = self.record(queue, fn, reads, writes, dma_key=key)
        if prev > 0 and self.seen[queue].get(("dma", key), 0) < prev:
            self.seen[queue][("dma", key)] = prev
            op.waits.append(("dma", key, prev))
        return op

    def mark(self, label):
        self.marks.append((label, self.n_mm))

    def barrier_on(self, eng, refs):
        return self.record(eng, None, refs, refs, dma_key=None)

    def emit(self):
        nc = self.nc
        with contextlib.ExitStack() as st:
            esem = {e: st.enter_context(nc.semaphore("sem_" + e)) for e in ENGINES}
            dsem = {}
            for k in self.dma_sems:
                dsem[k] = st.enter_context(nc.semaphore("dsem%d" % len(dsem)))
            for e in ENGINES:
                c = 0
                for op in self.ops[e]:
                    if op.need_sig:
                        c += 1
                        op.sigval = c
            block = st.enter_context(nc.Block())
            bmap = {"pe": block.tensor, "act": block.scalar, "dve": block.vector,
                    "pool": block.gpsimd, "sp": block.sync}

            def make(ename):
                def body(eng):
                    for op in self.ops[ename]:
                        for w in op.waits:
                            if w[0] == "dma":
                                eng.wait_ge(dsem[w[1]], w[2])
                            else:
                                d = w[1]
                                eng.wait_ge(esem[d.eng], d.sigval)
                        if op.fn is None:
                            continue
                        ins = op.fn(eng)
                        if op.dma_sem is not None:
                            ins.then_inc(dsem[op.dma_sem], 16)
                        elif op.need_sig:
                            ins.then_inc(esem[ename], 1)
                return body
            for e in ENGINES:
                if self.ops[e]:
                    bmap[e](make(e))


D = 1024
KC = 8
SEQ = 2048
DSEQ = 16
NT = SEQ + DSEQ
L = 4
DFF = 2816
FC = DFF // 128
QL, KVL, RD = 384, 256, 32
NH = 8
EPS = 1e-6
SM_SCALE = (64 + 32) ** -0.5
CBW, CDW = 31, 3
PADB, PADD = CBW - 1, CDW - 1

VOFF = {}
_o = 0
for _n, _w in (("f1pre", 8), ("f1post", 8), ("mpre", 8), ("mpost", 8), ("f2pre", 8), ("f2post", 8),
               ("qn", 3), ("kvn", 2), ("cbb", 2), ("cblg", 2), ("cblb", 2), ("cbw", CBW * 2), ("cdw", CDW * 2),
               ("vng", 2), ("vnb", 2)):
    VOFF[_n] = _o
    _o += _w
NV = _o

PASSES = (
    (0, 1024, ((0, 512), (512, 512))),
    (1024, 1040, ((1024, 512), (1536, 512), (2048, 16))),
)


class Arena:
    def __init__(self, prog, space, size):
        self.prog, self.space, self.size = prog, space, size
        self.top = 0
        self.peak = 0

    def alloc(self, ncols, dt, name=""):
        esz = 2 if dt == BF16 else 4
        n32 = (ncols * esz + 3) // 4
        n32 = (n32 + 127) // 128 * 128
        assert self.top + n32 <= self.size, ("arena overflow", name, self.top, n32, self.size)
        b = Buf(self.prog, self.space, self.top, ncols, dt, name)
        self.top += n32
        self.peak = max(self.peak, self.top)
        return b

    def at(self, base32, ncols, dt, name=""):
        return Buf(self.prog, self.space, base32, ncols, dt, name)


import os


def build_program(n_layers=L, do_mixer=True, do_ffn=True, stop=99):
    nc = bass.Bass("TRN2", target_bir_lowering=False)

    def din(name, shape):
        return nc.dram_tensor(name, list(shape), F32, kind="ExternalInput").ap()

    def dout(name, shape):
        return nc.dram_tensor(name, list(shape), F32, kind="ExternalOutput").ap()

    d_x = din("xT", (128, KC * NT))
    d_vecs = din("vecs", (128, L * NV))
    d_ident = din("ident", (128, 128))
    d_cos = din("cosT", (32, NT))
    d_sin = din("sinS", (32, NT))
    d_wgu = din("wgu", (L, 2, FC, 128, KC * 256))
    d_wdn = din("wdn", (L, 2, 8, 128, FC * 128))
    d_wcq = din("wcq", (L, 128, KC * QL))
    d_wckv = din("wckv", (L, 128, KC * KVL))
    d_wkr = din("wkr", (L, 2, 128, KC * 96))
    d_wglu = din("wglu", (L, 128, KC * 512))
    d_wuv = din("wuv", (L, 128, KC * 512))
    d_wdp = din("wdp", (L, 128, KC * 768))
    d_wgm = din("wgm", (L, 8, 128, KC * 512 + 10 * 128))
    d_wo = din("wo", (L, 8, 128, KC * 128))
    d_wuq = din("wuq", (L, 2, 128, 3 * 768))
    d_wuk = din("wuk", (L, 128, 2 * NH * 64))
    d_wuv2 = din("wuv2", (L, 128, 2 * 512))
    d_wukT = din("wukT", (L, 64, NH * 256))
    d_wsT = din("wsT", (L, 128, 512))
    d_bs = din("bsrow", (L, 1, 512))
    d_gvb = din("gvb", (L, 128, 512))
    d_clatT = din("clatT", (L, 128, 2 * SEQ))
    d_clat = din("clat", (L, 128, 16 * 256))
    d_ckr = din("ckrT", (L, 32, SEQ))
    d_scb = din("scbT", (L, 128, 2 * PADB))
    d_scd = din("scdT", (L, 128, 2 * PADD))

    o_y = dout("yT", (128, KC * NT))
    o_lat = dout("latT", (L, 128, 2 * NT))
    o_kr = dout("krT", (L, 32, NT))
    o_cb = dout("cbT", (L, 128, 2 * 2 * PADB))
    o_cd = dout("cdT", (L, 128, 2 * 2 * PADD))
    o_gv = dout("gvT", (L, 128, 2 * DSEQ))

    with contextlib.ExitStack() as st:
        XCOLS = KC * NT
        CCOLS = L * NV + 128 + 64 + 64 + 2 * ATOM + 8
        CCOLS = (CCOLS + 127) // 128 * 128
        ASIZE = 35456
        t_x = st.enter_context(nc.sbuf_tensor("xres", [128, XCOLS], F32))
        t_c = st.enter_context(nc.sbuf_tensor("consts", [128, CCOLS], F32))
        t_a = st.enter_context(nc.sbuf_tensor("arena", [128, ASIZE], F32))
        t_p = st.enter_context(nc.psum_tensor("psum", [128, 8 * 512], F32))
        P = Prog(nc)
        P.tensors.update({"X": t_x, "C": t_c, "A": t_a, "PS": t_p})
        xT = Buf(P, "X", 0, XCOLS, F32, "xT")
        vecs = Buf(P, "C", 0, L * NV, F32, "vecs")
        identf = Buf(P, "C", L * NV, 128, F32, "identf")
        onesb = Buf(P, "C", L * NV + 128, 128, BF16, "ones")
        identb = Buf(P, "C", L * NV + 192, 128, BF16, "identb")
        banks = [Buf(P, "PS", i * 512, 512, F32, "bank%d" % i) for i in range(8)]
        AR = Arena(P, "A", ASIZE)
        state = {"bank": 0}

        def bank(lo=0, hi=8):
            b = banks[lo + state["bank"] % (hi - lo)]
            state["bank"] += 1
            return b

        def vcol(l, name, i=0):
            c = l * NV + VOFF[name] + i
            return vecs.s(c, c + 1)

        def xs(k, t0, n):
            return xT.s(k * NT + t0, k * NT + t0 + n)

        for k in range(KC):
            P.dma("sp", xT.s(k * NT, (k + 1) * NT), d_x[:, k * NT:(k + 1) * NT])
        P.dma("sp", vecs.all(), d_vecs)
        P.dma("sp", identf.all(), d_ident)
        epsb = {1.0: Buf(P, "C", L * NV + 256, 1, F32, "eps1"), 4.0: Buf(P, "C", L * NV + 256 + ATOM, 1, F32, "eps4")}
        for f2_, b_ in epsb.items():
            P.memset(b_.all(), EPS * f2_, eng="dve")
        epsb = {k_: v_.all() for k_, v_ in epsb.items()}
        P.memset(onesb.all(), 1.0, eng="dve")
        P.copy(identb.all(), identf.all(), eng="dve")

        wq = {"i": 0}

        def wload(dst, src):
            P.dma("pool", dst, src, key=("w", wq["i"] % 6))
            wq["i"] += 1

        def recip(out, in_):
            P.record("dve", (lambda e, o=out, i=in_: e.reciprocal(o.ap, i.ap)), [in_], [out])

        def rstd_from(srcs, n, Dn, out, sq, post=1.0):
            sqs = []
            for i, r in enumerate(srcs):
                q = sq.s(i * 512, i * 512 + n)
                P.act(q, r, AF.Square)
                sqs.append(q)
            ps = bank().s(0, n)
            P.mm(ps, [(onesb.all(), q) for q in sqs])
            f2 = 1.0 / (post * post)
            P.act(out, ps, AF.Ln, scale=f2 / Dn, bias=epsb[f2])
            P.act(out, out, AF.Exp, scale=-0.5)

        def prenorm(l, gname, tiles, T0, TP, nT, sq, rstd):
            for (t0, n) in tiles:
                tl = t0 - T0
                r = rstd.s(0, n)
                rstd_from([xs(k, t0, n) for k in range(KC)], n, D, r, sq)
                for k in range(KC):
                    P.stt(nT.s(k * TP + tl, k * TP + tl + n), xs(k, t0, n), vcol(l, gname, k), r,
                          ALU.mult, ALU.mult)

        def postnorm_residual(l, gname, tiles, T0, TP, ytmp, sq, rstd, half):
            for (t0, n) in tiles:
                tl = t0 - T0
                r = rstd.s(0, n)
                ys = [ytmp.s(k * TP + tl, k * TP + tl + n) for k in range(KC)]
                rstd_from(ys, n, D, r, sq, post=0.5 if half else 1.0)
                for k in range(KC):
                    P.stt(ys[k], ys[k], vcol(l, gname, k), r, ALU.mult, ALU.mult)
                    P.tt(xs(k, t0, n), xs(k, t0, n), ys[k], ALU.add)

        def ffn_pair(l, f, next_pre=None, skip_preA=False, nTA_alt=False, chain_next=None):
            pre, post = ("f1pre", "f1post") if f == 0 else ("f2pre", "f2post")
            (TA0, TPA, tilesA), (TB0, TPB, tilesB) = PASSES
            AR.top = 0
            ytmp = AR.alloc(KC * TPB, F32, "ytmp")
            nTA = AR.at(ytmp.base32, KC * TPA, BF16, "nTA")
            wgu = [AR.alloc(KC * 256, BF16, "wgu%d" % i) for i in range(3)]
            assert AR.top <= 12288
            nTB = AR.alloc(KC * TPB, BF16, "nTB")
            nTA2 = AR.at(nTB.base32, KC * TPA, BF16, "nTA2")
            if nTA_alt:
                nTA = nTA2
            hT = AR.alloc(FC * TPB, BF16, "hT")
            wdn = [AR.alloc(FC * 128, BF16, "wdn%d" % i) for i in range(3)]
            sq = AR.alloc(KC * 512, BF16, "sq")
            rstd = AR.alloc(512, F32, "rstd")
            sg = [AR.alloc(512, F32, "sg%d" % i) for i in range(2)]
            cnt = {"s": 0}

            def up(T0, TP, tiles, nT, hook=None):
                for c in range(FC):
                    w = wgu[c % 3]
                    wload(w.all(), d_wgu[l, f, c])
                    for (t0, n) in tiles:
                        tl = t0 - T0
                        pg = bank().s(0, n)
                        pu = bank().s(0, n)
                        P.mm(pg, [(w.s(k * 256, k * 256 + 128), nT.s(k * TP + tl, k * TP + tl + n)) for k in range(KC)])
                        P.mm(pu, [(w.s(k * 256 + 128, k * 256 + 256), nT.s(k * TP + tl, k * TP + tl + n)) for k in range(KC)])
                        s_ = sg[cnt["s"] % 2].s(0, n)
                        cnt["s"] += 1
                        P.act(s_, pg, AF.Silu)
                        P.tt(hT.s(c * TP + tl, c * TP + tl + n), pu, s_, ALU.mult)
                    if hook is not None:
                        hook(c)

            def down(T0, TP, tiles):
                for m in range(8):
                    w = wdn[m % 3]
                    wload(w.all(), d_wdn[l, f, m])
                    for (t0, n) in tiles:
                        tl = t0 - T0
                        py = bank().s(0, n)
                        P.mm(py, [(w.s(k * 128, k * 128 + 128), hT.s(k * TP + tl, k * TP + tl + n)) for k in range(FC)])
                        P.act(ytmp.s(m * TP + tl, m * TP + tl + n), py, AF.Copy)

            def hookA(c):
                if c % 4 == 2 and c // 4 < len(tilesA):
                    postnorm_residual(l, post, [tilesA[c // 4]], TA0, TPA, ytmp, sq, rstd, True)

            P.mark("L%d ffn%d upA" % (l, f))
            if not skip_preA:
                prenorm(l, pre, tilesA, TA0, TPA, nTA, sq, rstd)
            up(TA0, TPA, tilesA, nTA)
            P.mark("L%d ffn%d downA" % (l, f))
            prenorm(l, pre, tilesB, TB0, TPB, nTB, sq, rstd)
            down(TA0, TPA, tilesA)
            P.mark("L%d ffn%d upB" % (l, f))
            up(TB0, TPB, tilesB, nTB, hook=hookA)
            P.mark("L%d ffn%d downB" % (l, f))
            down(TB0, TPB, tilesB)
            P.mark("L%d ffn%d end" % (l, f))
            if next_pre is not None:
                next_pre()
            if chain_next is not None:
                prenorm(chain_next, "f1pre", tilesA, TA0, TPA, nTA2, sq, rstd)
            postnorm_residual(l, post, tilesB, TB0, TPB, ytmp, sq, rstd, True)

        def mixer_layer(l, early_f2=None):
            AR.top = 0
            kT = AR.alloc(NH * SEQ, BF16, "kT")
            Vt = AR.alloc(16 * 512, BF16, "V")
            carryb = AR.alloc(2 * PADB, BF16, "carryb")
            carryd = AR.alloc(2 * PADD, BF16, "carryd")
            base_top = AR.top
            for pi, (T0, TP, tiles) in enumerate(PASSES):
                AR.top = base_top
                for _ in mixer_pass(l, pi, T0, TP, tiles, kT, Vt, carryb, carryd, early_f2):
                    if pi == 0:
                        yield

        def mixer_pass(l, pi, T0, TP, tiles, kT, Vt, carryb, carryd, early_f2=None):
            isB = pi == 1
            ptiles = [t for t in tiles if t[0] < SEQ]
            stile = (SEQ, DSEQ) if isB else None
            PT0 = T0
            PTN = sum(n for (_, n) in ptiles)
            ybase = AR.top
            nT = AR.alloc(KC * TP, BF16, "nT")
            aT = AR.alloc(4 * TP, BF16, "aT")
            ybT = AR.alloc(2 * TP, BF16, "ybT")
            ycT = AR.alloc(2 * TP, BF16, "ycT")
            ydT = AR.alloc(2 * TP, BF16, "ydT")
            need = (KC * TP * 4 - (AR.top - ybase) * 4)
            if need > 0:
                AR.alloc(need // 4, F32, "ypad")
            ytmp = AR.at(ybase, KC * TP, F32, "ytmp")
            sq = AR.alloc(KC * 512, BF16, "sq")
            rstd = AR.alloc(512, F32, "rstd")
            tf = [AR.alloc(512, F32, "tf%d" % i) for i in range(4)]
            tabc = AR.alloc(512, F32, "tabc")
            tabs = AR.alloc(512, F32, "tabs")
            stage_top = AR.top

            def ntk(k, tl, n):
                return nT.s(k * TP + tl, k * TP + tl + n)

            prenorm(l, "mpre", tiles, T0, TP, nT, sq, rstd)
            yield
            P.mark("L%d mix%d st2" % (l, pi))

            cqn = AR.alloc(3 * TP, BF16, "cqn")
            att_top = AR.top
            latb = AR.alloc(2 * TP, BF16, "latb")
            krs = AR.alloc(DSEQ, BF16, "krs")
            wsl = AR.alloc(2 * KC * 96, BF16, "wslot")
            wsl2 = AR.alloc(KC * KVL, BF16, "wslot2")
            latf = AR.alloc(2 * 512, F32, "latf")
            krf = AR.alloc(512, F32, "krf")
            cqf = AR.at(tf[1].base32, 3 * 512, F32, "cqf")
            r = None
            wUK = AR.at(wsl2.base32, 2 * NH * 64, BF16, "wUK")
            wUV = AR.at(wsl2.base32 + 512, 2 * 512, BF16, "wUV")

            def load_cd():
                wload(wsl.s(0, KC * 96), d_wkr[l, 0])
                wload(wsl.s(KC * 96, 2 * KC * 96), d_wkr[l, 1])
                wload(wUK.all(), d_wuk[l])
                wload(wUV.all(), d_wuv2[l])
            assert wsl2.base32 == wsl.base32 + 768
            if pi == 0:
                wslA = AR.at(Vt.base32 + 128, KC * QL, BF16, "wslA")
                wslB = AR.at(Vt.base32 + 128 + 1536, KC * KVL, BF16, "wslB")
            else:
                wslA = AR.at(wsl.base32, KC * QL, BF16, "wslA")
                wslB = AR.at(wsl.base32, KC * KVL, BF16, "wslB")
            wload(wslA.all(), d_wcq[l])
            if pi == 0:
                wload(wslB.all(), d_wckv[l])
                load_cd()
            for (t0, n) in tiles:
                tl = t0 - T0
                for mc in range(3):
                    ps = bank().s(0, n)
                    P.mm(ps, [(wslA.s(k * QL + mc * 128, k * QL + mc * 128 + 128), ntk(k, tl, n)) for k in range(KC)])
                    P.act(cqf.s(mc * 512, mc * 512 + n), ps, AF.Copy)
                r = rstd.s(0, n)
                rstd_from([cqf.s(mc * 512, mc * 512 + n) for mc in range(3)], n, QL, r, sq)
                for mc in range(3):
                    P.stt(cqn.s(mc * TP + tl, mc * TP + tl + n), cqf.s(mc * 512, mc * 512 + n),
                          vcol(l, "qn", mc), r, ALU.mult, ALU.mult)
            if pi == 1:
                wload(wslB.all(), d_wckv[l])
            for (t0, n) in tiles:
                tl = t0 - T0
                r = rstd.s(0, n)
                for mc in range(2):
                    ps = bank().s(0, n)
                    P.mm(ps, [(wslB.s(k * KVL + mc * 128, k * KVL + mc * 128 + 128), ntk(k, tl, n)) for k in range(KC)])
                    P.act(latf.s(mc * 512, mc * 512 + n), ps, AF.Copy)
                rstd_from([latf.s(mc * 512, mc * 512 + n) for mc in range(2)], n, KVL, r, sq)
                for mc in range(2):
                    P.stt(latf.s(mc * 512, mc * 512 + n), latf.s(mc * 512, mc * 512 + n),
                          vcol(l, "kvn", mc), r, ALU.mult, ALU.mult)
                    P.act(latb.s(mc * TP + tl, mc * TP + tl + n), latf.s(mc * 512, mc * 512 + n), AF.Copy)
                    P.dma("sp", d_lat_out(l, mc, t0, n), latf.s(mc * 512, mc * 512 + n))
            if pi == 1:
                load_cd()
            for (t0, n) in tiles:
                tl = t0 - T0
                samp = t0 >= SEQ
                P.dma("sp", tabc.s(0, n, 64, 96), d_cos[:, t0:t0 + n])
                P.dma("sp", tabs.s(0, n, 64, 96), d_sin[:, t0:t0 + n])
                pa = bank().s(0, n, 0, 96)
                pb = bank().s(0, n, 0, 96)
                P.mm(pa, [(wsl.s(k * 96, k * 96 + 96), ntk(k, tl, n)) for k in range(KC)])
                P.mm(pb, [(wsl.s(KC * 96 + k * 96, KC * 96 + k * 96 + 96), ntk(k, tl, n)) for k in range(KC)])
                t1 = tf[0].s(0, n, 64, 96)
                P.tt(t1, bank_rows(pa, 64, 96), tabc.s(0, n, 64, 96), ALU.mult)
                P.tt(krf.s(0, n, 64, 96), bank_rows(pb, 64, 96), tabs.s(0, n, 64, 96), ALU.mult)
                P.tt(krf.s(0, n, 64, 96), krf.s(0, n, 64, 96), t1, ALU.add)
                P.dma("sp", o_kr[l, :, t0:t0 + n], krf.s(0, n, 64, 96))
                if samp:
                    P.copy(krs.s(0, n, 64, 96), krf.s(0, n, 64, 96), eng="dve")
                else:
                    for h in range(NH):
                        dst = kT.s(h * SEQ + t0, h * SEQ + t0 + n, 64, 96)
                        if h % 2 == 0:
                            P.act(dst, krf.s(0, n, 64, 96), AF.Copy)
                        else:
                            P.copy(dst, krf.s(0, n, 64, 96), eng="dve")
            for (t0, n) in ptiles:
                tl = t0 - T0
                for h in range(NH):
                    ps = bank().s(0, n, 0, 64)
                    P.mm(ps, [(wUK.s((k * NH + h) * 64, (k * NH + h) * 64 + 64), latb.s(k * TP + tl, k * TP + tl + n))
                              for k in range(2)])
                    dst = kT.s(h * SEQ + t0, h * SEQ + t0 + n, 0, 64)
                    if h % 2 == 0:
                        P.copy(dst, ps, eng="dve")
                    else:
                        P.act(dst, ps, AF.Copy)
                for bi in range(n // 128):
                    b = (t0 + bi * 128) // 128
                    ps = bank().all()
                    P.mm(ps, [(latb.s(k * TP + tl + bi * 128, k * TP + tl + bi * 128 + 128), wUV.s(k * 512, k * 512 + 512))
                              for k in range(2)])
                    if bi % 2 == 0:
                        P.act(Vt.s(b * 512, b * 512 + 512), ps, AF.Copy)
                    else:
                        P.copy(Vt.s(b * 512, b * 512 + 512), ps, eng="dve")

            if stop <= 2:
                return
            P.mark("L%d mix%d att" % (l, pi))
            AR.top = att_top
            latb2 = AR.alloc(2 * TP, BF16, "latb")
            krs2 = AR.alloc(DSEQ, BF16, "krs")
            assert latb2.base32 == latb.base32 and krs2.base32 == krs.base32
            wUQ = [AR.alloc(3 * 768, BF16, "wUQ%d" % i) for i in range(2)]
            qT = AR.alloc(NH * 512, BF16, "qT")
            Pt = [AR.at(sq.base32 + i * 256, 512, BF16, "Pt%d" % i) for i in range(4)]
            rden = [tf[2], tf[3]]
            wload(wUQ[0].all(), d_wuq[l, 0])
            wload(wUQ[1].all(), d_wuq[l, 1])

            def load_tabs(t0, n):
                P.dma("sp", tabc.s(0, n, 64, 96), d_cos[:, t0:t0 + n])
                P.dma("sp", tabs.s(0, n, 64, 96), d_sin[:, t0:t0 + n])

            def q_head(t0, n, tl, qdst, qstride, h):
                pa = bank(0, 4).s(0, n, 0, 96)
                pb = bank(0, 4).s(0, n, 0, 96)
                P.mm(pa, [(wUQ[0].s(k * 768 + h * 96, k * 768 + h * 96 + 96), cqn.s(k * TP + tl, k * TP + tl + n))
                          for k in range(3)])
                P.mm(pb, [(wUQ[1].s(k * 768 + h * 96, k * 768 + h * 96 + 96), cqn.s(k * TP + tl, k * TP + tl + n))
                          for k in range(3)])
                P.copy(qdst.s(h * qstride, h * qstride + n, 0, 64), bank_rows(pa, 0, 64), eng="dve")
                t1 = tf[0].s(0, n, 64, 96)
                t2 = tf[1].s(0, n, 64, 96)
                P.tt(t1, bank_rows(pa, 64, 96), tabc.s(0, n, 64, 96), ALU.mult)
                P.tt(t2, bank_rows(pb, 64, 96), tabs.s(0, n, 64, 96), ALU.mult)
                P.tt(qdst.s(h * qstride, h * qstride + n, 64, 96), t1, t2, ALU.add, eng="pool")

            qs = AR.alloc(NH * DSEQ, BF16, "qs") if isB else None
            pcount = {"i": 0}
            for ti, (t0, n) in enumerate(ptiles):
                tl = t0 - T0
                gi = t0 // 512
                if ti == 0:
                    load_tabs(t0, n)
                    for h in range(NH):
                        q_head(t0, n, tl, qT, 512, h)
                if ti + 1 < len(ptiles):
                    nxt = (ptiles[ti + 1][0], ptiles[ti + 1][1], ptiles[ti + 1][0] - T0, qT, 512)
                elif isB:
                    nxt = (stile[0], stile[1], stile[0] - T0, qs, DSEQ)
                else:
                    nxt = None
                if nxt is not None:
                    load_tabs(nxt[0], nxt[1])
                nblk = 4 * gi + 4
                items = [(h, j) for h in range(NH) for j in range(nblk)]
                pts = {}

                def issue_scores(idx):
                    h, j = items[idx]
                    m = j - 4 * gi
                    c0 = 0 if m < 0 else 128 * m
                    ps = bank(0, 4).s(c0, 512)
                    P.mm(ps, [(kT.s(h * SEQ + j * 128, h * SEQ + j * 128 + 128, 0, 96),
                               qT.s(h * 512 + c0, h * 512 + 512, 0, 96))])
                    pt = Pt[pcount["i"] % 4]
                    pcount["i"] += 1
                    P.act(pt.s(c0, 512), ps, AF.Exp, scale=SM_SCALE)
                    if m >= 0:
                        P.memset(pt.s(c0, c0 + 64, 64, 128), 0.0, eng="pool" if gi == 0 else "dve")
                    pts[idx] = (pt, c0)

                def issue_pv(idx):
                    h, j = items[idx]
                    hp, half = h // 2, h % 2
                    acc = banks[4 + (h % 2) * 2]
                    den = banks[5 + (h % 2) * 2]
                    pt, c0 = pts.pop(idx)
                    P.mm(acc.s(c0, 512), [(Vt.s(j * 512 + hp * 128, j * 512 + hp * 128 + 128), pt.s(c0, 512))],
                         start=(j == 0), stop=(j == nblk - 1))
                    P.mm(den.s(c0, 512), [(onesb.all(), pt.s(c0, 512))],
                         start=(j == 0), stop=(j == nblk - 1))
                    if j == nblk - 1:
                        def fin(h=h, hp=hp, half=half, acc=acc, den=den):
                            r0, r1 = half * 64, half * 64 + 64
                            rd = rden[h % 2]
                            if nblk <= 8:
                                P.act(rd.s(0, 512, r0, r1), den.s(0, 512, r0, r1), AF.Ln)
                                P.act(rd.s(0, 512, r0, r1), rd.s(0, 512, r0, r1), AF.Exp, scale=-1.0)
                            else:
                                recip(rd.s(0, 512, r0, r1), den.s(0, 512, r0, r1))
                            P.tt(aT.s(hp * TP + tl, hp * TP + tl + n, r0, r1), acc.s(0, 512, r0, r1), rd.s(0, 512, r0, r1), ALU.mult)
                        pend.append(fin)
                        if nxt is not None:
                            q_head(nxt[0], nxt[1], nxt[2], nxt[3], nxt[4], h)

                LA = 3
                pend = []
                for idx in range(len(items) + LA):
                    if idx < len(items):
                        issue_scores(idx)
                        if pend:
                            pend.pop(0)()
                    if idx >= LA:
                        issue_pv(idx - LA)
                while pend:
                    pend.pop(0)()

            if stop <= 3:
                return
            if isB:
                t0, n = stile
                tl = t0 - T0
                mark = AR.top
                AR.top = kT.base32
                clatT = AR.alloc(2 * SEQ, BF16, "clatT")
                clat = AR.alloc(16 * 256, BF16, "clat")
                ckr = AR.alloc(SEQ, BF16, "ckr")
                wUKT = AR.alloc(NH * 256, BF16, "wUKT")
                qabs = AR.alloc(2 * 128, BF16, "qabs")
                latn = AR.alloc(256, BF16, "latn")
                olat = AR.alloc(2 * 128, BF16, "olat")
                pts = [AR.alloc(128, BF16, "pts%d" % i) for i in range(4)]
                wUVs = AR.alloc(2 * 512, BF16, "wUVs")
                wload(wUVs.all(), d_wuv2[l])
                assert AR.top <= Vt.base32 + Vt.n32
                wload(clatT.all(), d_clatT[l])
                wload(clat.all(), d_clat[l])
                wload(ckr.s(0, SEQ, 64, 96), d_ckr[l])
                wload(wUKT.s(0, NH * 256, 0, 64), d_wukT[l])
                pq = banks[0]
                for h in range(NH):
                    for kc in range(2):
                        P.mm(pq.s(kc * 128 + h * 16, kc * 128 + h * 16 + 16),
                             [(wUKT.s(h * 256 + kc * 128, h * 256 + kc * 128 + 128, 0, 64), qs.s(h * 16, h * 16 + 16, 0, 64))])
                P.act(qabs.all(), pq.s(0, 256), AF.Copy)
                pl = banks[1]
                for kc in range(2):
                    P.mm(pl.s(kc * 128, kc * 128 + 128, 0, 16),
                         [(latb.s(kc * TP + tl, kc * TP + tl + n), identb.all())])
                P.copy(latn.s(0, 256, 0, 16), pl.s(0, 256, 0, 16), eng="dve")
                ol = [banks[4], banks[5]]
                dn = banks[6]
                for j in range(17):
                    new = j == 16
                    kk = 16 if new else 128
                    ps = banks[2 + j % 2].s(0, 128, 0, kk)
                    if new:
                        pairs = [(latb.s(kc * TP + tl, kc * TP + tl + n), qabs.s(kc * 128, kc * 128 + 128)) for kc in range(2)]
                        pairs.append((krs.s(0, n, 64, 96), qs.s(0, 128, 64, 96)))
                    else:
                        pairs = [(clatT.s(kc * SEQ + j * 128, kc * SEQ + j * 128 + 128), qabs.s(kc * 128, kc * 128 + 128))
                                 for kc in range(2)]
                        pairs.append((ckr.s(j * 128, j * 128 + 128, 64, 96), qs.s(0, 128, 64, 96)))
                    P.mm(ps, pairs)
                    pt = pts[j % 4].s(0, 128, 0, kk)
                    P.act(pt, ps, AF.Exp, scale=SM_SCALE)
                    for kc in range(2):
                        lhs = latn.s(kc * 128, kc * 128 + 128, 0, 16) if new else clat.s(j * 256 + kc * 128, j * 256 + kc * 128 + 128)
                        P.mm(ol[kc].s(0, 128), [(lhs, pt)], start=(j == 0), stop=new)
                    P.mm(dn.s(0, 128), [(onesb.s(0, 128, 0, kk), pt)], start=(j == 0), stop=new)
                for kc in range(2):
                    P.act(olat.s(kc * 128, kc * 128 + 128), ol[kc].s(0, 128), AF.Copy)
                rd = rden[0]
                P.record("dve", (lambda e, o=rd.s(0, 128), i=dn.s(0, 128): e.reciprocal(o.ap, i.ap)),
                         [dn.s(0, 128)], [rd.s(0, 128)])
                for h in range(NH):
                    hp, half = h // 2, h % 2
                    r0, r1 = half * 64, half * 64 + 64
                    ps = banks[h % 2].s(256, 256 + 16)
                    P.mm(ps, [(wUVs.s(kc * 512 + hp * 128, kc * 512 + hp * 128 + 128), olat.s(kc * 128 + h * 16, kc * 128 + h * 16 + 16))
                              for kc in range(2)])
                    P.tt(aT.s(hp * TP + tl, hp * TP + tl + n, r0, r1), bank_rows(ps, r0, r1), rd.s(h * 16, h * 16 + 16, r0, r1), ALU.mult)
                AR.top = mark

            if isB and early_f2 is not None:
                early_f2(sq, rstd)
            if stop <= 4:
                return
            P.mark("L%d mix%d brB" % (l, pi))
            AR.top = stage_top
            branch_b(l, pi, T0, TP, tiles, ptiles, stile, nT, ybT, carryb, sq, rstd, tf + [tabc, tabs])
            if stop <= 5:
                return
            P.mark("L%d mix%d brC" % (l, pi))
            AR.top = stage_top
            branch_c(l, pi, T0, TP, tiles, ptiles, stile, nT, ycT, sq, rstd, tf + [tabc, tabs])
            if stop <= 6:
                return
            P.mark("L%d mix%d brD" % (l, pi))
            AR.top = stage_top
            branch_d(l, pi, T0, TP, tiles, ptiles, stile, nT, ydT, carryd, tf)

            if stop <= 7:
                return
            P.mark("L%d mix%d merge" % (l, pi))
            AR.top = stage_top
            mrg = AR.alloc(KC * TP, BF16, "merged")
            wG = [AR.alloc(KC * 512 + 10 * 128, BF16, "wG0"),
                  AR.at(sq.base32, KC * 512 + 10 * 128, BF16, "wG1")]
            assert sq.base32 + 2688 <= tf[1].base32
            wO = [AR.at(wG[0].base32 + i * 512, KC * 128, BF16, "wO%d" % i) for i in range(5)]
            gsb = [tabc, tabs]
            brs = ((aT, 4), (ybT, 2), (ycT, 2), (ydT, 2))
            for m in range(8):
                w = wG[m % 2]
                wload(w.all(), d_wgm[l, m])
                for (t0, n) in tiles:
                    tl = t0 - T0
                    kb0 = 0
                    for b, (src, nk) in enumerate(brs):
                        pg = bank().s(0, n)
                        pp = bank().s(0, n)
                        P.mm(pg, [(w.s((k * 4 + b) * 128, (k * 4 + b) * 128 + 128), ntk(k, tl, n)) for k in range(KC)])
                        P.mm(pp, [(w.s(KC * 512 + (kb0 + kk) * 128, KC * 512 + (kb0 + kk) * 128 + 128),
                                   src.s(kk * TP + tl, kk * TP + tl + n)) for kk in range(nk)])
                        kb0 += nk
                        g = gsb[b % 2].s(0, n)
                        P.act(g, pg, AF.Sigmoid)
                        if b == 0:
                            P.tt(tf[2].s(0, n), pp, g, ALU.mult)
                        else:
                            P.tt(tf[3].s(0, n), pp, g, ALU.mult)
                            dst = tf[2].s(0, n) if b < 3 else mrg.s(m * TP + tl, m * TP + tl + n)
                            P.tt(dst, tf[2].s(0, n), tf[3].s(0, n), ALU.add)
            P.mark("L%d mix%d wo" % (l, pi))
            for m in range(8):
                w = wO[m % 5]
                wload(w.all(), d_wo[l, m])
                for (t0, n) in tiles:
                    tl = t0 - T0
                    py = bank().s(0, n)
                    P.mm(py, [(w.s(k * 128, k * 128 + 128), mrg.s(k * TP + tl, k * TP + tl + n)) for k in range(KC)])
                    P.act(ytmp.s(m * TP + tl, m * TP + tl + n), py, AF.Copy)
            postnorm_residual(l, "mpost", tiles, T0, TP, ytmp, sq, rstd, False)

        def bank_rows(ref, p0, p1):
            return Ref(ref.ap[p0:p1] if p0 == 0 and False else ref.ap[p0:p1, :], ref.atoms)

        def d_lat_out(l, mc, t0, n):
            return o_lat[l, :, mc * NT + t0: mc * NT + t0 + n]


        def reduce_sum(out, in_):
            P.record("dve", (lambda e, o=out, i=in_: e.tensor_reduce(o.ap, i.ap, mybir.AxisListType.X, ALU.add)),
                     [in_], [out])

        def branch_b(l, pi, T0, TP, tiles, ptiles, stile, nT, ybT, carryb, sq, rstd, tf):
            PTN = sum(n for (_, n) in ptiles)
            W = PADB + PTN
            WS = PADB + DSEQ
            wgl = AR.alloc(KC * 512, BF16, "wgl")
            wload(wgl.all(), d_wglu[l])
            diag = AR.alloc(CBW * 2 * 128, BF16, "diagb")
            xbp = AR.alloc(2 * W, BF16, "xbp")
            xbs = AR.alloc(2 * WS, BF16, "xbs")
            cbfs = [rstd, tf[3]]
            mean = tf[4]
            rs = tf[5]
            cbst = AR.alloc(2 * 2 * PADB, F32, "cbst")
            for j in range(CBW):
                for c in range(2):
                    P.ts(diag.s((j * 2 + c) * 128, (j * 2 + c) * 128 + 128), identf.all(), vcol(l, "cbw", j * 2 + c), ALU.mult)
            for c in range(2):
                if pi == 0:
                    P.memset(xbp.s(c * W, c * W + PADB), 0.0, eng="dve")
                else:
                    P.copy(xbp.s(c * W, c * W + PADB), carryb.s(c * PADB, (c + 1) * PADB), eng="dve")
                    wload(xbs.s(c * WS, c * WS + PADB), d_scb[l][:, c * PADB:(c + 1) * PADB])
                    P.dma("sp", cbst.s(c * 60 + 30, c * 60 + 44), d_scb[l][:, c * PADB + 16:c * PADB + 30])
            def phA(t0, n):
                tl = t0 - T0
                samp = t0 >= SEQ
                for c in range(2):
                    pa = bank().s(0, n)
                    pg = bank().s(0, n)
                    P.mm(pa, [(wgl.s(k * 512 + c * 128, k * 512 + c * 128 + 128), nT.s(k * TP + tl, k * TP + tl + n)) for k in range(KC)])
                    P.mm(pg, [(wgl.s(k * 512 + 256 + c * 128, k * 512 + 256 + c * 128 + 128), nT.s(k * TP + tl, k * TP + tl + n)) for k in range(KC)])
                    sg = tf[c].s(0, n)
                    P.act(sg, pg, AF.Sigmoid)
                    if samp:
                        dst = xbs.s(c * WS + PADB, c * WS + PADB + n)
                        P.tt(cbst.s(c * 60 + 44, c * 60 + 60), pa, sg, ALU.mult)
                    else:
                        dst = xbp.s(c * W + PADB + tl, c * W + PADB + tl + n)
                        if t0 + n == SEQ:
                            P.tt(cbst.s(c * 60, c * 60 + 30), Ref(pa.ap[:, n - 30:n], pa.atoms), tf[c].s(n - 30, n), ALU.mult)
                    P.tt(dst, pa, sg, ALU.mult)

            def phB(t0, n):
                tl = t0 - T0
                samp = t0 >= SEQ
                for c in range(2):
                    ps = bank().s(0, n)
                    if samp:
                        P.mm(ps, [(diag.s((j * 2 + c) * 128, (j * 2 + c) * 128 + 128), xbs.s(c * WS + j, c * WS + j + n)) for j in range(CBW)])
                    else:
                        P.mm(ps, [(diag.s((j * 2 + c) * 128, (j * 2 + c) * 128 + 128), xbp.s(c * W + tl + j, c * W + tl + j + n)) for j in range(CBW)])
                    P.act(cbfs[c].s(0, n), ps, AF.Identity, bias=vcol(l, "cbb", c))
                    P.act(sq.s(c * 512, c * 512 + n), cbfs[c].s(0, n), AF.Copy)
                    P.act(sq.s((2 + c) * 512, (2 + c) * 512 + n), cbfs[c].s(0, n), AF.Square)

            def phC(t0, n):
                tl = t0 - T0
                pm = bank().s(0, n)
                pq = bank().s(0, n)
                P.mm(pm, [(onesb.all(), sq.s(c * 512, c * 512 + n)) for c in range(2)])
                P.mm(pq, [(onesb.all(), sq.s((2 + c) * 512, (2 + c) * 512 + n)) for c in range(2)])
                mn = mean.s(0, n)
                r = rs.s(0, n)
                m2 = tf[2].s(0, n)
                P.ts(mn, pm, 1.0 / 256, ALU.mult)
                P.tt(m2, mn, mn, ALU.mult)
                P.stt(r, pq, 1.0 / 256, m2, ALU.mult, ALU.subtract)
                P.act(r, r, AF.Ln, bias=epsb[1.0])
                P.act(r, r, AF.Exp, scale=-0.5)
                for c in range(2):
                    x = cbfs[c].s(0, n)
                    P.tt(x, x, mn, ALU.subtract)
                    P.tt(x, x, r, ALU.mult)
                    P.act(ybT.s(c * TP + tl, c * TP + tl + n), x, AF.Silu, scale=vcol(l, "cblg", c), bias=vcol(l, "cblb", c))

            phA(*tiles[0])
            phB(*tiles[0])
            for ti in range(1, len(tiles)):
                phA(*tiles[ti])
                phC(*tiles[ti - 1])
                phB(*tiles[ti])
            phC(*tiles[-1])
            if pi == 0:
                for c in range(2):
                    P.copy(carryb.s(c * PADB, (c + 1) * PADB), xbp.s(c * W + PTN, c * W + PTN + PADB), eng="dve")
            else:
                P.dma("sp", o_cb[l], cbst.all())

        def branch_c(l, pi, T0, TP, tiles, ptiles, stile, nT, ycT, sq, rstd, tf):
            wuv = AR.alloc(KC * 512, BF16, "wuv")
            wload(wuv.all(), d_wuv[l])
            wsT = AR.alloc(512, BF16, "wsT")
            wload(wsT.all(), d_wsT[l])
            bsr = AR.alloc(512, BF16, "bsr")
            wload(bsr.s(0, 512, 0, 1), d_bs[l])
            for g in range(4):
                P.memset(wsT.s(g * 128, g * 128 + 64, 64, 128), 0.0, eng="dve")
            uT = AR.alloc(2 * TP, BF16, "uT")
            vnT = AR.alloc(2 * 512, BF16, "vnT")
            vtm = [AR.alloc(256, BF16, "vtm%d" % i) for i in range(2)]
            gvst = AR.alloc(2 * DSEQ, F32, "gvst")
            vfs = [rstd, tf[3]]
            mean, rs = tf[0], tf[1]
            bcount = 0
            def phA(ti, t0, n):
                tl = t0 - T0
                vf_ = vfsets[ti % 2]
                so = (ti % 2) * 4
                for c in range(2):
                    pu = bank().s(0, n)
                    P.mm(pu, [(wuv.s(k * 512 + c * 128, k * 512 + c * 128 + 128), nT.s(k * TP + tl, k * TP + tl + n)) for k in range(KC)])
                    P.act(uT.s(c * TP + tl, c * TP + tl + n), pu, AF.Copy)
                for c in range(2):
                    pv = bank().s(0, n)
                    P.mm(pv, [(wuv.s(k * 512 + 256 + c * 128, k * 512 + 256 + c * 128 + 128), nT.s(k * TP + tl, k * TP + tl + n)) for k in range(KC)])
                    P.act(vf_[c].s(0, n), pv, AF.Copy)
                    P.act(sq.s((so + c) * 512, (so + c) * 512 + n), vf_[c].s(0, n), AF.Copy)
                    P.act(sq.s((so + 2 + c) * 512, (so + 2 + c) * 512 + n), vf_[c].s(0, n), AF.Square)

            def phC(ti, t0, n):
                tl = t0 - T0
                samp = t0 >= SEQ
                vf_ = vfsets[ti % 2]
                so = (ti % 2) * 4
                pm = bank().s(0, n)
                pq = bank().s(0, n)
                P.mm(pm, [(onesb.all(), sq.s((so + c) * 512, (so + c) * 512 + n)) for c in range(2)])
                P.mm(pq, [(onesb.all(), sq.s((so + 2 + c) * 512, (so + 2 + c) * 512 + n)) for c in range(2)])
                mn = mean.s(0, n)
                r = rs.s(0, n)
                m2 = tf[2].s(0, n)
                P.ts(mn, pm, 1.0 / 256, ALU.mult)
                P.tt(m2, mn, mn, ALU.mult)
                P.stt(r, pq, 1.0 / 256, m2, ALU.mult, ALU.subtract)
                P.act(r, r, AF.Ln, bias=epsb[1.0])
                P.act(r, r, AF.Exp, scale=-0.5)
                for c in range(2):
                    x = vf_[c].s(0, n)
                    P.tt(x, x, mn, ALU.subtract)
                    P.tt(x, x, r, ALU.mult)
                    if samp:
                        P.act(gvst.s(c * DSEQ, (c + 1) * DSEQ), x, AF.Identity, scale=vcol(l, "vng", c), bias=vcol(l, "vnb", c))
                    P.act(vnT.s(c * 512, c * 512 + n), x, AF.Identity, scale=vcol(l, "vng", c), bias=vcol(l, "vnb", c))
                if samp:
                    P.dma("sp", o_gv[l], gvst.all())
                bsz = min(n, 128)
                for bi in range(max(1, n // 128)):
                    c0 = tl + bi * 128
                    v16 = vtm[bc["i"] % 2]
                    bc["i"] += 1
                    pt = bank().s(0, 256, 0, bsz)
                    for ch in range(2):
                        P.mm(Ref(pt.ap[:, ch * 128:(ch + 1) * 128], pt.atoms),
                             [(vnT.s(ch * 512 + bi * 128, ch * 512 + bi * 128 + bsz), identb.all())])
                    P.copy(v16.s(0, 256, 0, bsz), pt, eng="dve")
                    for ch in range(2):
                        for gg in range(2):
                            g = 2 * ch + gg
                            psg = bank().s(0, bsz)
                            P.mm(psg, [(v16.s(ch * 128, ch * 128 + 128, 0, bsz), wsT.s(g * 128, g * 128 + bsz, 0, bsz)),
                                       (onesb.s(0, 128, 0, 1), bsr.s(g * 128, g * 128 + bsz, 0, 1))])
                            r0, r1 = gg * 64, gg * 64 + 64
                            P.tt(ycT.s(ch * TP + c0, ch * TP + c0 + bsz, r0, r1), Ref(psg.ap[r0:r1, :], psg.atoms),
                                 uT.s(ch * TP + c0, ch * TP + c0 + bsz, r0, r1), ALU.mult)

            bc = {"i": 0}
            vfsets = [[rstd, tf[3]], [tf[4], tf[5]]]
            phA(0, *tiles[0])
            for ti in range(1, len(tiles)):
                phA(ti, *tiles[ti])
                phC(ti - 1, *tiles[ti - 1])
            phC(len(tiles) - 1, *tiles[-1])

        def branch_d(l, pi, T0, TP, tiles, ptiles, stile, nT, ydT, carryd, tf):
            PTN = sum(n for (_, n) in ptiles)
            W = PADD + PTN
            WS = PADD + DSEQ
            wdp = AR.alloc(KC * 768, BF16, "wdp")
            wload(wdp.all(), d_wdp[l])
            diag = AR.alloc(CDW * 2 * 128, BF16, "diagd")
            xdp = AR.alloc(2 * W, BF16, "xdp")
            xds = AR.alloc(2 * WS, BF16, "xds")
            bgT = AR.alloc(2 * TP, BF16, "bgT")
            cdst = AR.alloc(2 * 2 * PADD, F32, "cdst")
            for j in range(CDW):
                for c in range(2):
                    P.ts(diag.s((j * 2 + c) * 128, (j * 2 + c) * 128 + 128), identf.all(), vcol(l, "cdw", j * 2 + c), ALU.mult)
            for c in range(2):
                if pi == 0:
                    P.memset(xdp.s(c * W, c * W + PADD), 0.0, eng="dve")
                else:
                    P.copy(xdp.s(c * W, c * W + PADD), carryd.s(c * PADD, (c + 1) * PADD), eng="dve")
                    wload(xds.s(c * WS, c * WS + PADD), d_scd[l][:, c * PADD:(c + 1) * PADD])
            for (t0, n) in tiles:
                tl = t0 - T0
                samp = t0 >= SEQ
                for c in range(2):
                    pb = bank().s(0, n)
                    pc = bank().s(0, n)
                    ph = bank().s(0, n)
                    for ps_, off in ((pb, 0), (pc, 256), (ph, 512)):
                        P.mm(ps_, [(wdp.s(k * 768 + off + c * 128, k * 768 + off + c * 128 + 128), nT.s(k * TP + tl, k * TP + tl + n)) for k in range(KC)])
                    P.act(bgT.s(c * TP + tl, c * TP + tl + n), pb, AF.Copy)
                    cg = tf[c].s(0, n)
                    P.act(cg, pc, AF.Copy)
                    if samp:
                        dst = xds.s(c * WS + PADD, c * WS + PADD + n)
                        P.tt(cdst.s(c * 4 + 2, c * 4 + 4), Ref(ph.ap[:, n - 2:n], ph.atoms), tf[c].s(n - 2, n), ALU.mult)
                    else:
                        dst = xdp.s(c * W + PADD + tl, c * W + PADD + tl + n)
                        if t0 + n == SEQ:
                            P.tt(cdst.s(c * 4, c * 4 + 2), Ref(ph.ap[:, n - 2:n], ph.atoms), tf[c].s(n - 2, n), ALU.mult)
                    P.tt(dst, ph, cg, ALU.mult)
                for c in range(2):
                    ps = bank().s(0, n)
                    if samp:
                        P.mm(ps, [(diag.s((j * 2 + c) * 128, (j * 2 + c) * 128 + 128), xds.s(c * WS + j, c * WS + j + n)) for j in range(CDW)])
                    else:
                        P.mm(ps, [(diag.s((j * 2 + c) * 128, (j * 2 + c) * 128 + 128), xdp.s(c * W + tl + j, c * W + tl + j + n)) for j in range(CDW)])
                    P.tt(ydT.s(c * TP + tl, c * TP + tl + n), ps, bgT.s(c * TP + tl, c * TP + tl + n), ALU.mult)
            if pi == 0:
                for c in range(2):
                    P.copy(carryd.s(c * PADD, (c + 1) * PADD), xdp.s(c * W + PTN, c * W + PTN + PADD), eng="dve")
            else:
                P.dma("sp", o_cd[l], cdst.all())

        for l in range(n_layers):
            early = None
            if do_mixer and do_ffn is True:
                def early(sq_, rstd_, l=l):
                    (TA0, TPA, tilesA) = PASSES[0]
                    prenorm(l, "f2pre", tilesA, TA0, TPA, Buf(P, "A", 0, KC * TPA, BF16, "nTA_early"), sq_, rstd_)
            gen = mixer_layer(l, early) if do_mixer else None
            chained = bool(do_ffn is True and l > 0)
            if do_ffn:
                ffn_pair(l, 0, next_pre=(lambda: next(gen)) if gen is not None else None,
                         skip_preA=chained, nTA_alt=chained)
            elif gen is not None:
                next(gen)
            if gen is not None:
                for _ in gen:
                    pass
            if do_ffn is True:
                ffn_pair(l, 1, skip_preA=(early is not None), chain_next=(l + 1 if l + 1 < n_layers else None))
        for k in range(KC):
            P.dma("sp", o_y[:, k * NT:(k + 1) * NT], xT.s(k * NT, (k + 1) * NT))
        P.barrier_on("sp", [xT.all(), Buf(P, "A", 0, ASIZE, F32).all()])
        P.mark("end")
        P.emit()
        import json as _json
        if os.environ.get("MARKS"):
            _json.dump(P.marks, open(os.environ["MARKS"], "w"))
        print("arena peak", AR.peak, "of", ASIZE, {e: len(P.ops[e]) for e in ENGINES}, "dma sems", len(P.dma_sems))
    return nc


def _slab(W, ms):
    K, M = W.shape
    return np.ascontiguousarray(
        W.reshape(K // 128, 128, M // ms, ms).transpose(2, 1, 0, 3)).reshape(M // ms, 128, (K // 128) * ms)


def _rope_tables():
    half = RD // 2
    inv = np.exp(-math.log(10000.0) * np.arange(half, dtype=np.float32) / half).astype(np.float32)
    pos = np.arange(NT, dtype=np.float32)
    ang = (pos[:, None] * inv[None, :]).astype(np.float32)
    c, s = np.cos(ang).astype(np.float32), np.sin(ang).astype(np.float32)
    cosT = np.concatenate([c, c], axis=1).T
    sinS = np.concatenate([-s, s], axis=1).T
    return np.ascontiguousarray(cosT), np.ascontiguousarray(sinS)


def prep_shared(I):
    f = lambda a: np.asarray(a, dtype=np.float32)
    S = {}
    wgu = np.empty((L, 2, FC, 128, KC * 256), np.float32)
    wdn = np.empty((L, 2, 8, 128, FC * 128), np.float32)
    for l in range(L):
        for fi, (gu, dn) in enumerate((("ffn1_w_gu", "ffn1_w_down"), ("ffn2_w_gu", "ffn2_w_down"))):
            W = f(I[gu][l]).reshape(KC, 128, 2 * DFF)
            G = W[:, :, :DFF].reshape(KC, 128, FC, 128)
            U = W[:, :, DFF:].reshape(KC, 128, FC, 128)
            wgu[l, fi] = np.stack([G, U], axis=3).transpose(2, 1, 0, 3, 4).reshape(FC, 128, KC * 256)
            wdn[l, fi] = _slab(f(I[dn][l]), 128)
    S["wgu"], S["wdn"] = wgu, wdn
    win = f(I["w_in"])
    S["wcq"] = np.stack([_slab(win[l][:, 0:384], 384)[0] for l in range(L)])
    S["wckv"] = np.stack([_slab(win[l][:, 384:640], 256)[0] for l in range(L)])
    wkr = np.zeros((L, 2, 1024, 96), np.float32)
    wkr[:, 0, :, 64:96] = win[:, :, 640:672]
    wkr[:, 1, :, 64:80] = win[:, :, 656:672]
    wkr[:, 1, :, 80:96] = win[:, :, 640:656]
    S["wkr"] = np.stack([np.stack([_slab(wkr[l, i], 96)[0] for i in range(2)]) for l in range(L)])
    S["wglu"] = np.stack([_slab(win[l][:, 672:1184], 512)[0] for l in range(L)])
    S["wuv"] = np.stack([_slab(win[l][:, 1184:1696], 512)[0] for l in range(L)])
    S["wdp"] = np.stack([_slab(win[l][:, 1696:2464], 768)[0] for l in range(L)])
    wgm = np.empty((L, 8, 128, KC * 512 + 10 * 128), np.float32)
    for l in range(L):
        Wg = win[l][:, 2464:].reshape(KC, 128, 4, 8, 128).transpose(3, 1, 0, 2, 4).reshape(8, 128, KC * 512)
        Wbr = np.concatenate([f(I["w_br_a"][l]), f(I["w_br_b"][l]), f(I["w_br_c"][l]), f(I["w_br_d"][l])], axis=0)
        Wb = Wbr.reshape(10, 128, 8, 128).transpose(2, 1, 0, 3).reshape(8, 128, 10 * 128)
        wgm[l] = np.concatenate([Wg, Wb], axis=2)
    S["wgm"] = wgm
    S["wo"] = np.stack([_slab(f(I["w_o"][l]), 128) for l in range(L)])
    wuq = f(I["w_uq"])
    wuq_sw = wuq.reshape(L, QL, NH, 96).copy()
    wuq_sw[..., 64:80] = wuq.reshape(L, QL, NH, 96)[..., 80:96]
    wuq_sw[..., 80:96] = wuq.reshape(L, QL, NH, 96)[..., 64:80]
    wuq_sw = wuq_sw.reshape(L, QL, NH * 96)
    S["wuq"] = np.stack([np.stack([_slab(wuq[l], 768)[0], _slab(wuq_sw[l], 768)[0]]) for l in range(L)])
    wukv = f(I["w_ukv"]).reshape(L, 2, 128, NH, 128)
    S["wuk"] = np.ascontiguousarray(wukv[..., :64].transpose(0, 2, 1, 3, 4)).reshape(L, 128, 2 * NH * 64)
    S["wuv2"] = np.ascontiguousarray(wukv[..., 64:].transpose(0, 2, 1, 3, 4)).reshape(L, 128, 2 * 512)
    w3 = f(I["w_ukv"]).reshape(L, KVL, NH, 128)[..., :64]
    S["wukT"] = np.ascontiguousarray(w3.transpose(0, 3, 2, 1)).reshape(L, 64, NH * 256)
    S["wsT"] = np.ascontiguousarray(f(I["gmlp_w_s"]).transpose(0, 3, 1, 2)).reshape(L, 128, 512)
    S["bsrow"] = np.ascontiguousarray(f(I["gmlp_b_s"])).reshape(L, 1, 512)
    gvb = np.concatenate([f(I["gmlp_vn_g"]), f(I["gmlp_vn_b"])], axis=1)
    S["gvb"] = np.ascontiguousarray(np.broadcast_to(gvb[:, None, :], (L, 128, 512)))
    vecs = np.zeros((128, L * NV), np.float32)

    def put(l, name, v, w):
        v = f(v).reshape(w, 128).T
        vecs[:, l * NV + VOFF[name]: l * NV + VOFF[name] + w] = v
    for l in range(L):
        put(l, "f1pre", I["ffn1_norm_pre"][l], 8)
        put(l, "f1post", I["ffn1_norm_post"][l], 8)
        put(l, "mpre", I["mix_norm_pre"][l], 8)
        put(l, "mpost", I["mix_norm_post"][l], 8)
        put(l, "f2pre", I["ffn2_norm_pre"][l], 8)
        put(l, "f2post", I["ffn2_norm_post"][l], 8)
        put(l, "qn", I["q_norm"][l], 3)
        put(l, "kvn", I["kv_norm"][l], 2)
        put(l, "cbb", I["conv_b_bias"][l], 2)
        put(l, "cblg", I["conv_b_ln_g"][l], 2)
        put(l, "cblb", I["conv_b_ln_b"][l], 2)
        put(l, "cbw", I["conv_b_w"][l], CBW * 2)
        put(l, "cdw", I["conv_d_w"][l], CDW * 2)
        put(l, "vng", I["gmlp_vn_g"][l], 2)
        put(l, "vnb", I["gmlp_vn_b"][l], 2)
    S["vecs"] = vecs
    S["ident"] = np.eye(128, dtype=np.float32)
    S["cosT"], S["sinS"] = _rope_tables()
    return S


def prep_core(I, c):
    f = lambda a: np.asarray(a, dtype=np.float32)
    C = {}
    X = np.concatenate([f(I["x_prompt"][c]), f(I["x_sample"][c])], axis=0)
    C["xT"] = np.ascontiguousarray(X.T.reshape(KC, 128, NT).transpose(1, 0, 2)).reshape(128, KC * NT)
    cl = f(I["cache_kv_latent"][:, c])
    C["clatT"] = np.ascontiguousarray(cl.transpose(0, 2, 1).reshape(L, 2, 128, SEQ).transpose(0, 2, 1, 3)).reshape(L, 128, 2 * SEQ)
    C["clat"] = np.ascontiguousarray(cl.reshape(L, 16, 128, 256).transpose(0, 2, 1, 3)).reshape(L, 128, 16 * 256)
    C["ckrT"] = np.ascontiguousarray(f(I["cache_k_rope"][:, c]).transpose(0, 2, 1))
    sb = f(I["state_conv_b"][:, c])
    C["scbT"] = np.ascontiguousarray(sb.transpose(0, 2, 1).reshape(L, 2, 128, PADB).transpose(0, 2, 1, 3)).reshape(L, 128, 2 * PADB)
    sd = f(I["state_conv_d"][:, c])
    C["scdT"] = np.ascontiguousarray(sd.transpose(0, 2, 1).reshape(L, 2, 128, PADD).transpose(0, 2, 1, 3)).reshape(L, 128, 2 * PADD)
    return C


def assemble(results):
    n = len(results)
    yp = np.empty((n, SEQ, D), np.float32)
    ys = np.empty((n, DSEQ, D), np.float32)
    latp = np.empty((L, n, SEQ, KVL), np.float32)
    lats = np.empty((L, n, DSEQ, KVL), np.float32)
    krp = np.empty((L, n, SEQ, RD), np.float32)
    krs = np.empty((L, n, DSEQ, RD), np.float32)
    cbp = np.empty((L, n, PADB, 256), np.float32)
    cbs = np.empty((L, n, PADB, 256), np.float32)
    cdp = np.empty((L, n, PADD, 256), np.float32)
    cds = np.empty((L, n, PADD, 256), np.float32)
    gv = np.empty((L, n, DSEQ, 256), np.float32)
    for c, r in enumerate(results):
        y = np.asarray(r["yT"]).reshape(128, KC, NT).transpose(2, 1, 0).reshape(NT, D)
        yp[c], ys[c] = y[:SEQ], y[SEQ:]
        lat = np.asarray(r["latT"]).reshape(L, 128, 2, NT).transpose(0, 3, 2, 1).reshape(L, NT, KVL)
        latp[:, c], lats[:, c] = lat[:, :SEQ], lat[:, SEQ:]
        kr = np.asarray(r["krT"]).transpose(0, 2, 1)
        krp[:, c], krs[:, c] = kr[:, :SEQ], kr[:, SEQ:]
        cb = np.asarray(r["cbT"]).reshape(L, 128, 2, 2, PADB).transpose(0, 3, 4, 2, 1).reshape(L, 2, PADB, 256)
        cbp[:, c], cbs[:, c] = cb[:, 0], cb[:, 1]
        cd = np.asarray(r["cdT"]).reshape(L, 128, 2, 2, PADD).transpose(0, 3, 4, 2, 1).reshape(L, 2, PADD, 256)
        cdp[:, c], cds[:, c] = cd[:, 0], cd[:, 1]
        gv[:, c] = np.asarray(r["gvT"]).reshape(L, 128, 2, DSEQ).transpose(0, 3, 2, 1).reshape(L, DSEQ, 256)
    return (yp, ys, latp, krp, cbp, cdp, lats, krs, cbs, gv, cds)


_NC_CACHE = {}


def kernel(**inputs):
    if "nc" not in _NC_CACHE:
        _NC_CACHE["nc"] = build_program()
    nc = _NC_CACHE["nc"]
    S = prep_shared(inputs)
    in_maps = []
    for c in range(8):
        m = dict(S)
        m.update(prep_core(inputs, c))
        in_maps.append(m)
    res = run_bass_kernel_spmd(nc, in_maps, core_ids=list(range(8)))
    return assemble(res.results)
```

```python
import contextlib
import math
import numpy as np
import concourse.bass as bass
import concourse.mybir as mybir
from concourse.bass_utils import run_bass_kernel_spmd

F32 = mybir.dt.float32
BF16 = mybir.dt.bfloat16
AF = mybir.ActivationFunctionType
ALU = mybir.AluOpType

ATOM = 8


class Ref:
    __slots__ = ("ap", "atoms")

    def __init__(self, ap, atoms):
        self.ap = ap
        self.atoms = atoms


class Buf:
    def __init__(self, prog, space, base32, ncols, dt, name=""):
        self.prog = prog
        self.space = space
        self.base32 = base32
        self.ncols = ncols
        self.dt = dt
        self.name = name
        self.esz = 2 if dt == BF16 else 4
        n32 = (ncols * self.esz + 3) // 4
        self.n32 = n32
        t = prog.tensors[space]
        ap = t[:, base32:base32 + n32]
        if dt != F32:
            ap = ap.bitcast(dt)
        self.full = ap

    def s(self, c0, c1, p0=0, p1=128, step=None):
        assert 0 <= c0 < c1 <= self.ncols, (self.name, c0, c1, self.ncols)
        if step is None:
            ap = self.full[p0:p1, c0:c1]
        else:
            ap = self.full[p0:p1, c0:c1:step]
        asz = (512 if self.space == "PS" else ATOM) * 4
        a0 = (self.base32 * 4 + c0 * self.esz) // asz
        a1 = (self.base32 * 4 + c1 * self.esz - 1) // asz
        return Ref(ap, [(self.space, a) for a in range(a0, a1 + 1)])

    def all(self):
        return self.s(0, self.ncols)


class Op:
    __slots__ = ("eng", "fn", "waits", "need_sig", "sigval", "dma_sem", "dma_val", "eidx")

    def __init__(self, eng, fn):
        self.eng = eng
        self.fn = fn
        self.waits = []
        self.need_sig = False
        self.sigval = None
        self.dma_sem = None
        self.dma_val = None
        self.eidx = None


ENGINES = ("pe", "act", "dve", "pool", "sp")


class Prog:
    def __init__(self, nc):
        self.nc = nc
        self.tensors = {}
        self.ops = {e: [] for e in ENGINES}
        self.last_w = {}
        self.readers = {}
        self.seen = {e: {} for e in ENGINES}
        self.dma_sems = {}
        self.n_dma_sems = 0
        self.rr = 0
        self.n_mm = 0
        self.marks = []

    def _add_wait(self, op, dep, raw):
        if dep is op:
            return
        e = op.eng
        if dep.dma_sem is not None:
            key = ("dma", dep.dma_sem)
            if self.seen[e].get(key, 0) >= dep.dma_val:
                return
            self.seen[e][key] = dep.dma_val
            op.waits.append(("dma", dep.dma_sem, dep.dma_val))
            return
        if dep.eng == e and op.dma_sem is None:
            if e == "pe":
                return
        key = ("eng", dep.eng)
        if self.seen[e].get(key, -1) >= dep.eidx:
            return
        self.seen[e][key] = dep.eidx
        dep.need_sig = True
        op.waits.append(("op", dep))

    def record(self, eng, fn, reads=(), writes=(), dma_key=None):
        op = Op(eng, fn)
        op.eidx = len(self.ops[eng])
        if dma_key is not None:
            v = self.dma_sems.get(dma_key, 0) + 16
            self.dma_sems[dma_key] = v
            op.dma_sem = dma_key
            op.dma_val = v
        for r in reads:
            for a in r.atoms:
                w = self.last_w.get(a)
                if w is not None:
                    self._add_wait(op, w, True)
        for r in writes:
            for a in r.atoms:
                w = self.last_w.get(a)
                if w is not None:
                    self._add_wait(op, w, False)
                for rd in self.readers.get(a, ()):
                    self._add_wait(op, rd, False)
        for r in reads:
            for a in r.atoms:
                lst = self.readers.setdefault(a, [])
                if not lst or lst[-1] is not op:
                    lst.append(op)
        for r in writes:
            for a in r.atoms:
                self.last_w[a] = op
                self.readers[a] = []
        self.ops[eng].append(op)
        return op

    def mm(self, out, pairs, start=True, stop=True):
        reads = []
        for l, r in pairs:
            reads.append(l)
            reads.append(r)
        n = len(pairs)
        self.n_mm += n

        def fn(eng, out=out, pairs=pairs):
            ins = None
            for i, (l, r) in enumerate(pairs):
                ins = eng.matmul(out.ap, l.ap, r.ap, start=(start and i == 0),
                                 stop=(stop and i == n - 1))
            return ins
        return self.record("pe", fn, reads, [out])

    def act(self, out, in_, func, scale=1.0, bias=0.0, accum=None, eng="act"):
        reads = [in_]
        sc = scale.ap if isinstance(scale, Ref) else scale
        bi = bias.ap if isinstance(bias, Ref) else bias
        if isinstance(scale, Ref):
            reads.append(scale)
        if isinstance(bias, Ref):
            reads.append(bias)
        writes = [out]
        if accum is not None:
            writes.append(accum)

        def fn(eng, out=out, in_=in_):
            kw = {}
            if accum is not None:
                kw["accum_out"] = accum.ap
            return eng.activation(out.ap, in_.ap, func, bias=bi, scale=sc, **kw)
        return self.record("act", fn, reads, writes)

    def tt(self, out, a, b, op, eng="dve"):
        def fn(e, out=out, a=a, b=b):
            return e.tensor_tensor(out.ap, a.ap, b.ap, op)
        return self.record(eng, fn, [a, b], [out])

    def stt(self, out, in0, scalar, in1, op0, op1, eng="dve"):
        reads = [in0, in1]
        sc = scalar.ap if isinstance(scalar, Ref) else scalar
        if isinstance(scalar, Ref):
            reads.append(scalar)

        def fn(e, out=out, in0=in0, in1=in1):
            return e.scalar_tensor_tensor(out.ap, in0.ap, sc, in1.ap, op0, op1)
        return self.record(eng, fn, reads, [out])

    def ts(self, out, in0, s1, op0, s2=None, op1=None, eng="dve"):
        reads = [in0]
        a1 = s1.ap if isinstance(s1, Ref) else s1
        a2 = s2.ap if isinstance(s2, Ref) else s2
        if isinstance(s1, Ref):
            reads.append(s1)
        if isinstance(s2, Ref):
            reads.append(s2)

        def fn(e, out=out, in0=in0):
            if op1 is None:
                return e.tensor_scalar(out.ap, in0.ap, a1, None, op0)
            return e.tensor_scalar(out.ap, in0.ap, a1, a2, op0, op1)
        return self.record(eng, fn, reads, [out])

    def copy(self, out, in_, eng="dve"):
        def fn(e, out=out, in_=in_):
            return e.tensor_copy(out.ap, in_.ap)
        return self.record(eng, fn, [in_], [out])

    def memset(self, out, val, eng="pool"):
        def fn(e, out=out):
            return e.memset(out.ap, val)
        return self.record(eng, fn, [], [out])

    def dma(self, queue, out, in_, key=None):
        reads = [in_] if isinstance(in_, Ref) else []
        writes = [out] if isinstance(out, Ref) else []
        oap = out.ap if isinstance(out, Ref) else out
        iap = in_.ap if isinstance(in_, Ref) else in_
        if key is None:
            key = ("auto", self.rr % 24)
            self.rr += 1

        def fn(e):
            return e.dma_start(out=oap, in_=iap)
        prev = self.dma_sems.get(key, 0)
        op = self.record(queue, fn, reads, writes, dma_key=key)
        if prev > 0 and self.seen[queue].get(("dma", key), 0) < prev:
            self.seen[queue][("dma", key)] = prev
            op.waits.append(("dma", key, prev))
        return op

    def mark(self, label):
        self.marks.append((label, self.n_mm))

    def barrier_on(self, eng, refs):
        return self.record(eng, None, refs, refs, dma_key=None)

    def emit(self):
        nc = self.nc
        with contextlib.ExitStack() as st:
            esem = {e: st.enter_context(nc.semaphore("sem_" + e)) for e in ENGINES}
            dsem = {}
            for k in self.dma_sems:
                dsem[k] = st.enter_context(nc.semaphore("dsem%d" % len(dsem)))
            for e in ENGINES:
                c = 0
                for op in self.ops[e]:
                    if op.need_sig:
                        c += 1
                        op.sigval = c
            block = st.enter_context(nc.Block())
            bmap = {"pe": block.tensor, "act": block.scalar, "dve": block.vector,
                    "pool": block.gpsimd, "sp": block.sync}

            def make(ename):
                def body(eng):
                    for op in self.ops[ename]:
                        for w in op.waits:
                            if w[0] == "dma":
                                eng.wait_ge(dsem[w[1]], w[2])
                            else:
                                d = w[1]
                                eng.wait_ge(esem[d.eng], d.sigval)
                        if op.fn is None:
                            continue
                        ins = op.fn(eng)
                        if op.dma_sem is not None:
                            ins.then_inc(dsem[op.dma_sem], 16)
                        elif op.need_sig:
                            ins.then_inc(esem[ename], 1)
                return body
            for e in ENGINES:
                if self.ops[e]:
                    bmap[e](make(e))


D = 1024
KC = 8
SEQ = 2048
DSEQ = 16
NT = SEQ + DSEQ
L = 4
DFF = 2816
FC = DFF // 128
QL, KVL, RD = 384, 256, 32
NH = 8
EPS = 1e-6
SM_SCALE = (64 + 32) ** -0.5
CBW, CDW = 31, 3
PADB, PADD = CBW - 1, CDW - 1

VOFF = {}
_o = 0
for _n, _w in (("f1pre", 8), ("f1post", 8), ("mpre", 8), ("mpost", 8), ("f2pre", 8), ("f2post", 8),
               ("qn", 3), ("kvn", 2), ("cbb", 2), ("cblg", 2), ("cblb", 2), ("cbw", CBW * 2), ("cdw", CDW * 2),
               ("vng", 2), ("vnb", 2)):
    VOFF[_n] = _o
    _o += _w
NV = _o

PASSES = (
    (0, 1024, ((0, 512), (512, 512))),
    (1024, 1040, ((1024, 512), (1536, 512), (2048, 16))),
)


class Arena:
    def __init__(self, prog, space, size):
        self.prog, self.space, self.size = prog, space, size
        self.top = 0
        self.peak = 0

    def alloc(self, ncols, dt, name=""):
        esz = 2 if dt == BF16 else 4
        n32 = (ncols * esz + 3) // 4
        n32 = (n32 + 127) // 128 * 128
        assert self.top + n32 <= self.size, ("arena overflow", name, self.top, n32, self.size)
        b = Buf(self.prog, self.space, self.top, ncols, dt, name)
        self.top += n32
        self.peak = max(self.peak, self.top)
        return b

    def at(self, base32, ncols, dt, name=""):
        return Buf(self.prog, self.space, base32, ncols, dt, name)


import os


def build_program(n_layers=L, do_mixer=True, do_ffn=True, stop=99):
    nc = bass.Bass("TRN2", target_bir_lowering=False)

    def din(name, shape):
        return nc.dram_tensor(name, list(shape), F32, kind="ExternalInput").ap()

    def dout(name, shape):
        return nc.dram_tensor(name, list(shape), F32, kind="ExternalOutput").ap()

    d_x = din("xT", (128, KC * NT))
    d_vecs = din("vecs", (128, L * NV))
    d_ident = din("ident", (128, 128))
    d_cos = din("cosT", (32, NT))
    d_sin = din("sinS", (32, NT))
    d_wgu = din("wgu", (L, 2, FC, 128, KC * 256))
    d_wdn = din("wdn", (L, 2, 8, 128, FC * 128))
    d_wcq = din("wcq", (L, 128, KC * QL))
    d_wckv = din("wckv", (L, 128, KC * KVL))
    d_wkr = din("wkr", (L, 2, 128, KC * 96))
    d_wglu = din("wglu", (L, 128, KC * 512))
    d_wuv = din("wuv", (L, 128, KC * 512))
    d_wdp = din("wdp", (L, 128, KC * 768))
    d_wgm = din("wgm", (L, 8, 128, KC * 512 + 10 * 128))
    d_wo = din("wo", (L, 8, 128, KC * 128))
    d_wuq = din("wuq", (L, 2, 128, 3 * 768))
    d_wuk = din("wuk", (L, 128, 2 * NH * 64))
    d_wuv2 = din("wuv2", (L, 128, 2 * 512))
    d_wukT = din("wukT", (L, 64, NH * 256))
    d_wsT = din("wsT", (L, 128, 512))
    d_bs = din("bsrow", (L, 1, 512))
    d_gvb = din("gvb", (L, 128, 512))
    d_clatT = din("clatT", (L, 128, 2 * SEQ))
    d_clat = din("clat", (L, 128, 16 * 256))
    d_ckr = din("ckrT", (L, 32, SEQ))
    d_scb = din("scbT", (L, 128, 2 * PADB))
    d_scd = din("scdT", (L, 128, 2 * PADD))

    o_y = dout("yT", (128, KC * NT))
    o_lat = dout("latT", (L, 128, 2 * NT))
    o_kr = dout("krT", (L, 32, NT))
    o_cb = dout("cbT", (L, 128, 2 * 2 * PADB))
    o_cd = dout("cdT", (L, 128, 2 * 2 * PADD))
    o_gv = dout("gvT", (L, 128, 2 * DSEQ))

    with contextlib.ExitStack() as st:
        XCOLS = KC * NT
        CCOLS = L * NV + 128 + 64 + 64 + 2 * ATOM + 8
        CCOLS = (CCOLS + 127) // 128 * 128
        ASIZE = 35456
        t_x = st.enter_context(nc.sbuf_tensor("xres", [128, XCOLS], F32))
        t_c = st.enter_context(nc.sbuf_tensor("consts", [128, CCOLS], F32))
        t_a = st.enter_context(nc.sbuf_tensor("arena", [128, ASIZE], F32))
        t_p = st.enter_context(nc.psum_tensor("psum", [128, 8 * 512], F32))
        P = Prog(nc)
        P.tensors.update({"X": t_x, "C": t_c, "A": t_a, "PS": t_p})
        xT = Buf(P, "X", 0, XCOLS, F32, "xT")
        vecs = Buf(P, "C", 0, L * NV, F32, "vecs")
        identf = Buf(P, "C", L * NV, 128, F32, "identf")
        onesb = Buf(P, "C", L * NV + 128, 128, BF16, "ones")
        identb = Buf(P, "C", L * NV + 192, 128, BF16, "identb")
        banks = [Buf(P, "PS", i * 512, 512, F32, "bank%d" % i) for i in range(8)]
        AR = Arena(P, "A", ASIZE)
        state = {"bank": 0}

        def bank(lo=0, hi=8):
            b = banks[lo + state["bank"] % (hi - lo)]
            state["bank"] += 1
            return b

        def vcol(l, name, i=0):
            c = l * NV + VOFF[name] + i
            return vecs.s(c, c + 1)

        def xs(k, t0, n):
            return xT.s(k * NT + t0, k * NT + t0 + n)

        for k in range(KC):
            P.dma("sp", xT.s(k * NT, (k + 1) * NT), d_x[:, k * NT:(k + 1) * NT])
        P.dma("sp", vecs.all(), d_vecs)
        P.dma("sp", identf.all(), d_ident)
        epsb = {1.0: Buf(P, "C", L * NV + 256, 1, F32, "eps1"), 4.0: Buf(P, "C", L * NV + 256 + ATOM, 1, F32, "eps4")}
        for f2_, b_ in epsb.items():
            P.memset(b_.all(), EPS * f2_, eng="dve")
        epsb = {k_: v_.all() for k_, v_ in epsb.items()}
        P.memset(onesb.all(), 1.0, eng="dve")
        P.copy(identb.all(), identf.all(), eng="dve")

        wq = {"i": 0}

        def wload(dst, src):
            P.dma("pool", dst, src, key=("w", wq["i"] % 6))
            wq["i"] += 1

        def recip(out, in_):
            P.record("dve", (lambda e, o=out, i=in_: e.reciprocal(o.ap, i.ap)), [in_], [out])

        def rstd_from(srcs, n, Dn, out, sq, post=1.0):
            sqs = []
            for i, r in enumerate(srcs):
                q = sq.s(i * 512, i * 512 + n)
                P.act(q, r, AF.Square)
                sqs.append(q)
            ps = bank().s(0, n)
            P.mm(ps, [(onesb.all(), q) for q in sqs])
            f2 = 1.0 / (post * post)
            P.act(out, ps, AF.Ln, scale=f2 / Dn, bias=epsb[f2])
            P.act(out, out, AF.Exp, scale=-0.5)

        def prenorm(l, gname, tiles, T0, TP, nT, sq, rstd):
            for (t0, n) in tiles:
                tl = t0 - T0
                r = rstd.s(0, n)
                rstd_from([xs(k, t0, n) for k in range(KC)], n, D, r, sq)
                for k in range(KC):
                    P.stt(nT.s(k * TP + tl, k * TP + tl + n), xs(k, t0, n), vcol(l, gname, k), r,
                          ALU.mult, ALU.mult)

        def postnorm_residual(l, gname, tiles, T0, TP, ytmp, sq, rstd, half):
            for (t0, n) in tiles:
                tl = t0 - T0
                r = rstd.s(0, n)
                ys = [ytmp.s(k * TP + tl, k * TP + tl + n) for k in range(KC)]
                rstd_from(ys, n, D, r, sq, post=0.5 if half else 1.0)
                for k in range(KC):
                    P.stt(ys[k], ys[k], vcol(l, gname, k), r, ALU.mult, ALU.mult)
                    P.tt(xs(k, t0, n), xs(k, t0, n), ys[k], ALU.add)

        def ffn_pair(l, f, next_pre=None, skip_preA=False, nTA_alt=False, chain_next=None):
            pre, post = ("f1pre", "f1post") if f == 0 else ("f2pre", "f2post")
            (TA0, TPA, tilesA), (TB0, TPB, tilesB) = PASSES
            AR.top = 0
            ytmp = AR.alloc(KC * TPB, F32, "ytmp")
            nTA = AR.at(ytmp.base32, KC * TPA, BF16, "nTA")
            wgu = [AR.alloc(KC * 256, BF16, "wgu%d" % i) for i in range(3)]
            assert AR.top <= 12288
            nTB = AR.alloc(KC * TPB, BF16, "nTB")
            nTA2 = AR.at(nTB.base32, KC * TPA, BF16, "nTA2")
            if nTA_alt:
                nTA = nTA2
            hT = AR.alloc(FC * TPB, BF16, "hT")
            wdn = [AR.alloc(FC * 128, BF16, "wdn%d" % i) for i in range(3)]
            sq = AR.alloc(KC * 512, BF16, "sq")
            rstd = AR.alloc(512, F32, "rstd")
            sg = [AR.alloc(512, F32, "sg%d" % i) for i in range(2)]
            cnt = {"s": 0}

            def up(T0, TP, tiles, nT, hook=None):
                for c in range(FC):
                    w = wgu[c % 3]
                    wload(w.all(), d_wgu[l, f, c])
                    for (t0, n) in tiles:
                        tl = t0 - T0
                        pg = bank().s(0, n)
                        pu = bank().s(0, n)
                        P.mm(pg, [(w.s(k * 256, k * 256 + 128), nT.s(k * TP + tl, k * TP + tl + n)) for k in range(KC)])
                        P.mm(pu, [(w.s(k * 256 + 128, k * 256 + 256), nT.s(k * TP + tl, k * TP + tl + n)) for k in range(KC)])
                        s_ = sg[cnt["s"] % 2].s(0, n)
                        cnt["s"] += 1
                        P.act(s_, pg, AF.Silu)
                        P.tt(hT.s(c * TP + tl, c * TP + tl + n), pu, s_, ALU.mult)
                    if hook is not None:
                        hook(c)

            def down(T0, TP, tiles):
                for m in range(8):
                    w = wdn[m % 3]
                    wload(w.all(), d_wdn[l, f, m])
                    for (t0, n) in tiles:
                        tl = t0 - T0
                        py = bank().s(0, n)
                        P.mm(py, [(w.s(k * 128, k * 128 + 128), hT.s(k * TP + tl, k * TP + tl + n)) for k in range(FC)])
                        P.act(ytmp.s(m * TP + tl, m * TP + tl + n), py, AF.Copy)

            def hookA(c):
                if c % 4 == 2 and c // 4 < len(tilesA):
                    postnorm_residual(l, post, [tilesA[c // 4]], TA0, TPA, ytmp, sq, rstd, True)

            P.mark("L%d ffn%d upA" % (l, f))
            if not skip_preA:
                prenorm(l, pre, tilesA, TA0, TPA, nTA, sq, rstd)
            up(TA0, TPA, tilesA, nTA)
            P.mark("L%d ffn%d downA" % (l, f))
            prenorm(l, pre, tilesB, TB0, TPB, nTB, sq, rstd)
            down(TA0, TPA, tilesA)
            P.mark("L%d ffn%d upB" % (l, f))
            up(TB0, TPB, tilesB, nTB, hook=hookA)
            P.mark("L%d ffn%d downB" % (l, f))
            down(TB0, TPB, tilesB)
            P.mark("L%d ffn%d end" % (l, f))
            if next_pre is not None:
                next_pre()
            if chain_next is not None:
                prenorm(chain_next, "f1pre", tilesA, TA0, TPA, nTA2, sq, rstd)
            postnorm_residual(l, post, tilesB, TB0, TPB, ytmp, sq, rstd, True)

        def mixer_layer(l, early_f2=None):
            AR.top = 0
            kT = AR.alloc(NH * SEQ, BF16, "kT")
            Vt = AR.alloc(16 * 512, BF16, "V")
            carryb = AR.alloc(2 * PADB, BF16, "carryb")
            carryd = AR.alloc(2 * PADD, BF16, "carryd")
            base_top = AR.top
            for pi, (T0, TP, tiles) in enumerate(PASSES):
                AR.top = base_top
                for _ in mixer_pass(l, pi, T0, TP, tiles, kT, Vt, carryb, carryd, early_f2):
                    if pi == 0:
                        yield

        def mixer_pass(l, pi, T0, TP, tiles, kT, Vt, carryb, carryd, early_f2=None):
            isB = pi == 1
            ptiles = [t for t in tiles if t[0] < SEQ]
            stile = (SEQ, DSEQ) if isB else None
            PT0 = T0
            PTN = sum(n for (_, n) in ptiles)
            ybase = AR.top
            nT = AR.alloc(KC * TP, BF16, "nT")
            aT = AR.alloc(4 * TP, BF16, "aT")
            ybT = AR.alloc(2 * TP, BF16, "ybT")
            ycT = AR.alloc(2 * TP, BF16, "ycT")
            ydT = AR.alloc(2 * TP, BF16, "ydT")
            need = (KC * TP * 4 - (AR.top - ybase) * 4)
            if need > 0:
                AR.alloc(need // 4, F32, "ypad")
            ytmp = AR.at(ybase, KC * TP, F32, "ytmp")
            sq = AR.alloc(KC * 512, BF16, "sq")
            rstd = AR.alloc(512, F32, "rstd")
            tf = [AR.alloc(512, F32, "tf%d" % i) for i in range(4)]
            tabc = AR.alloc(512, F32, "tabc")
            tabs = AR.alloc(512, F32, "tabs")
            stage_top = AR.top

            def ntk(k, tl, n):
                return nT.s(k * TP + tl, k * TP + tl + n)

            prenorm(l, "mpre", tiles, T0, TP, nT, sq, rstd)
            yield
            P.mark("L%d mix%d st2" % (l, pi))

            cqn = AR.alloc(3 * TP, BF16, "cqn")
            att_top = AR.top
            latb = AR.alloc(2 * TP, BF16, "latb")
            krs = AR.alloc(DSEQ, BF16, "krs")
            wsl = AR.alloc(2 * KC * 96, BF16, "wslot")
            wsl2 = AR.alloc(KC * KVL, BF16, "wslot2")
            latf = AR.alloc(2 * 512, F32, "latf")
            krf = AR.alloc(512, F32, "krf")
            cqf = AR.at(tf[1].base32, 3 * 512, F32, "cqf")
            r = None
            wUK = AR.at(wsl2.base32, 2 * NH * 64, BF16, "wUK")
            wUV = AR.at(wsl2.base32 + 512, 2 * 512, BF16, "wUV")

            def load_cd():
                wload(wsl.s(0, KC * 96), d_wkr[l, 0])
                wload(wsl.s(KC * 96, 2 * KC * 96), d_wkr[l, 1])
                wload(wUK.all(), d_wuk[l])
                wload(wUV.all(), d_wuv2[l])
            assert wsl2.base32 == wsl.base32 + 768
            if pi == 0:
                wslA = AR.at(Vt.base32 + 128, KC * QL, BF16, "wslA")
                wslB = AR.at(Vt.base32 + 128 + 1536, KC * KVL, BF16, "wslB")
            else:
                wslA = AR.at(wsl.base32, KC * QL, BF16, "wslA")
                wslB = AR.at(wsl.base32, KC * KVL, BF16, "wslB")
            wload(wslA.all(), d_wcq[l])
            if pi == 0:
                wload(wslB.all(), d_wckv[l])
                load_cd()
            for (t0, n) in tiles:
                tl = t0 - T0
                for mc in range(3):
                    ps = bank().s(0, n)
                    P.mm(ps, [(wslA.s(k * QL + mc * 128, k * QL + mc * 128 + 128), ntk(k, tl, n)) for k in range(KC)])
                    P.act(cqf.s(mc * 512, mc * 512 + n), ps, AF.Copy)
                r = rstd.s(0, n)
                rstd_from([cqf.s(mc * 512, mc * 512 + n) for mc in range(3)], n, QL, r, sq)
                for mc in range(3):
                    P.stt(cqn.s(mc * TP + tl, mc * TP + tl + n), cqf.s(mc * 512, mc * 512 + n),
                          vcol(l, "qn", mc), r, ALU.mult, ALU.mult)
            if pi == 1:
                wload(wslB.all(), d_wckv[l])
            for (t0, n) in tiles:
                tl = t0 - T0
                r = rstd.s(0, n)
                for mc in range(2):
                    ps = bank().s(0, n)
                    P.mm(ps, [(wslB.s(k * KVL + mc * 128, k * KVL + mc * 128 + 128), ntk(k, tl, n)) for k in range(KC)])
                    P.act(latf.s(mc * 512, mc * 512 + n), ps, AF.Copy)
                rstd_from([latf.s(mc * 512, mc * 512 + n) for mc in range(2)], n, KVL, r, sq)
                for mc in range(2):
                    P.stt(latf.s(mc * 512, mc * 512 + n), latf.s(mc * 512, mc * 512 + n),
                          vcol(l, "kvn", mc), r, ALU.mult, ALU.mult)
                    P.act(latb.s(mc * TP + tl, mc * TP + tl + n), latf.s(mc * 512, mc * 512 + n), AF.Copy)
                    P.dma("sp", d_lat_out(l, mc, t0, n), latf.s(mc * 512, mc * 512 + n))
            if pi == 1:
                load_cd()
            for (t0, n) in tiles:
                tl = t0 - T0
                samp = t0 >= SEQ
                P.dma("sp", tabc.s(0, n, 64, 96), d_cos[:, t0:t0 + n])
                P.dma("sp", tabs.s(0, n, 64, 96), d_sin[:, t0:t0 + n])
                pa = bank().s(0, n, 0, 96)
                pb = bank().s(0, n, 0, 96)
                P.mm(pa, [(wsl.s(k * 96, k * 96 + 96), ntk(k, tl, n)) for k in range(KC)])
                P.mm(pb, [(wsl.s(KC * 96 + k * 96, KC * 96 + k * 96 + 96), ntk(k, tl, n)) for k in range(KC)])
                t1 = tf[0].s(0, n, 64, 96)
                P.tt(t1, bank_rows(pa, 64, 96), tabc.s(0, n, 64, 96), ALU.mult)
                P.tt(krf.s(0, n, 64, 96), bank_rows(pb, 64, 96), tabs.s(0, n, 64, 96), ALU.mult)
                P.tt(krf.s(0, n, 64, 96), krf.s(0, n, 64, 96), t1, ALU.add)
                P.dma("sp", o_kr[l, :, t0:t0 + n], krf.s(0, n, 64, 96))
                if samp:
                    P.copy(krs.s(0, n, 64, 96), krf.s(0, n, 64, 96), eng="dve")
                else:
                    for h in range(NH):
                        dst = kT.s(h * SEQ + t0, h * SEQ + t0 + n, 64, 96)
                        if h % 2 == 0:
                            P.act(dst, krf.s(0, n, 64, 96), AF.Copy)
                        else:
                            P.copy(dst, krf.s(0, n, 64, 96), eng="dve")
            for (t0, n) in ptiles:
                tl = t0 - T0
                for h in range(NH):
                    ps = bank().s(0, n, 0, 64)
                    P.mm(ps, [(wUK.s((k * NH + h) * 64, (k * NH + h) * 64 + 64), latb.s(k * TP + tl, k * TP + tl + n))
                              for k in range(2)])
                    dst = kT.s(h * SEQ + t0, h * SEQ + t0 + n, 0, 64)
                    if h % 2 == 0:
                        P.copy(dst, ps, eng="dve")
                    else:
                        P.act(dst, ps, AF.Copy)
                for bi in range(n // 128):
                    b = (t0 + bi * 128) // 128
                    ps = bank().all()
                    P.mm(ps, [(latb.s(k * TP + tl + bi * 128, k * TP + tl + bi * 128 + 128), wUV.s(k * 512, k * 512 + 512))
                              for k in range(2)])
                    if bi % 2 == 0:
                        P.act(Vt.s(b * 512, b * 512 + 512), ps, AF.Copy)
                    else:
                        P.copy(Vt.s(b * 512, b * 512 + 512), ps, eng="dve")

            if stop <= 2:
                return
            P.mark("L%d mix%d att" % (l, pi))
            AR.top = att_top
            latb2 = AR.alloc(2 * TP, BF16, "latb")
            krs2 = AR.alloc(DSEQ, BF16, "krs")
            assert latb2.base32 == latb.base32 and krs2.base32 == krs.base32
            wUQ = [AR.alloc(3 * 768, BF16, "wUQ%d" % i) for i in range(2)]
            qT = AR.alloc(NH * 512, BF16, "qT")
            Pt = [AR.at(sq.base32 + i * 256, 512, BF16, "Pt%d" % i) for i in range(4)]
            rden = [tf[2], tf[3]]
            wload(wUQ[0].all(), d_wuq[l, 0])
            wload(wUQ[1].all(), d_wuq[l, 1])

            def load_tabs(t0, n):
                P.dma("sp", tabc.s(0, n, 64, 96), d_cos[:, t0:t0 + n])
                P.dma("sp", tabs.s(0, n, 64, 96), d_sin[:, t0:t0 + n])

            def q_head(t0, n, tl, qdst, qstride, h):
                pa = bank(0, 4).s(0, n, 0, 96)
                pb = bank(0, 4).s(0, n, 0, 96)
                P.mm(pa, [(wUQ[0].s(k * 768 + h * 96, k * 768 + h * 96 + 96), cqn.s(k * TP + tl, k * TP + tl + n))
                          for k in range(3)])
                P.mm(pb, [(wUQ[1].s(k * 768 + h * 96, k * 768 + h * 96 + 96), cqn.s(k * TP + tl, k * TP + tl + n))
                          for k in range(3)])
                P.copy(qdst.s(h * qstride, h * qstride + n, 0, 64), bank_rows(pa, 0, 64), eng="dve")
                t1 = tf[0].s(0, n, 64, 96)
                t2 = tf[1].s(0, n, 64, 96)
                P.tt(t1, bank_rows(pa, 64, 96), tabc.s(0, n, 64, 96), ALU.mult)
                P.tt(t2, bank_rows(pb, 64, 96), tabs.s(0, n, 64, 96), ALU.mult)
                P.tt(qdst.s(h * qstride, h * qstride + n, 64, 96), t1, t2, ALU.add, eng="pool")

            qs = AR.alloc(NH * DSEQ, BF16, "qs") if isB else None
            pcount = {"i": 0}
            for ti, (t0, n) in enumerate(ptiles):
                tl = t0 - T0
                gi = t0 // 512
                if ti == 0:
                    load_tabs(t0, n)
                    for h in range(NH):
                        q_head(t0, n, tl, qT, 512, h)
                if ti + 1 < len(ptiles):
                    nxt = (ptiles[ti + 1][0], ptiles[ti + 1][1], ptiles[ti + 1][0] - T0, qT, 512)
                elif isB:
                    nxt = (stile[0], stile[1], stile[0] - T0, qs, DSEQ)
                else:
                    nxt = None
                if nxt is not None:
                    load_tabs(nxt[0], nxt[1])
                nblk = 4 * gi + 4
                items = [(h, j) for h in range(NH) for j in range(nblk)]
                pts = {}

                def issue_scores(idx):
                    h, j = items[idx]
                    m = j - 4 * gi
                    c0 = 0 if m < 0 else 128 * m
                    ps = bank(0, 4).s(c0, 512)
                    P.mm(ps, [(kT.s(h * SEQ + j * 128, h * SEQ + j * 128 + 128, 0, 96),
                               qT.s(h * 512 + c0, h * 512 + 512, 0, 96))])
                    pt = Pt[pcount["i"] % 4]
                    pcount["i"] += 1
                    P.act(pt.s(c0, 512), ps, AF.Exp, scale=SM_SCALE)
                    if m >= 0:
                        P.memset(pt.s(c0, c0 + 64, 64, 128), 0.0, eng="pool" if gi == 0 else "dve")
                    pts[idx] = (pt, c0)

                def issue_pv(idx):
                    h, j = items[idx]
                    hp, half = h // 2, h % 2
                    acc = banks[4 + (h % 2) * 2]
                    den = banks[5 + (h % 2) * 2]
                    pt, c0 = pts.pop(idx)
                    P.mm(acc.s(c0, 512), [(Vt.s(j * 512 + hp * 128, j * 512 + hp * 128 + 128), pt.s(c0, 512))],
                         start=(j == 0), stop=(j == nblk - 1))
                    P.mm(den.s(c0, 512), [(onesb.all(), pt.s(c0, 512))],
                         start=(j == 0), stop=(j == nblk - 1))
                    if j == nblk - 1:
                        def fin(h=h, hp=hp, half=half, acc=acc, den=den):
                            r0, r1 = half * 64, half * 64 + 64
                            rd = rden[h % 2]
                            if nblk <= 8:
                                P.act(rd.s(0, 512, r0, r1), den.s(0, 512, r0, r1), AF.Ln)
                                P.act(rd.s(0, 512, r0, r1), rd.s(0, 512, r0, r1), AF.Exp, scale=-1.0)
                            else:
                                recip(rd.s(0, 512, r0, r1), den.s(0, 512, r0, r1))
                            P.tt(aT.s(hp * TP + tl, hp * TP + tl + n, r0, r1), acc.s(0, 512, r0, r1), rd.s(0, 512, r0, r1), ALU.mult)
                        pend.append(fin)
                        if nxt is not None:
                            q_head(nxt[0], nxt[1], nxt[2], nxt[3], nxt[4], h)

                LA = 3
                pend = []
                for idx in range(len(items) + LA):
                    if idx < len(items):
                        issue_scores(idx)
                        if pend:
                            pend.pop(0)()
                    if idx >= LA:
                        issue_pv(idx - LA)
                while pend:
                    pend.pop(0)()

            if stop <= 3:
                return
            sample_loads = sample_compute = None
            if isB:
                t0, n = stile
                tl = t0 - T0
                kb = Vt.base32 + 15 * 256
                lat_s = AR.at(kb, 2 * DSEQ, BF16, "lat_s")
                krs_k = AR.at(kb + 32, DSEQ, BF16, "krs_k")
                qs_k = AR.at(kb + 64, NH * DSEQ, BF16, "qs_k")
                for kc in range(2):
                    P.copy(lat_s.s(kc * DSEQ, (kc + 1) * DSEQ), latb.s(kc * TP + tl, kc * TP + tl + n), eng="dve")
                P.copy(krs_k.s(0, n, 64, 96), krs.s(0, n, 64, 96), eng="dve")
                P.copy(qs_k.s(0, NH * DSEQ, 0, 96), qs.s(0, NH * DSEQ, 0, 96), eng="dve")
                sb = {}

                def sample_loads():
                    mark = AR.top
                    AR.top = kT.base32
                    sb["clatT"] = AR.alloc(2 * SEQ, BF16, "clatT")
                    sb["clat"] = AR.alloc(16 * 256, BF16, "clat")
                    sb["ckr"] = AR.alloc(SEQ, BF16, "ckr")
                    sb["wUKT"] = AR.alloc(NH * 256, BF16, "wUKT")
                    sb["qabs"] = AR.alloc(2 * 128, BF16, "qabs")
                    sb["latn"] = AR.alloc(256, BF16, "latn")
                    sb["olat"] = AR.alloc(2 * 128, BF16, "olat")
                    sb["pts"] = [AR.alloc(128, BF16, "pts%d" % i) for i in range(4)]
                    sb["wUVs"] = AR.alloc(2 * 512, BF16, "wUVs")
                    assert AR.top <= kb
                    AR.top = mark
                    wload(sb["wUVs"].all(), d_wuv2[l])
                    wload(sb["clatT"].all(), d_clatT[l])
                    wload(sb["clat"].all(), d_clat[l])
                    wload(sb["ckr"].s(0, SEQ, 64, 96), d_ckr[l])
                    wload(sb["wUKT"].s(0, NH * 256, 0, 64), d_wukT[l])

                def sample_compute():
                    clatT, clat, ckr, wUKT = sb["clatT"], sb["clat"], sb["ckr"], sb["wUKT"]
                    qabs, latn, olat, pts, wUVs = sb["qabs"], sb["latn"], sb["olat"], sb["pts"], sb["wUVs"]
                    pq = banks[0]
                    for h in range(NH):
                        for kc in range(2):
                            P.mm(pq.s(kc * 128 + h * 16, kc * 128 + h * 16 + 16),
                                 [(wUKT.s(h * 256 + kc * 128, h * 256 + kc * 128 + 128, 0, 64), qs_k.s(h * 16, h * 16 + 16, 0, 64))])
                    P.act(qabs.all(), pq.s(0, 256), AF.Copy)
                    pl = banks[1]
                    for kc in range(2):
                        P.mm(pl.s(kc * 128, kc * 128 + 128, 0, 16),
                             [(lat_s.s(kc * DSEQ, (kc + 1) * DSEQ), identb.all())])
                    P.copy(latn.s(0, 256, 0, 16), pl.s(0, 256, 0, 16), eng="dve")
                    ol = [banks[4], banks[5]]
                    dn = banks[6]
                    for j in range(17):
                        new = j == 16
                        kk = 16 if new else 128
                        ps = banks[2 + j % 2].s(0, 128, 0, kk)
                        if new:
                            pairs = [(lat_s.s(kc * DSEQ, (kc + 1) * DSEQ), qabs.s(kc * 128, kc * 128 + 128)) for kc in range(2)]
                            pairs.append((krs_k.s(0, n, 64, 96), qs_k.s(0, 128, 64, 96)))
                        else:
                            pairs = [(clatT.s(kc * SEQ + j * 128, kc * SEQ + j * 128 + 128), qabs.s(kc * 128, kc * 128 + 128))
                                     for kc in range(2)]
                            pairs.append((ckr.s(j * 128, j * 128 + 128, 64, 96), qs_k.s(0, 128, 64, 96)))
                        P.mm(ps, pairs)
                        pt = pts[j % 4].s(0, 128, 0, kk)
                        P.act(pt, ps, AF.Exp, scale=SM_SCALE)
                        for kc in range(2):
                            lhs = latn.s(kc * 128, kc * 128 + 128, 0, 16) if new else clat.s(j * 256 + kc * 128, j * 256 + kc * 128 + 128)
                            P.mm(ol[kc].s(0, 128), [(lhs, pt)], start=(j == 0), stop=new)
                        P.mm(dn.s(0, 128), [(onesb.s(0, 128, 0, kk), pt)], start=(j == 0), stop=new)
                    for kc in range(2):
                        P.act(olat.s(kc * 128, kc * 128 + 128), ol[kc].s(0, 128), AF.Copy)
                    rd = rden[0]
                    recip(rd.s(0, 128), dn.s(0, 128))
                    for h in range(NH):
                        hp, half = h // 2, h % 2
                        r0, r1 = half * 64, half * 64 + 64
                        ps = banks[h % 2].s(256, 256 + 16)
                        P.mm(ps, [(wUVs.s(kc * 512 + hp * 128, kc * 512 + hp * 128 + 128), olat.s(kc * 128 + h * 16, kc * 128 + h * 16 + 16))
                                  for kc in range(2)])
                        P.tt(aT.s(hp * TP + tl, hp * TP + tl + n, r0, r1), bank_rows(ps, r0, r1), rd.s(h * 16, h * 16 + 16, r0, r1), ALU.mult)

            if stop <= 4:
                return
            P.mark("L%d mix%d brB" % (l, pi))
            AR.top = stage_top
            branch_b(l, pi, T0, TP, tiles, ptiles, stile, nT, ybT, carryb, sq, rstd, tf + [tabc, tabs])
            if sample_loads is not None:
                sample_loads()
            if stop <= 5:
                return
            P.mark("L%d mix%d brC" % (l, pi))
            AR.top = stage_top
            branch_c(l, pi, T0, TP, tiles, ptiles, stile, nT, ycT, sq, rstd, tf + [tabc, tabs])
            if stop <= 6:
                return
            P.mark("L%d mix%d brD" % (l, pi))
            AR.top = stage_top
            branch_d(l, pi, T0, TP, tiles, ptiles, stile, nT, ydT, carryd, tf)
            if sample_compute is not None:
                sample_compute()
            if isB and early_f2 is not None:
                early_f2(sq, rstd)

            if stop <= 7:
                return
            P.mark("L%d mix%d merge" % (l, pi))
            AR.top = stage_top
            mrg = AR.alloc(KC * TP, BF16, "merged")
            wG = [AR.alloc(KC * 512 + 10 * 128, BF16, "wG0"),
                  AR.at(sq.base32, KC * 512 + 10 * 128, BF16, "wG1")]
            assert sq.base32 + 2688 <= tf[1].base32
            wO = [AR.at(wG[0].base32 + i * 512, KC * 128, BF16, "wO%d" % i) for i in range(5)]
            gsb = [tabc, tabs]
            brs = ((aT, 4), (ybT, 2), (ycT, 2), (ydT, 2))
            for m in range(8):
                w = wG[m % 2]
                wload(w.all(), d_wgm[l, m])
                for (t0, n) in tiles:
                    tl = t0 - T0
                    kb0 = 0
                    for b, (src, nk) in enumerate(brs):
                        pg = bank().s(0, n)
                        pp = bank().s(0, n)
                        P.mm(pg, [(w.s((k * 4 + b) * 128, (k * 4 + b) * 128 + 128), ntk(k, tl, n)) for k in range(KC)])
                        P.mm(pp, [(w.s(KC * 512 + (kb0 + kk) * 128, KC * 512 + (kb0 + kk) * 128 + 128),
                                   src.s(kk * TP + tl, kk * TP + tl + n)) for kk in range(nk)])
                        kb0 += nk
                        g = gsb[b % 2].s(0, n)
                        P.act(g, pg, AF.Sigmoid)
                        if b == 0:
                            P.tt(tf[2].s(0, n), pp, g, ALU.mult)
                        else:
                            P.tt(tf[3].s(0, n), pp, g, ALU.mult)
                            dst = tf[2].s(0, n) if b < 3 else mrg.s(m * TP + tl, m * TP + tl + n)
                            P.tt(dst, tf[2].s(0, n), tf[3].s(0, n), ALU.add)
            P.mark("L%d mix%d wo" % (l, pi))
            for m in range(8):
                w = wO[m % 5]
                wload(w.all(), d_wo[l, m])
                for (t0, n) in tiles:
                    tl = t0 - T0
                    py = bank().s(0, n)
                    P.mm(py, [(w.s(k * 128, k * 128 + 128), mrg.s(k * TP + tl, k * TP + tl + n)) for k in range(KC)])
                    P.act(ytmp.s(m * TP + tl, m * TP + tl + n), py, AF.Copy)
            postnorm_residual(l, "mpost", tiles, T0, TP, ytmp, sq, rstd, False)

        def bank_rows(ref, p0, p1):
            return Ref(ref.ap[p0:p1] if p0 == 0 and False else ref.ap[p0:p1, :], ref.atoms)

        def d_lat_out(l, mc, t0, n):
            return o_lat[l, :, mc * NT + t0: mc * NT + t0 + n]


        def reduce_sum(out, in_):
            P.record("dve", (lambda e, o=out, i=in_: e.tensor_reduce(o.ap, i.ap, mybir.AxisListType.X, ALU.add)),
                     [in_], [out])

        def branch_b(l, pi, T0, TP, tiles, ptiles, stile, nT, ybT, carryb, sq, rstd, tf):
            PTN = sum(n for (_, n) in ptiles)
            W = PADB + PTN
            WS = PADB + DSEQ
            wgl = AR.alloc(KC * 512, BF16, "wgl")
            wload(wgl.all(), d_wglu[l])
            diag = AR.alloc(CBW * 2 * 128, BF16, "diagb")
            xbp = AR.alloc(2 * W, BF16, "xbp")
            xbs = AR.alloc(2 * WS, BF16, "xbs")
            cbfs = [rstd, tf[3]]
            mean = tf[4]
            rs = tf[5]
            cbst = AR.alloc(2 * 2 * PADB, F32, "cbst")
            for j in range(CBW):
                for c in range(2):
                    P.ts(diag.s((j * 2 + c) * 128, (j * 2 + c) * 128 + 128), identf.all(), vcol(l, "cbw", j * 2 + c), ALU.mult)
            for c in range(2):
                if pi == 0:
                    P.memset(xbp.s(c * W, c * W + PADB), 0.0, eng="dve")
                else:
                    P.copy(xbp.s(c * W, c * W + PADB), carryb.s(c * PADB, (c + 1) * PADB), eng="dve")
                    wload(xbs.s(c * WS, c * WS + PADB), d_scb[l][:, c * PADB:(c + 1) * PADB])
                    P.dma("sp", cbst.s(c * 60 + 30, c * 60 + 44), d_scb[l][:, c * PADB + 16:c * PADB + 30])
            def phA(t0, n):
                tl = t0 - T0
                samp = t0 >= SEQ
                for c in range(2):
                    pa = bank().s(0, n)
                    pg = bank().s(0, n)
                    P.mm(pa, [(wgl.s(k * 512 + c * 128, k * 512 + c * 128 + 128), nT.s(k * TP + tl, k * TP + tl + n)) for k in range(KC)])
                    P.mm(pg, [(wgl.s(k * 512 + 256 + c * 128, k * 512 + 256 + c * 128 + 128), nT.s(k * TP + tl, k * TP + tl + n)) for k in range(KC)])
                    sg = tf[c].s(0, n)
                    P.act(sg, pg, AF.Sigmoid)
                    if samp:
                        dst = xbs.s(c * WS + PADB, c * WS + PADB + n)
                        P.tt(cbst.s(c * 60 + 44, c * 60 + 60), pa, sg, ALU.mult)
                    else:
                        dst = xbp.s(c * W + PADB + tl, c * W + PADB + tl + n)
                        if t0 + n == SEQ:
                            P.tt(cbst.s(c * 60, c * 60 + 30), Ref(pa.ap[:, n - 30:n], pa.atoms), tf[c].s(n - 30, n), ALU.mult)
                    P.tt(dst, pa, sg, ALU.mult)

            def phB(t0, n):
                tl = t0 - T0
                samp = t0 >= SEQ
                for c in range(2):
                    ps = bank().s(0, n)
                    if samp:
                        P.mm(ps, [(diag.s((j * 2 + c) * 128, (j * 2 + c) * 128 + 128), xbs.s(c * WS + j, c * WS + j + n)) for j in range(CBW)])
                    else:
                        P.mm(ps, [(diag.s((j * 2 + c) * 128, (j * 2 + c) * 128 + 128), xbp.s(c * W + tl + j, c * W + tl + j + n)) for j in range(CBW)])
                    P.act(cbfs[c].s(0, n), ps, AF.Identity, bias=vcol(l, "cbb", c))
                    P.act(sq.s(c * 512, c * 512 + n), cbfs[c].s(0, n), AF.Copy)
                    P.act(sq.s((2 + c) * 512, (2 + c) * 512 + n), cbfs[c].s(0, n), AF.Square)

            def phC(t0, n):
                tl = t0 - T0
                pm = bank().s(0, n)
                pq = bank().s(0, n)
                P.mm(pm, [(onesb.all(), sq.s(c * 512, c * 512 + n)) for c in range(2)])
                P.mm(pq, [(onesb.all(), sq.s((2 + c) * 512, (2 + c) * 512 + n)) for c in range(2)])
                mn = mean.s(0, n)
                r = rs.s(0, n)
                m2 = tf[2].s(0, n)
                P.ts(mn, pm, 1.0 / 256, ALU.mult)
                P.tt(m2, mn, mn, ALU.mult)
                P.stt(r, pq, 1.0 / 256, m2, ALU.mult, ALU.subtract)
                P.act(r, r, AF.Ln, bias=epsb[1.0])
                P.act(r, r, AF.Exp, scale=-0.5)
                for c in range(2):
                    x = cbfs[c].s(0, n)
                    P.tt(x, x, mn, ALU.subtract)
                    P.tt(x, x, r, ALU.mult)
                    P.act(ybT.s(c * TP + tl, c * TP + tl + n), x, AF.Silu, scale=vcol(l, "cblg", c), bias=vcol(l, "cblb", c))

            phA(*tiles[0])
            phB(*tiles[0])
            for ti in range(1, len(tiles)):
                phA(*tiles[ti])
                phC(*tiles[ti - 1])
                phB(*tiles[ti])
            phC(*tiles[-1])
            if pi == 0:
                for c in range(2):
                    P.copy(carryb.s(c * PADB, (c + 1) * PADB), xbp.s(c * W + PTN, c * W + PTN + PADB), eng="dve")
            else:
                P.dma("sp", o_cb[l], cbst.all())

        def branch_c(l, pi, T0, TP, tiles, ptiles, stile, nT, ycT, sq, rstd, tf):
            wuv = AR.alloc(KC * 512, BF16, "wuv")
            wload(wuv.all(), d_wuv[l])
            wsT = AR.alloc(512, BF16, "wsT")
            wload(wsT.all(), d_wsT[l])
            bsr = AR.alloc(512, BF16, "bsr")
            wload(bsr.s(0, 512, 0, 1), d_bs[l])
            for g in range(4):
                P.memset(wsT.s(g * 128, g * 128 + 64, 64, 128), 0.0, eng="dve")
            uT = AR.alloc(2 * TP, BF16, "uT")
            vnT = AR.alloc(2 * 512, BF16, "vnT")
            vtm = [AR.alloc(256, BF16, "vtm%d" % i) for i in range(2)]
            gvst = AR.alloc(2 * DSEQ, F32, "gvst")
            vfs = [rstd, tf[3]]
            mean, rs = tf[0], tf[1]
            bcount = 0
            def phA(ti, t0, n):
                tl = t0 - T0
                vf_ = vfsets[ti % 2]
                so = (ti % 2) * 4
                for c in range(2):
                    pu = bank().s(0, n)
                    P.mm(pu, [(wuv.s(k * 512 + c * 128, k * 512 + c * 128 + 128), nT.s(k * TP + tl, k * TP + tl + n)) for k in range(KC)])
                    P.act(uT.s(c * TP + tl, c * TP + tl + n), pu, AF.Copy)
                for c in range(2):
                    pv = bank().s(0, n)
                    P.mm(pv, [(wuv.s(k * 512 + 256 + c * 128, k * 512 + 256 + c * 128 + 128), nT.s(k * TP + tl, k * TP + tl + n)) for k in range(KC)])
                    P.act(vf_[c].s(0, n), pv, AF.Copy)
                    P.act(sq.s((so + c) * 512, (so + c) * 512 + n), vf_[c].s(0, n), AF.Copy)
                    P.act(sq.s((so + 2 + c) * 512, (so + 2 + c) * 512 + n), vf_[c].s(0, n), AF.Square)

            def phC(ti, t0, n):
                tl = t0 - T0
                samp = t0 >= SEQ
                vf_ = vfsets[ti % 2]
                so = (ti % 2) * 4
                pm = bank().s(0, n)
                pq = bank().s(0, n)
                P.mm(pm, [(onesb.all(), sq.s((so + c) * 512, (so + c) * 512 + n)) for c in range(2)])
                P.mm(pq, [(onesb.all(), sq.s((so + 2 + c) * 512, (so + 2 + c) * 512 + n)) for c in range(2)])
                mn = mean.s(0, n)
                r = rs.s(0, n)
                m2 = tf[2].s(0, n)
                P.ts(mn, pm, 1.0 / 256, ALU.mult)
                P.tt(m2, mn, mn, ALU.mult)
                P.stt(r, pq, 1.0 / 256, m2, ALU.mult, ALU.subtract)
                P.act(r, r, AF.Ln, bias=epsb[1.0])
                P.act(r, r, AF.Exp, scale=-0.5)
                for c in range(2):
                    x = vf_[c].s(0, n)
                    P.tt(x, x, mn, ALU.subtract)
                    P.tt(x, x, r, ALU.mult)
                    if samp:
                        P.act(gvst.s(c * DSEQ, (c + 1) * DSEQ), x, AF.Identity, scale=vcol(l, "vng", c), bias=vcol(l, "vnb", c))
                    P.act(vnT.s(c * 512, c * 512 + n), x, AF.Identity, scale=vcol(l, "vng", c), bias=vcol(l, "vnb", c))
                if samp:
                    P.dma("sp", o_gv[l], gvst.all())
                bsz = min(n, 128)
                for bi in range(max(1, n // 128)):
                    c0 = tl + bi * 128
                    v16 = vtm[bc["i"] % 2]
                    bc["i"] += 1
                    pt = bank().s(0, 256, 0, bsz)
                    for ch in range(2):
                        P.mm(Ref(pt.ap[:, ch * 128:(ch + 1) * 128], pt.atoms),
                             [(vnT.s(ch * 512 + bi * 128, ch * 512 + bi * 128 + bsz), identb.all())])
                    P.copy(v16.s(0, 256, 0, bsz), pt, eng="dve")
                    for ch in range(2):
                        for gg in range(2):
                            g = 2 * ch + gg
                            psg = bank().s(0, bsz)
                            P.mm(psg, [(v16.s(ch * 128, ch * 128 + 128, 0, bsz), wsT.s(g * 128, g * 128 + bsz, 0, bsz)),
                                       (onesb.s(0, 128, 0, 1), bsr.s(g * 128, g * 128 + bsz, 0, 1))])
                            r0, r1 = gg * 64, gg * 64 + 64
                            P.tt(ycT.s(ch * TP + c0, ch * TP + c0 + bsz, r0, r1), Ref(psg.ap[r0:r1, :], psg.atoms),
                                 uT.s(ch * TP + c0, ch * TP + c0 + bsz, r0, r1), ALU.mult)

            bc = {"i": 0}
            vfsets = [[rstd, tf[3]], [tf[4], tf[5]]]
            phA(0, *tiles[0])
            for ti in range(1, len(tiles)):
                phA(ti, *tiles[ti])
                phC(ti - 1, *tiles[ti - 1])
            phC(len(tiles) - 1, *tiles[-1])

        def branch_d(l, pi, T0, TP, tiles, ptiles, stile, nT, ydT, carryd, tf):
            PTN = sum(n for (_, n) in ptiles)
            W = PADD + PTN
            WS = PADD + DSEQ
            wdp = AR.alloc(KC * 768, BF16, "wdp")
            wload(wdp.all(), d_wdp[l])
            diag = AR.alloc(CDW * 2 * 128, BF16, "diagd")
            xdp = AR.alloc(2 * W, BF16, "xdp")
            xds = AR.alloc(2 * WS, BF16, "xds")
            bgT = AR.alloc(2 * TP, BF16, "bgT")
            cdst = AR.alloc(2 * 2 * PADD, F32, "cdst")
            for j in range(CDW):
                for c in range(2):
                    P.ts(diag.s((j * 2 + c) * 128, (j * 2 + c) * 128 + 128), identf.all(), vcol(l, "cdw", j * 2 + c), ALU.mult)
            for c in range(2):
                if pi == 0:
                    P.memset(xdp.s(c * W, c * W + PADD), 0.0, eng="dve")
                else:
                    P.copy(xdp.s(c * W, c * W + PADD), carryd.s(c * PADD, (c + 1) * PADD), eng="dve")
                    wload(xds.s(c * WS, c * WS + PADD), d_scd[l][:, c * PADD:(c + 1) * PADD])
            for (t0, n) in tiles:
                tl = t0 - T0
                samp = t0 >= SEQ
                for c in range(2):
                    pb = bank().s(0, n)
                    pc = bank().s(0, n)
                    ph = bank().s(0, n)
                    for ps_, off in ((pb, 0), (pc, 256), (ph, 512)):
                        P.mm(ps_, [(wdp.s(k * 768 + off + c * 128, k * 768 + off + c * 128 + 128), nT.s(k * TP + tl, k * TP + tl + n)) for k in range(KC)])
                    P.act(bgT.s(c * TP + tl, c * TP + tl + n), pb, AF.Copy)
                    cg = tf[c].s(0, n)
                    P.act(cg, pc, AF.Copy)
                    if samp:
                        dst = xds.s(c * WS + PADD, c * WS + PADD + n)
                        P.tt(cdst.s(c * 4 + 2, c * 4 + 4), Ref(ph.ap[:, n - 2:n], ph.atoms), tf[c].s(n - 2, n), ALU.mult)
                    else:
                        dst = xdp.s(c * W + PADD + tl, c * W + PADD + tl + n)
                        if t0 + n == SEQ:
                            P.tt(cdst.s(c * 4, c * 4 + 2), Ref(ph.ap[:, n - 2:n], ph.atoms), tf[c].s(n - 2, n), ALU.mult)
                    P.tt(dst, ph, cg, ALU.mult)
                for c in range(2):
                    ps = bank().s(0, n)
                    if samp:
                        P.mm(ps, [(diag.s((j * 2 + c) * 128, (j * 2 + c) * 128 + 128), xds.s(c * WS + j, c * WS + j + n)) for j in range(CDW)])
                    else:
                        P.mm(ps, [(diag.s((j * 2 + c) * 128, (j * 2 + c) * 128 + 128), xdp.s(c * W + tl + j, c * W + tl + j + n)) for j in range(CDW)])
                    P.tt(ydT.s(c * TP + tl, c * TP + tl + n), ps, bgT.s(c * TP + tl, c * TP + tl + n), ALU.mult)
            if pi == 0:
                for c in range(2):
                    P.copy(carryd.s(c * PADD, (c + 1) * PADD), xdp.s(c * W + PTN, c * W + PTN + PADD), eng="dve")
            else:
                P.dma("sp", o_cd[l], cdst.all())

        for l in range(n_layers):
            early = None
            if do_mixer and do_ffn is True:
                def early(sq_, rstd_, l=l):
                    (TA0, TPA, tilesA) = PASSES[0]
                    prenorm(l, "f2pre", tilesA, TA0, TPA, Buf(P, "A", 0, KC * TPA, BF16, "nTA_early"), sq_, rstd_)
            gen = mixer_layer(l, early) if do_mixer else None
            chained = bool(do_ffn is True and l > 0)
            if do_ffn:
                ffn_pair(l, 0, next_pre=(lambda: next(gen)) if gen is not None else None,
                         skip_preA=chained, nTA_alt=chained)
            elif gen is not None:
                next(gen)
            if gen is not None:
                for _ in gen:
                    pass
            if do_ffn is True:
                ffn_pair(l, 1, skip_preA=(early is not None), chain_next=(l + 1 if l + 1 < n_layers else None))
        for k in range(KC):
            P.dma("sp", o_y[:, k * NT:(k + 1) * NT], xT.s(k * NT, (k + 1) * NT))
        P.barrier_on("sp", [xT.all(), Buf(P, "A", 0, ASIZE, F32).all()])
        P.mark("end")
        P.emit()
        import json as _json
        if os.environ.get("MARKS"):
            _json.dump(P.marks, open(os.environ["MARKS"], "w"))
        print("arena peak", AR.peak, "of", ASIZE, {e: len(P.ops[e]) for e in ENGINES}, "dma sems", len(P.dma_sems))
    return nc


def _slab(W, ms):
    K, M = W.shape
    return np.ascontiguousarray(
        W.reshape(K // 128, 128, M // ms, ms).transpose(2, 1, 0, 3)).reshape(M // ms, 128, (K // 128) * ms)


def _rope_tables():
    half = RD // 2
    inv = np.exp(-math.log(10000.0) * np.arange(half, dtype=np.float32) / half).astype(np.float32)
    pos = np.arange(NT, dtype=np.float32)
    ang = (pos[:, None] * inv[None, :]).astype(np.float32)
    c, s = np.cos(ang).astype(np.float32), np.sin(ang).astype(np.float32)
    cosT = np.concatenate([c, c], axis=1).T
    sinS = np.concatenate([-s, s], axis=1).T
    return np.ascontiguousarray(cosT), np.ascontiguousarray(sinS)


def prep_shared(I):
    f = lambda a: np.asarray(a, dtype=np.float32)
    S = {}
    wgu = np.empty((L, 2, FC, 128, KC * 256), np.float32)
    wdn = np.empty((L, 2, 8, 128, FC * 128), np.float32)
    for l in range(L):
        for fi, (gu, dn) in enumerate((("ffn1_w_gu", "ffn1_w_down"), ("ffn2_w_gu", "ffn2_w_down"))):
            W = f(I[gu][l]).reshape(KC, 128, 2 * DFF)
            G = W[:, :, :DFF].reshape(KC, 128, FC, 128)
            U = W[:, :, DFF:].reshape(KC, 128, FC, 128)
            wgu[l, fi] = np.stack([G, U], axis=3).transpose(2, 1, 0, 3, 4).reshape(FC, 128, KC * 256)
            wdn[l, fi] = _slab(f(I[dn][l]), 128)
    S["wgu"], S["wdn"] = wgu, wdn
    win = f(I["w_in"])
    S["wcq"] = np.stack([_slab(win[l][:, 0:384], 384)[0] for l in range(L)])
    S["wckv"] = np.stack([_slab(win[l][:, 384:640], 256)[0] for l in range(L)])
    wkr = np.zeros((L, 2, 1024, 96), np.float32)
    wkr[:, 0, :, 64:96] = win[:, :, 640:672]
    wkr[:, 1, :, 64:80] = win[:, :, 656:672]
    wkr[:, 1, :, 80:96] = win[:, :, 640:656]
    S["wkr"] = np.stack([np.stack([_slab(wkr[l, i], 96)[0] for i in range(2)]) for l in range(L)])
    S["wglu"] = np.stack([_slab(win[l][:, 672:1184], 512)[0] for l in range(L)])
    S["wuv"] = np.stack([_slab(win[l][:, 1184:1696], 512)[0] for l in range(L)])
    S["wdp"] = np.stack([_slab(win[l][:, 1696:2464], 768)[0] for l in range(L)])
    wgm = np.empty((L, 8, 128, KC * 512 + 10 * 128), np.float32)
    for l in range(L):
        Wg = win[l][:, 2464:].reshape(KC, 128, 4, 8, 128).transpose(3, 1, 0, 2, 4).reshape(8, 128, KC * 512)
        Wbr = np.concatenate([f(I["w_br_a"][l]), f(I["w_br_b"][l]), f(I["w_br_c"][l]), f(I["w_br_d"][l])], axis=0)
        Wb = Wbr.reshape(10, 128, 8, 128).transpose(2, 1, 0, 3).reshape(8, 128, 10 * 128)
        wgm[l] = np.concatenate([Wg, Wb], axis=2)
    S["wgm"] = wgm
    S["wo"] = np.stack([_slab(f(I["w_o"][l]), 128) for l in range(L)])
    wuq = f(I["w_uq"])
    wuq_sw = wuq.reshape(L, QL, NH, 96).copy()
    wuq_sw[..., 64:80] = wuq.reshape(L, QL, NH, 96)[..., 80:96]
    wuq_sw[..., 80:96] = wuq.reshape(L, QL, NH, 96)[..., 64:80]
    wuq_sw = wuq_sw.reshape(L, QL, NH * 96)
    S["wuq"] = np.stack([np.stack([_slab(wuq[l], 768)[0], _slab(wuq_sw[l], 768)[0]]) for l in range(L)])
    wukv = f(I["w_ukv"]).reshape(L, 2, 128, NH, 128)
    S["wuk"] = np.ascontiguousarray(wukv[..., :64].transpose(0, 2, 1, 3, 4)).reshape(L, 128, 2 * NH * 64)
    S["wuv2"] = np.ascontiguousarray(wukv[..., 64:].transpose(0, 2, 1, 3, 4)).reshape(L, 128, 2 * 512)
    w3 = f(I["w_ukv"]).reshape(L, KVL, NH, 128)[..., :64]
    S["wukT"] = np.ascontiguousarray(w3.transpose(0, 3, 2, 1)).reshape(L, 64, NH * 256)
    S["wsT"] = np.ascontiguousarray(f(I["gmlp_w_s"]).transpose(0, 3, 1, 2)).reshape(L, 128, 512)
    S["bsrow"] = np.ascontiguousarray(f(I["gmlp_b_s"])).reshape(L, 1, 512)
    gvb = np.concatenate([f(I["gmlp_vn_g"]), f(I["gmlp_vn_b"])], axis=1)
    S["gvb"] = np.ascontiguousarray(np.broadcast_to(gvb[:, None, :], (L, 128, 512)))
    vecs = np.zeros((128, L * NV), np.float32)

    def put(l, name, v, w):
        v = f(v).reshape(w, 128).T
        vecs[:, l * NV + VOFF[name]: l * NV + VOFF[name] + w] = v
    for l in range(L):
        put(l, "f1pre", I["ffn1_norm_pre"][l], 8)
        put(l, "f1post", I["ffn1_norm_post"][l], 8)
        put(l, "mpre", I["mix_norm_pre"][l], 8)
        put(l, "mpost", I["mix_norm_post"][l], 8)
        put(l, "f2pre", I["ffn2_norm_pre"][l], 8)
        put(l, "f2post", I["ffn2_norm_post"][l], 8)
        put(l, "qn", I["q_norm"][l], 3)
        put(l, "kvn", I["kv_norm"][l], 2)
        put(l, "cbb", I["conv_b_bias"][l], 2)
        put(l, "cblg", I["conv_b_ln_g"][l], 2)
        put(l, "cblb", I["conv_b_ln_b"][l], 2)
        put(l, "cbw", I["conv_b_w"][l], CBW * 2)
        put(l, "cdw", I["conv_d_w"][l], CDW * 2)
        put(l, "vng", I["gmlp_vn_g"][l], 2)
        put(l, "vnb", I["gmlp_vn_b"][l], 2)
    S["vecs"] = vecs
    S["ident"] = np.eye(128, dtype=np.float32)
    S["cosT"], S["sinS"] = _rope_tables()
    return S


def prep_core(I, c):
    f = lambda a: np.asarray(a, dtype=np.float32)
    C = {}
    X = np.concatenate([f(I["x_prompt"][c]), f(I["x_sample"][c])], axis=0)
    C["xT"] = np.ascontiguousarray(X.T.reshape(KC, 128, NT).transpose(1, 0, 2)).reshape(128, KC * NT)
    cl = f(I["cache_kv_latent"][:, c])
    C["clatT"] = np.ascontiguousarray(cl.transpose(0, 2, 1).reshape(L, 2, 128, SEQ).transpose(0, 2, 1, 3)).reshape(L, 128, 2 * SEQ)
    C["clat"] = np.ascontiguousarray(cl.reshape(L, 16, 128, 256).transpose(0, 2, 1, 3)).reshape(L, 128, 16 * 256)
    C["ckrT"] = np.ascontiguousarray(f(I["cache_k_rope"][:, c]).transpose(0, 2, 1))
    sb = f(I["state_conv_b"][:, c])
    C["scbT"] = np.ascontiguousarray(sb.transpose(0, 2, 1).reshape(L, 2, 128, PADB).transpose(0, 2, 1, 3)).reshape(L, 128, 2 * PADB)
    sd = f(I["state_conv_d"][:, c])
    C["scdT"] = np.ascontiguousarray(sd.transpose(0, 2, 1).reshape(L, 2, 128, PADD).transpose(0, 2, 1, 3)).reshape(L, 128, 2 * PADD)
    return C


def assemble(results):
    n = len(results)
    yp = np.empty((n, SEQ, D), np.float32)
    ys = np.empty((n, DSEQ, D), np.float32)
    latp = np.empty((L, n, SEQ, KVL), np.float32)
    lats = np.empty((L, n, DSEQ, KVL), np.float32)
    krp = np.empty((L, n, SEQ, RD), np.float32)
    krs = np.empty((L, n, DSEQ, RD), np.float32)
    cbp = np.empty((L, n, PADB, 256), np.float32)
    cbs = np.empty((L, n, PADB, 256), np.float32)
    cdp = np.empty((L, n, PADD, 256), np.float32)
    cds = np.empty((L, n, PADD, 256), np.float32)
    gv = np.empty((L, n, DSEQ, 256), np.float32)
    for c, r in enumerate(results):
        y = np.asarray(r["yT"]).reshape(128, KC, NT).transpose(2, 1, 0).reshape(NT, D)
        yp[c], ys[c] = y[:SEQ], y[SEQ:]
        lat = np.asarray(r["latT"]).reshape(L, 128, 2, NT).transpose(0, 3, 2, 1).reshape(L, NT, KVL)
        latp[:, c], lats[:, c] = lat[:, :SEQ], lat[:, SEQ:]
        kr = np.asarray(r["krT"]).transpose(0, 2, 1)
        krp[:, c], krs[:, c] = kr[:, :SEQ], kr[:, SEQ:]
        cb = np.asarray(r["cbT"]).reshape(L, 128, 2, 2, PADB).transpose(0, 3, 4, 2, 1).reshape(L, 2, PADB, 256)
        cbp[:, c], cbs[:, c] = cb[:, 0], cb[:, 1]
        cd = np.asarray(r["cdT"]).reshape(L, 128, 2, 2, PADD).transpose(0, 3, 4, 2, 1).reshape(L, 2, PADD, 256)
        cdp[:, c], cds[:, c] = cd[:, 0], cd[:, 1]
        gv[:, c] = np.asarray(r["gvT"]).reshape(L, 128, 2, DSEQ).transpose(0, 3, 2, 1).reshape(L, DSEQ, 256)
    return (yp, ys, latp, krp, cbp, cdp, lats, krs, cbs, gv, cds)


_NC_CACHE = {}


def kernel(**inputs):
    if "nc" not in _NC_CACHE:
        _NC_CACHE["nc"] = build_program()
    nc = _NC_CACHE["nc"]
    S = prep_shared(inputs)
    in_maps = []
    for c in range(8):
        m = dict(S)
        m.update(prep_core(inputs, c))
        in_maps.append(m)
    res = run_bass_kernel_spmd(nc, in_maps, core_ids=list(range(8)))
    return assemble(res.results)
```
